# Optimizing a Trainium2 kernel written in Bass

```python
import math
import jax, jax.numpy as jnp
from jax import lax
import numpy as np

D_MODEL = 2048
BATCH = 4
SEQ = 4096
DEPTH = 2

HEAD_DIM = 128
GRID_W = 64
BLOCK_Q = 128
RMS_EPS = 1e-6
NEG_INF = -1e30

T5_BUCKETS = 32
T5_MAX_DIST = 1024

DILATED_PATTERNS = ((128, 1), (512, 4), (2048, 16))
A_GROUPS = len(DILATED_PATTERNS)
A_HEADS = 8
B_HEADS = 4
B_QK_DIM = 128
B_V_DIM = 2 * B_QK_DIM
C_HEADS = 8
C_KV_HEADS = 2
ROPE_THETA = 10000.0
ROPE_AXIS_DIM = HEAD_DIM // 2
D_HEADS = 8
NA_ROWS = 8
NA_COLS = 16
D_FF = 5632
CONV_W = 3

N_T5_HEADS = A_GROUPS * A_HEADS + B_HEADS
EVEN_IN = A_GROUPS * 3 * A_HEADS * HEAD_DIM + B_HEADS * (4 * B_QK_DIM + B_V_DIM)
EVEN_OUT = A_HEADS * HEAD_DIM + B_HEADS * B_V_DIM
ODD_IN = (C_HEADS + 2 * C_KV_HEADS) * HEAD_DIM + 3 * D_HEADS * HEAD_DIM
ODD_OUT = (C_HEADS + D_HEADS) * HEAD_DIM

kernel_name = "hybrid_dilated_diff_axial_na_encoder"


def rmsnorm(x, gain):
    xf = x.astype(jnp.float32)
    y = xf * lax.rsqrt(jnp.mean(xf * xf, axis=-1, keepdims=True) + RMS_EPS)
    return (y * gain.astype(jnp.float32)).astype(x.dtype)


def t5_bucket(rel):
    nb = T5_BUCKETS // 2
    max_exact = nb // 2
    ret = jnp.where(rel > 0, nb, 0)
    n = jnp.abs(rel)
    n_f = jnp.maximum(n, 1).astype(jnp.float32)
    large = max_exact + (jnp.log(n_f / max_exact) / math.log(T5_MAX_DIST / max_exact)
                         * (nb - max_exact)).astype(jnp.int32)
    large = jnp.minimum(large, nb - 1)
    return ret + jnp.where(n < max_exact, n, large)


def query_blocks(q):
    b, s = q.shape[:2]
    nb = s // BLOCK_Q
    qb = jnp.moveaxis(q.reshape((b, nb, BLOCK_Q) + q.shape[2:]), 1, 0)
    return qb, jnp.arange(nb, dtype=jnp.int32) * BLOCK_Q


def merge_blocks(o):
    o = jnp.moveaxis(o, 0, 1)
    return o.reshape((o.shape[0], o.shape[1] * o.shape[2]) + o.shape[3:])


def dilated_window_attention(q, k, v, bias_cols, window, dilation):
    s = q.shape[1]
    half = window // (2 * dilation)
    offs = dilation * jnp.arange(-half, half + 1, dtype=jnp.int32)
    rel_bias = bias_cols[t5_bucket(offs)].T.astype(jnp.float32)
    scale = HEAD_DIM ** -0.5
    qb, starts = query_blocks(q)

    def block(args):
        qi, start = args
        pos = start + jnp.arange(BLOCK_Q, dtype=jnp.int32)
        idx = pos[:, None] + offs[None, :]
        valid = (idx >= 0) & (idx < s)
        idx = jnp.clip(idx, 0, s - 1)
        kg = k[:, idx]
        vg = v[:, idx]
        logits = (jnp.einsum("bqhd,bqkhd->bhqk", qi, kg).astype(jnp.float32) * scale
                  + rel_bias[None, :, None, :])
        logits = jnp.where(valid[None, None], logits, NEG_INF)
        m = jnp.max(logits, axis=-1, keepdims=True)
        p = jnp.exp(logits - m)
        denom = jnp.sum(p, axis=-1)
        o = (jnp.einsum("bhqk,bqkhd->bqhd", p, vg.astype(jnp.float32))
             / jnp.transpose(denom, (0, 2, 1))[..., None])
        lse = m[..., 0] + jnp.log(denom)
        return o, jnp.transpose(lse, (0, 2, 1))

    o, lse = lax.map(block, (qb, starts))
    return merge_blocks(o), merge_blocks(lse)


def differential_attention(q, k, v, bias_cols, lam):
    s = q.shape[1]
    scale = B_QK_DIM ** -0.5
    key_pos = jnp.arange(s, dtype=jnp.int32)
    vf = v.astype(jnp.float32)
    qb, starts = query_blocks(q)

    def block(args):
        qi, start = args
        pos = start + jnp.arange(BLOCK_Q, dtype=jnp.int32)
        bias = bias_cols[t5_bucket(key_pos[None, :] - pos[:, None])]
        bias = jnp.transpose(bias, (2, 0, 1)).astype(jnp.float32)
        logits = (jnp.einsum("bqhmd,bkhmd->bhmqk", qi, k).astype(jnp.float32) * scale
                  + bias[None, :, None])
        p = jax.nn.softmax(logits, axis=-1)
        w = p[:, :, 0] - lam * p[:, :, 1]
        return jnp.einsum("bhqk,bkhd->bqhd", w, vf)

    return merge_blocks(lax.map(block, (qb, starts)))


def axial_rope_tables(s):
    t = jnp.arange(s, dtype=jnp.int32)
    row = (t // GRID_W).astype(jnp.float32)
    col = (t % GRID_W).astype(jnp.float32)
    inv_freq = ROPE_THETA ** (-(jnp.arange(0, ROPE_AXIS_DIM, 2, dtype=jnp.float32) / ROPE_AXIS_DIM))
    ang = jnp.concatenate([row[:, None] * inv_freq[None], col[:, None] * inv_freq[None]], axis=-1)
    return jnp.cos(ang), jnp.sin(ang)


def apply_rope(x, cos, sin):
    xp = x.astype(jnp.float32).reshape(x.shape[:-1] + (HEAD_DIM // 2, 2))
    x0, x1 = xp[..., 0], xp[..., 1]
    c = cos[None, :, None, :]
    sn = sin[None, :, None, :]
    out = jnp.stack([x0 * c - x1 * sn, x0 * sn + x1 * c], axis=-1)
    return out.reshape(x.shape).astype(x.dtype)


def gqa_block_attention(q, k, v):
    b, s = q.shape[:2]
    g = C_HEADS // C_KV_HEADS
    scale = HEAD_DIM ** -0.5
    vf = v.astype(jnp.float32)
    qb, _ = query_blocks(q.reshape(b, s, C_KV_HEADS, g, HEAD_DIM))

    def block(qi):
        logits = jnp.einsum("bqngd,bknd->bngqk", qi, k).astype(jnp.float32) * scale
        p = jax.nn.softmax(logits, axis=-1)
        return jnp.einsum("bngqk,bknd->bqngd", p, vf)

    o = merge_blocks(lax.map(block, qb))
    return o.reshape(b, s, C_HEADS * HEAD_DIM)


def neighbourhood_attention(q, k, v, rpb, rows):
    b, s, h, dh = q.shape
    kr = min(NA_ROWS, rows)
    kc = NA_COLS
    scale = dh ** -0.5
    kg = k.reshape(b, rows, GRID_W, h, dh)
    vg = v.reshape(b, rows, GRID_W, h, dh)
    qrows = jnp.moveaxis(q.reshape(b, rows, GRID_W, h, dh), 1, 0)
    cols = jnp.arange(GRID_W, dtype=jnp.int32)
    col_start = jnp.clip(cols - kc // 2, 0, GRID_W - kc)
    col_idx = col_start[:, None] + jnp.arange(kc, dtype=jnp.int32)[None, :]
    dc = col_idx - cols[:, None] + (NA_COLS - 1)

    def row(args):
        qi, i = args
        rs = jnp.clip(i - kr // 2, 0, rows - kr)
        kband = lax.dynamic_slice_in_dim(kg, rs, kr, axis=1)
        vband = lax.dynamic_slice_in_dim(vg, rs, kr, axis=1)
        kq = kband[:, :, col_idx]
        vq = vband[:, :, col_idx]
        dr = rs + jnp.arange(kr, dtype=jnp.int32) - i + (NA_ROWS - 1)
        bias = rpb[:, dr][:, :, dc]
        bias = jnp.transpose(bias, (0, 2, 1, 3)).astype(jnp.float32)
        logits = (jnp.einsum("bjhd,brjchd->bhjrc", qi, kq).astype(jnp.float32) * scale
                  + bias[None])
        p = jax.nn.softmax(logits.reshape(b, h, GRID_W, kr * kc), axis=-1).reshape(b, h, GRID_W, kr, kc)
        return jnp.einsum("bhjrc,brjchd->bjhd", p, vq.astype(jnp.float32))

    o = lax.map(row, (qrows, jnp.arange(rows, dtype=jnp.int32)))
    return jnp.moveaxis(o, 0, 1).reshape(b, s, h * dh)


def even_mixer(y, w_in, w_out, lq1, lk1, lq2, lk2, subln, t5_table, lambda_init):
    b, s, _ = y.shape
    proj = jnp.einsum("bsd,de->bse", y, w_in)
    a_cols = A_GROUPS * 3 * A_HEADS * HEAD_DIM
    pa = proj[..., :a_cols].reshape(b, s, A_GROUPS, 3, A_HEADS, HEAD_DIM)
    outs, lses = [], []
    for gi, (window, dilation) in enumerate(DILATED_PATTERNS):
        cols = t5_table[:, gi * A_HEADS:(gi + 1) * A_HEADS]
        o, lse = dilated_window_attention(pa[:, :, gi, 0], pa[:, :, gi, 1], pa[:, :, gi, 2],
                                          cols, window, dilation)
        outs.append(o)
        lses.append(lse)
    alpha = jax.nn.softmax(jnp.stack(lses), axis=0)[..., None]
    out_a = jnp.sum(alpha * jnp.stack(outs), axis=0).reshape(b, s, A_HEADS * HEAD_DIM)

    pb = proj[..., a_cols:]
    qk_cols = B_HEADS * 2 * B_QK_DIM
    q_b = pb[..., :qk_cols].reshape(b, s, B_HEADS, 2, B_QK_DIM)
    k_b = pb[..., qk_cols:2 * qk_cols].reshape(b, s, B_HEADS, 2, B_QK_DIM)
    v_b = pb[..., 2 * qk_cols:].reshape(b, s, B_HEADS, B_V_DIM)
    f32 = jnp.float32
    lam = (jnp.exp(jnp.sum(lq1.astype(f32) * lk1.astype(f32)))
           - jnp.exp(jnp.sum(lq2.astype(f32) * lk2.astype(f32))) + lambda_init)
    o_b = differential_attention(q_b, k_b, v_b, t5_table[:, A_GROUPS * A_HEADS:], lam)
    o_b = rmsnorm(o_b, subln) * (1.0 - lambda_init)
    out_b = o_b.reshape(b, s, B_HEADS * B_V_DIM)
    mixed = jnp.concatenate([out_a, out_b], axis=-1).astype(y.dtype)
    return jnp.einsum("bse,ed->bsd", mixed, w_out)


def odd_mixer(y, w_in, w_out, q_norm, k_norm, rpb, rows):
    b, s, _ = y.shape
    proj = jnp.einsum("bsd,de->bse", y, w_in)
    nq = C_HEADS * HEAD_DIM
    nkv = C_KV_HEADS * HEAD_DIM
    q_c = proj[..., :nq].reshape(b, s, C_HEADS, HEAD_DIM)
    k_c = proj[..., nq:nq + nkv].reshape(b, s, C_KV_HEADS, HEAD_DIM)
    v_c = proj[..., nq + nkv:nq + 2 * nkv].reshape(b, s, C_KV_HEADS, HEAD_DIM)
    pd = proj[..., nq + 2 * nkv:].reshape(b, s, 3, D_HEADS, HEAD_DIM)
    cos, sin = axial_rope_tables(s)
    q_c = apply_rope(rmsnorm(q_c, q_norm), cos, sin)
    k_c = apply_rope(rmsnorm(k_c, k_norm), cos, sin)
    o_c = gqa_block_attention(q_c, k_c, v_c)
    o_d = neighbourhood_attention(pd[:, :, 0], pd[:, :, 1], pd[:, :, 2], rpb, rows)
    mixed = jnp.concatenate([o_c, o_d], axis=-1).astype(y.dtype)
    return jnp.einsum("bse,ed->bsd", mixed, w_out)


def conv_ffn(x, w_up, conv_w, conv_b, w_down):
    h = jnp.einsum("bsd,df->bsf", x, w_up)
    g, u = h[..., :D_FF], h[..., D_FF:]
    gp = jnp.pad(g, ((0, 0), (1, 1), (0, 0)))
    g = conv_w[0] * gp[:, :-2] + conv_w[1] * gp[:, 1:-1] + conv_w[2] * gp[:, 2:] + conv_b
    return jnp.einsum("bsf,fd->bsd", jax.nn.gelu(g) * u, w_down)


def setup_inputs(seed: int = 0) -> dict:
    key = jax.random.key(seed)
    ks = jax.random.split(key, 21)
    ne, no = (DEPTH + 1) // 2, DEPTH // 2
    f32 = jnp.float32

    def w(k, shape, fan_in):
        return jax.random.normal(k, shape, f32) * fan_in ** -0.5

    def gain(k, shape):
        return 1.0 + 0.1 * jax.random.normal(k, shape, f32)

    return {
        "x": jax.random.normal(ks[0], (BATCH, SEQ, D_MODEL), f32),
        "ln_mix": gain(ks[1], (DEPTH, D_MODEL)),
        "ln_ffn": gain(ks[2], (DEPTH, D_MODEL)),
        "ln_final": gain(ks[3], (D_MODEL,)),
        "t5_table": 0.5 * jax.random.normal(ks[4], (T5_BUCKETS, N_T5_HEADS), f32),
        "ev_w_in": w(ks[5], (ne, D_MODEL, EVEN_IN), D_MODEL),
        "ev_w_out": w(ks[6], (ne, EVEN_OUT, D_MODEL), EVEN_OUT),
        "diff_lq1": 0.1 * jax.random.normal(ks[7], (ne, B_QK_DIM), f32),
        "diff_lk1": 0.1 * jax.random.normal(ks[8], (ne, B_QK_DIM), f32),
        "diff_lq2": 0.1 * jax.random.normal(ks[9], (ne, B_QK_DIM), f32),
        "diff_lk2": 0.1 * jax.random.normal(ks[10], (ne, B_QK_DIM), f32),
        "diff_subln": gain(ks[11], (ne, B_V_DIM)),
        "od_w_in": w(ks[12], (no, D_MODEL, ODD_IN), D_MODEL),
        "od_w_out": w(ks[13], (no, ODD_OUT, D_MODEL), ODD_OUT),
        "gqa_q_norm": gain(ks[14], (no, HEAD_DIM)),
        "gqa_k_norm": gain(ks[15], (no, HEAD_DIM)),
        "na_rpb": 0.5 * jax.random.normal(ks[16], (no, D_HEADS, 2 * NA_ROWS - 1, 2 * NA_COLS - 1), f32),
        "ffn_w_up": w(ks[17], (DEPTH, D_MODEL, 2 * D_FF), D_MODEL),
        "ffn_conv_w": jax.random.normal(ks[18], (DEPTH, CONV_W, D_FF), f32) * CONV_W ** -0.5,
        "ffn_conv_b": 0.02 * jax.random.normal(ks[19], (DEPTH, D_FF), f32),
        "ffn_w_down": w(ks[20], (DEPTH, D_FF, D_MODEL), D_FF),
    }


def reference(x, ln_mix, ln_ffn, ln_final, t5_table, ev_w_in, ev_w_out, diff_lq1, diff_lk1,
              diff_lq2, diff_lk2, diff_subln, od_w_in, od_w_out, gqa_q_norm, gqa_k_norm, na_rpb,
              ffn_w_up, ffn_conv_w, ffn_conv_b, ffn_w_down):
    rows = x.shape[1] // GRID_W
    h = x
    for layer in range(DEPTH):
        y = rmsnorm(h, ln_mix[layer])
        if layer % 2 == 0:
            e = layer // 2
            lambda_init = 0.8 - 0.6 * math.exp(-0.3 * layer)
            mix = even_mixer(y, ev_w_in[e], ev_w_out[e], diff_lq1[e], diff_lk1[e], diff_lq2[e],
                             diff_lk2[e], diff_subln[e], t5_table, lambda_init)
        else:
            o = layer // 2
            mix = odd_mixer(y, od_w_in[o], od_w_out[o], gqa_q_norm[o], gqa_k_norm[o], na_rpb[o], rows)
        h = h + mix.astype(h.dtype)
        f = conv_ffn(rmsnorm(h, ln_ffn[layer]), ffn_w_up[layer], ffn_conv_w[layer],
                     ffn_conv_b[layer], ffn_w_down[layer])
        h = h + f.astype(h.dtype)
    return rmsnorm(h, ln_final)
```

```python
import numpy as np
import concourse.bass as bass
import concourse.mybir as mybir

F32 = mybir.dt.float32
BF16 = mybir.dt.bfloat16
AF = mybir.ActivationFunctionType
ALU = mybir.AluOpType
AX = mybir.AxisListType

ENGS = ("pe", "act", "dve", "pool", "sp")
N_DMA_SEMS = 24
SYNC_SAME_ENGINE = True


class Res:
    __slots__ = ("lw", "rd", "name", "excl")

    def __init__(self, name="", excl=False):
        self.lw = None
        self.rd = []
        self.name = name
        self.excl = excl


class Op:
    __slots__ = ("eng", "fn", "deps", "signaled", "count", "is_dma", "dsem", "dtarget", "prev_same_sem", "idx", "epoch")


class Sched:
    def __init__(self, nc, es):
        self.nc = nc
        self.ops = {e: [] for e in ENGS}
        self.emitted = {e: 0 for e in ENGS}
        self.sig_count = {e: 0 for e in ENGS}
        self.esem = {e: es.enter_context(nc.semaphore("s_" + e)) for e in ENGS if e != "sp"}
        self.dsems = [es.enter_context(nc.semaphore("d%d" % i)) for i in range(N_DMA_SEMS)]
        self.dsem_count = [0] * N_DMA_SEMS
        self.dsem_last = [None] * N_DMA_SEMS
        self.dma_rr = 0
        self.waited = {e: {} for e in ENGS}
        self.last_op = {e: None for e in ENGS}
        self.all_dma = []
        self.epoch = 0

    def engine(self, e):
        nc = self.nc
        return {"pe": nc.tensor, "act": nc.scalar, "dve": nc.vector, "pool": nc.gpsimd, "sp": nc.sync}[e]

    def _mk(self, eng, fn, reads, writes, is_dma):
        op = Op()
        op.eng = eng
        op.fn = fn
        op.is_dma = is_dma
        op.signaled = False
        op.count = None
        op.dsem = None
        op.prev_same_sem = None
        deps = []
        xr = [r for r in reads if r.excl]
        if xr:
            reads = [r for r in reads if not r.excl]
            writes = list(writes) + [r for r in xr if r not in writes]
        for r in reads:
            if r.lw is not None:
                deps.append(r.lw)
        for w in writes:
            if w.lw is not None:
                deps.append(w.lw)
            deps.extend(w.rd)
        for r in reads:
            r.rd.append(op)
        for w in writes:
            w.lw = op
            w.rd = []
        seen = set()
        ud = []
        for d in deps:
            if id(d) in seen or d is op:
                continue
            seen.add(id(d))
            ud.append(d)
        op.deps = ud
        op.epoch = self.epoch
        op.idx = len(self.ops[eng])
        self.ops[eng].append(op)
        self.last_op[eng] = op
        return op

    def op(self, eng, fn, reads=(), writes=()):
        return self._mk(eng, fn, reads, writes, False)

    def dma(self, eng, out, in_, reads=(), writes=(), **kw):
        def fn(e):
            return e.dma_start(out=out, in_=in_, **kw)
        op = self._mk(eng, fn, reads, writes, True)
        s = self.dma_rr
        self.dma_rr = (self.dma_rr + 1) % N_DMA_SEMS
        op.dsem = s
        self.dsem_count[s] += 16
        op.dtarget = self.dsem_count[s]
        op.prev_same_sem = self.dsem_last[s]
        self.dsem_last[s] = op
        self.all_dma.append(op)
        return op

    def barrier(self):
        lasts = [self.last_op[e] for e in ENGS if self.last_op[e] is not None and not self.last_op[e].is_dma]
        dl = [d for d in self.dsem_last if d is not None]
        lastc = []
        for e in ENGS:
            for o in reversed(self.ops[e]):
                if not o.is_dma and o.fn is not None:
                    lastc.append(o)
                    break
        for e in ENGS:
            op = Op()
            op.eng = e
            op.fn = None
            op.is_dma = False
            op.signaled = False
            op.count = None
            op.dsem = None
            op.prev_same_sem = None
            op.deps = [d for d in (lastc + dl)]
            op.epoch = self.epoch
            op.idx = len(self.ops[e])
            self.ops[e].append(op)
        self.epoch += 1

    def flush(self):
        nc = self.nc
        self.barrier()
        for e in ENGS:
            for o in self.ops[e][self.emitted[e]:]:
                for d in o.deps:
                    if d.epoch < o.epoch:
                        continue
                    if not d.is_dma:
                        if d.eng == o.eng and (d.eng == "pe" or not SYNC_SAME_ENGINE):
                            continue
                        d.signaled = True
        for e in ENGS:
            for o in self.ops[e][self.emitted[e]:]:
                if o.signaled and o.count is None and not o.is_dma:
                    self.sig_count[e] += 1
                    o.count = self.sig_count[e]
        sched = self

        def emit_engine(e, engobj):
            waited = sched.waited[e]
            for o in sched.ops[e][sched.emitted[e]:]:
                deps = list(o.deps)
                if o.is_dma and o.prev_same_sem is not None:
                    deps.append(o.prev_same_sem)
                for d in deps:
                    if d.epoch < o.epoch:
                        continue
                    if d.is_dma:
                        key = ("d", d.dsem)
                        val = d.dtarget
                        sem = sched.dsems[d.dsem]
                    else:
                        if d.eng == e and (e == "pe" or not SYNC_SAME_ENGINE):
                            continue
                        assert d.count is not None, "dep on unsignaled op"
                        key = ("e", d.eng)
                        val = d.count
                        sem = sched.esem[d.eng]
                    if waited.get(key, 0) >= val:
                        continue
                    waited[key] = val
                    engobj.wait_ge(sem, val)
                if o.fn is None:
                    continue
                ins = o.fn(engobj)
                if o.is_dma:
                    ins.then_inc(sched.dsems[o.dsem], 16)
                elif o.signaled:
                    ins.then_inc(sched.esem[e], 1)
            sched.emitted[e] = len(sched.ops[e])

        with nc.Block() as block:
            @block.tensor
            def _(eng):
                emit_engine("pe", eng)

            @block.scalar
            def _(eng):
                emit_engine("act", eng)

            @block.vector
            def _(eng):
                emit_engine("dve", eng)

            @block.gpsimd
            def _(eng):
                emit_engine("pool", eng)

            @block.sync
            def _(eng):
                emit_engine("sp", eng)

import math
from contextlib import ExitStack
from concourse.bass_utils import run_bass_kernel_spmd

S_LEN = 4096
D = 2048
DFF = 5632
NT = S_LEN // 128
EPS = 1e-6
NEG = -30000.0
SCALE = 128 ** -0.5
DIL = (1, 4, 16)


class KB:
    def __init__(self, debug=(), stop_after=None):
        self.nc = bass.Bass("TRN2", target_bir_lowering=False)
        self.debug = set(debug)
        self.stop_after = stop_after
        self.es = ExitStack()
        self.S = Sched(self.nc, self.es)
        self.rid = 0

    def dram(self, name, shape, dtype, kind=None):
        if kind is None:
            kind = "ExternalOutput" if name in self.debug else "Internal"
        return self.nc.dram_tensor(name, list(shape), dtype, kind=kind).ap()

    def inp(self, name, shape, dtype=F32):
        return self.nc.dram_tensor(name, list(shape), dtype, kind="ExternalInput").ap()

    def res(self, name=""):
        self.rid += 1
        return Res(name + str(self.rid))

    def sb(self, st, name, shape, dtype, n=1):
        out = []
        for i in range(n):
            t = st.enter_context(self.nc.sbuf_tensor("%s_%d_%d" % (name, i, self.rid), list(shape), dtype))
            self.rid += 1
            out.append((t, self.res(name)))
        return out

    def ps(self, st, name, shape, dtype, n=1):
        out = []
        for i in range(n):
            t = st.enter_context(self.nc.psum_tensor("%s_%d_%d" % (name, i, self.rid), list(shape), dtype))
            self.rid += 1
            r = self.res(name)
            r.excl = True
            out.append((t, r))
        return out

    def phase_end(self, name):
        self.S.flush()
        return self.stop_after == name

    def precast(self, w, name, K, N, sc, kstep=16):
        KC = K // 128
        ns = N // sc
        wb = self.dram(name, [ns, 128, KC, sc], BF16)
        for s in range(ns):
            for k0 in range(0, KC, kstep):
                k1 = min(KC, k0 + kstep)
                self.S.dma("pool", wb[s, :, k0:k1, :],
                           w[k0 * 128:k1 * 128, s * sc:(s + 1) * sc].rearrange("(k p) n -> p k n", p=128))
        return wb

    def setup_consts(self):
        st = self.es
        S = self.S
        (identf, r_if), = self.sb(st, "identf", [128, 128], F32)
        (ident, r_id), = self.sb(st, "ident", [128, 128], BF16)

        def mk_ident(e):
            e.memset(identf[:], 0.0)
            return e.affine_select(identf[:], identf[:], pattern=[[-1, 128]], compare_op=ALU.not_equal,
                                   fill=1.0, base=0, channel_multiplier=1)
        S.op("pool", mk_ident, writes=[r_if])
        S.op("dve", lambda e: e.tensor_copy(ident[:], identf[:]), reads=[r_if], writes=[r_id])
        self.ident = ident
        self.r_ident = r_id

    def load_T(self, st, src, t0, ntok, xT, r_xT, col0, gain_b=None, r_gain=None, dt_in=F32, bufs=None):
        S = self.S
        ht_ring, yb_ring, junk, ssr, pT_ring = bufs
        ntile = (ntok + 127) // 128
        for ti in range(ntile):
            r0 = t0 + ti * 128
            n = min(128, ntok - ti * 128)
            ht, r_ht = ht_ring[self.cnt_ht % len(ht_ring)]
            self.cnt_ht += 1
            lo = max(r0, 0)
            hi = min(r0 + n, S_LEN)
            if lo > r0 or hi < r0 + n:
                S.op("pool", lambda e, ht=ht, n=n: e.memset(ht[0:n, :], 0.0), writes=[r_ht])
            if hi > lo:
                S.dma("sp", ht[lo - r0:hi - r0, :], src[lo:hi, :], writes=[r_ht])
            if gain_b is not None:
                yb, r_yb = yb_ring[self.cnt_yb % len(yb_ring)]
                self.cnt_yb += 1
                (jk, r_jk) = junk
                (ss, r_ss) = ssr[self.cnt_yb % len(ssr)]
                S.op("act", lambda e, ht=ht, jk=jk, ss=ss, n=n: e.activation(jk[0:n, :], ht[0:n, :], AF.Square, scale=D ** -0.5, accum_out=ss[0:n, :]),
                     reads=[r_ht], writes=[r_jk, r_ss])
                S.op("dve", lambda e, ss=ss, n=n: e.tensor_scalar(ss[0:n, :], ss[0:n, :], EPS, None, ALU.add), reads=[r_ss], writes=[r_ss])
                S.op("act", lambda e, ss=ss, n=n: e.activation(ss[0:n, :], ss[0:n, :], AF.Sqrt), reads=[r_ss], writes=[r_ss])
                S.op("dve", lambda e, ss=ss, n=n: e.reciprocal(ss[0:n, :], ss[0:n, :]), reads=[r_ss], writes=[r_ss])
                S.op("dve", lambda e, yb=yb, ht=ht, ss=ss, n=n: e.scalar_tensor_tensor(yb[0:n, :], ht[0:n, :], ss[0:n, 0:1], gain_b[0:n, :], ALU.mult, ALU.mult),
                     reads=[r_ht, r_ss, r_gain], writes=[r_yb])
                srcT, r_srcT = yb, r_yb
            else:
                srcT, r_srcT = ht, r_ht
            for g in range(2):
                pT, r_pT = pT_ring[self.cnt_pT % len(pT_ring)]
                self.cnt_pT += 1

                def tr(e, pT=pT, srcT=srcT, g=g, n=n):
                    for j in range(8):
                        k = g * 8 + j
                        i = e.transpose(pT[:, j, 0:n], srcT[0:n, k * 128:(k + 1) * 128], self.ident[0:n, 0:n])
                    return i
                S.op("pe", tr, reads=[r_srcT, self.r_ident], writes=[r_pT])
                c0 = col0 + ti * 128
                eng = "act" if (self.cnt_pT % 2) else "dve"
                if eng == "act":
                    S.op("act", lambda e, pT=pT, g=g, c0=c0, n=n: e.activation(xT[:, g * 8:(g + 1) * 8, c0:c0 + n], pT[:, :, 0:n], AF.Copy),
                         reads=[r_pT], writes=[r_xT])
                else:
                    S.op("dve", lambda e, pT=pT, g=g, c0=c0, n=n: e.tensor_copy(xT[:, g * 8:(g + 1) * 8, c0:c0 + n], pT[:, :, 0:n]),
                         reads=[r_pT], writes=[r_xT])

    def load_T_bufs(self, st, norm, dt_in):
        ht_ring = self.sb(st, "ht", [128, D], dt_in, 2)
        if norm:
            yb_ring = self.sb(st, "yb", [128, D], BF16, 2)
            junk = self.sb(st, "junk", [128, D], BF16, 1)[0]
            ssr = self.sb(st, "ss", [128, 1], F32, 4)
        else:
            yb_ring = junk = ssr = None
        pT_ring = self.ps(st, "pT", [128, 8, 128], BF16, 2)
        self.cnt_ht = self.cnt_yb = self.cnt_pT = 0
        return (ht_ring, yb_ring, junk, ssr, pT_ring)

    def load_gain(self, st, g_ap, n=D):
        (gb, r_gb), = self.sb(st, "gb", [128, n], F32)
        self.S.dma("sp", gb[:], g_ap.partition_broadcast(128), writes=[r_gb])
        return gb, r_gb

    def proj_phase(self, h_src, gain_ap, wb, slabs, rope=None):
        S = self.S
        nc = self.nc
        with ExitStack() as st:
            bufs = self.load_T_bufs(st, True, F32)
            gb, r_gb = self.load_gain(st, gain_ap)
            (yT, r_yT), = self.sb(st, "yT", [128, 16, 2048], BF16)
            wt_ring = self.sb(st, "wt", [128, 16, 512], BF16, 2)
            po_ring = self.ps(st, "po", [128, 512], F32, 4)
            qst_ring = self.sb(st, "qst", [128, 2048], BF16, 3)
            vst_ring = {}
            cnt = {"wt": 0, "po": 0, "qst": 0, "ev": 0}
            if rope is not None:
                cos_d, sin_d, gq_ap, gk_ap = rope
                (gqb, r_gqb), = self.sb(st, "gqb", [128, 128], F32)
                (gkb, r_gkb), = self.sb(st, "gkb", [128, 128], F32)
                S.dma("sp", gqb[:], gq_ap.partition_broadcast(128), writes=[r_gqb])
                S.dma("sp", gkb[:], gk_ap.partition_broadcast(128), writes=[r_gkb])
                cs_ring = self.sb(st, "cs", [128, 2, 64], F32, 3)
                sq_ring = self.sb(st, "sq", [128, 512], F32, 2)
                ss4_ring = self.sb(st, "ss4", [128, 4], F32, 3)
                xn_ring = self.sb(st, "xn", [128, 4, 128], F32, 2)
                tt_ring = self.sb(st, "tt", [128, 4, 4, 64], F32, 2)
                xr_ring = self.sb(st, "xr", [128, 4, 128], BF16, 2)
                hst = self.sb(st, "hst", [128, 2048], BF16, 8)
                pTc_ring = self.ps(st, "pTc", [128, 4, 128], BF16, 2)
                cnt.update({"cs": 0, "sq": 0, "xn": 0, "hst": 0, "pTc": 0})

            def get_vst(nh, dv):
                key = (nh, dv)
                if key not in vst_ring:
                    ring = self.sb(st, "vst", [128, nh, dv + 1], BF16, 3)
                    for (t, r) in ring:
                        S.op("pool", lambda e, t=t: e.memset(t[:], 1.0), writes=[r])
                    vst_ring[key] = [ring, 0]
                ent = vst_ring[key]
                t, r = ent[0][ent[1] % 3]
                ent[1] += 1
                return t, r

            def evac(out_ap, in_ap, reads, writes):
                cnt["ev"] += 1
                if cnt["ev"] % 2:
                    S.op("act", lambda e: e.activation(out_ap, in_ap, AF.Copy), reads=reads, writes=writes)
                else:
                    S.op("dve", lambda e: e.tensor_copy(out_ap, in_ap), reads=reads, writes=writes)

            for c in range(2):
                tok0 = c * 2048
                self.load_T(st, h_src, tok0, 2048, yT, r_yT, 0, gain_b=gb, r_gain=r_gb, bufs=bufs)
                for si, spec in enumerate(slabs):
                    wt, r_wt = wt_ring[cnt["wt"] % 2]
                    cnt["wt"] += 1
                    S.dma("sp", wt[:], wb[si], writes=[r_wt])
                    kind = spec[0]
                    if kind == "fm":
                        for cc in range(4):
                            qid = spec[1][cc]
                            qst, r_qst = qst_ring[cnt["qst"] % 3]
                            cnt["qst"] += 1
                            for tg in range(4):
                                po, r_po = po_ring[cnt["po"] % 4]
                                cnt["po"] += 1

                                def mm(e, po=po, wt=wt, cc=cc, tg=tg):
                                    for k in range(16):
                                        i = e.matmul(po[:], wt[:, k, cc * 128:(cc + 1) * 128], yT[:, k, tg * 512:(tg + 1) * 512],
                                                     start=(k == 0), stop=(k == 15))
                                    return i
                                S.op("pe", mm, reads=[r_wt, r_yT], writes=[r_po])
                                evac(qst[:, tg * 512:(tg + 1) * 512], po[:], [r_po], [r_qst])
                            S.dma("sp", self.qt[qid, :, tok0:tok0 + 2048], qst[:], reads=[r_qst])
                    elif kind == "tmv":
                        _, v1, h0, nh, dv = spec
                        for t in range(16):
                            po, r_po = po_ring[cnt["po"] % 4]
                            cnt["po"] += 1

                            def mm(e, po=po, wt=wt, t=t):
                                for k in range(16):
                                    i = e.matmul(po[:], yT[:, k, t * 128:(t + 1) * 128], wt[:, k, :], start=(k == 0), stop=(k == 15))
                                return i
                            S.op("pe", mm, reads=[r_wt, r_yT], writes=[r_po])
                            vst, r_vst = get_vst(nh, dv)
                            evac(vst[:, :, 0:dv], po[:].rearrange("p (h d) -> p h d", h=nh), [r_po], [r_vst])
                            S.dma("sp", v1[tok0 + t * 128:tok0 + (t + 1) * 128, h0:h0 + nh, :], vst[:], reads=[r_vst])
                    elif kind == "tmc":
                        heads = spec[1]
                        hsts = []
                        for hh in range(4):
                            if heads[hh][0] in ("q", "k"):
                                hsts.append(hst[cnt["hst"] % 8])
                                cnt["hst"] += 1
                            else:
                                hsts.append(None)
                        nqk = sum(1 for x in heads if x[0] in ("q", "k"))
                        vheads = [x for x in heads if x[0] == "v"]
                        for t in range(16):
                            po, r_po = po_ring[cnt["po"] % 4]
                            cnt["po"] += 1

                            def mm(e, po=po, wt=wt, t=t):
                                for k in range(16):
                                    i = e.matmul(po[:], yT[:, k, t * 128:(t + 1) * 128], wt[:, k, :], start=(k == 0), stop=(k == 15))
                                return i
                            S.op("pe", mm, reads=[r_wt, r_yT], writes=[r_po])
                            if vheads:
                                nv = len(vheads)
                                vst, r_vst = get_vst(nv, 128)
                                evac(vst[:, :, 0:128], po[:, nqk * 128:512].rearrange("p (h d) -> p h d", h=nv), [r_po], [r_vst])
                                v1 = vheads[0][1]
                                S.dma("sp", v1[tok0 + t * 128:tok0 + (t + 1) * 128, vheads[0][2]:vheads[0][2] + nv, :], vst[:], reads=[r_vst])
                            W = nqk * 128
                            sq, r_sq = sq_ring[cnt["sq"] % 2]
                            ss4, r_ss4 = ss4_ring[cnt["sq"] % 3]
                            cnt["sq"] += 1
                            S.op("act", lambda e, sq=sq, po=po, W=W: e.activation(sq[:, 0:W], po[:, 0:W], AF.Square, scale=128 ** -0.5),
                                 reads=[r_po], writes=[r_sq])
                            S.op("dve", lambda e, sq=sq, ss4=ss4, nqk=nqk, W=W: e.reduce_sum(ss4[:, 0:nqk], sq[:, 0:W].rearrange("p (h d) -> p h d", h=nqk), AX.X),
                                 reads=[r_sq], writes=[r_ss4])
                            S.op("dve", lambda e, ss4=ss4, nqk=nqk: e.tensor_scalar(ss4[:, 0:nqk], ss4[:, 0:nqk], EPS, None, ALU.add), reads=[r_ss4], writes=[r_ss4])
                            S.op("act", lambda e, ss4=ss4, nqk=nqk: e.activation(ss4[:, 0:nqk], ss4[:, 0:nqk], AF.Sqrt), reads=[r_ss4], writes=[r_ss4])
                            S.op("dve", lambda e, ss4=ss4, nqk=nqk: e.reciprocal(ss4[:, 0:nqk], ss4[:, 0:nqk]), reads=[r_ss4], writes=[r_ss4])
                            xn, r_xn = xn_ring[cnt["xn"] % 2]
                            tt, r_tt = tt_ring[cnt["xn"] % 2]
                            xr, r_xr = xr_ring[cnt["xn"] % 2]
                            cnt["xn"] += 1
                            for hh in range(nqk):
                                gbx, r_gbx = (gqb, r_gqb) if heads[hh][0] == "q" else (gkb, r_gkb)
                                S.op("dve", lambda e, xn=xn, po=po, ss4=ss4, hh=hh, gbx=gbx: e.scalar_tensor_tensor(
                                    xn[:, hh, :], po[:, hh * 128:(hh + 1) * 128], ss4[:, hh:hh + 1], gbx[:], ALU.mult, ALU.mult),
                                    reads=[r_po, r_ss4, r_gbx], writes=[r_xn])
                            cs, r_cs = cs_ring[cnt["cs"] % 3]
                            cnt["cs"] += 1
                            S.dma("sp", cs[:, 0, :], cos_d[tok0 + t * 128:tok0 + (t + 1) * 128, :], writes=[r_cs])
                            S.dma("sp", cs[:, 1, :], sin_d[tok0 + t * 128:tok0 + (t + 1) * 128, :], writes=[r_cs])
                            x0 = xn[:, 0:nqk, 0:128:2]
                            x1 = xn[:, 0:nqk, 1:128:2]
                            cb = cs[:, 0:1, :].broadcast_to([128, nqk, 64])
                            sb_ = cs[:, 1:2, :].broadcast_to([128, nqk, 64])
                            S.op("dve", lambda e, tt=tt, x0=x0, cb=cb, nqk=nqk: e.tensor_tensor(tt[:, 0, 0:nqk, :], x0, cb, ALU.mult), reads=[r_xn, r_cs], writes=[r_tt])
                            S.op("dve", lambda e, tt=tt, x1=x1, sb_=sb_, nqk=nqk: e.tensor_tensor(tt[:, 1, 0:nqk, :], x1, sb_, ALU.mult), reads=[r_xn, r_cs], writes=[r_tt])
                            S.op("dve", lambda e, tt=tt, x0=x0, sb_=sb_, nqk=nqk: e.tensor_tensor(tt[:, 2, 0:nqk, :], x0, sb_, ALU.mult), reads=[r_xn, r_cs], writes=[r_tt])
                            S.op("dve", lambda e, tt=tt, x1=x1, cb=cb, nqk=nqk: e.tensor_tensor(tt[:, 3, 0:nqk, :], x1, cb, ALU.mult), reads=[r_xn, r_cs], writes=[r_tt])
                            S.op("dve", lambda e, tt=tt, xr=xr, nqk=nqk: e.tensor_tensor(xr[:, 0:nqk, 0:128:2], tt[:, 0, 0:nqk, :], tt[:, 1, 0:nqk, :], ALU.subtract), reads=[r_tt], writes=[r_xr])
                            S.op("dve", lambda e, tt=tt, xr=xr, nqk=nqk: e.tensor_tensor(xr[:, 0:nqk, 1:128:2], tt[:, 2, 0:nqk, :], tt[:, 3, 0:nqk, :], ALU.add), reads=[r_tt], writes=[r_xr])
                            pTc, r_pTc = pTc_ring[cnt["pTc"] % 2]
                            cnt["pTc"] += 1

                            def tr(e, pTc=pTc, xr=xr, nqk=nqk):
                                for hh in range(nqk):
                                    i = e.transpose(pTc[:, hh, :], xr[:, hh, :], self.ident[:])
                                return i
                            S.op("pe", tr, reads=[r_xr, self.r_ident], writes=[r_pTc])
                            for hh in range(nqk):
                                evac(hsts[hh][0][:, t * 128:(t + 1) * 128], pTc[:, hh, :], [r_pTc], [hsts[hh][1]])
                        for hh in range(nqk):
                            S.dma("sp", self.qt[heads[hh][1], :, tok0:tok0 + 2048], hsts[hh][0][:], reads=[hsts[hh][1]])
            return self.phase_end("proj")

    def attn_full(self, units, dv, bias=None, finish=None, extra_setup=None):
        S = self.S
        with ExitStack() as st:
            vt_ring = self.sb(st, "vt", [128, NT, dv + 1], BF16, 2)
            kt_ring = self.sb(st, "kt", [128, S_LEN], BF16, 2)
            q_ring = self.sb(st, "qsb", [128, 512], BF16, 2)
            pt_ring = self.sb(st, "pt", [128, 512], BF16, 3)
            pss_ring = self.ps(st, "pss", [128, 512], F32, 3)
            acc_ring = self.ps(st, "acc", [128, dv + 1], F32, 4)
            if bias is not None:
                band_d, constb_d = bias
                band_ring = self.sb(st, "band", [128, 2176], F32, 2)
                tmp_ring = self.sb(st, "tmpb", [128, 512], F32, 2)
                nhb = band_d.shape[0]
                (cb, r_cb), = self.sb(st, "constb", [128, nhb * 2], F32)
                S.dma("sp", cb[:], constb_d, writes=[r_cb])
            ctx = extra_setup(st) if extra_setup is not None else None
            cnt = {"vt": 0, "kt": 0, "q": 0, "pt": 0, "pss": 0, "band": 0, "tmp": 0}
            cur_v = None
            cur_band = None
            for u in units:
                vkey = (id(u["v"][0]), u["v"][1])
                if vkey != cur_v:
                    vt, r_vt = vt_ring[cnt["vt"] % 2]
                    cnt["vt"] += 1
                    v1, hidx = u["v"]
                    for part in range(4):
                        S.dma("pool", vt[:, part * 8:(part + 1) * 8, :],
                              v1[part * 1024:(part + 1) * 1024, hidx, :].rearrange("(t p) c -> p t c", p=128), writes=[r_vt])
                    cur_v = vkey
                kt, r_kt = kt_ring[cnt["kt"] % 2]
                cnt["kt"] += 1
                S.dma("sp", kt[:], self.qt[u["kid"]], writes=[r_kt])
                bh = u.get("bias_h")
                if bias is not None and bh != cur_band:
                    band, r_band = band_ring[cnt["band"] % 2]
                    cnt["band"] += 1
                    S.dma("sp", band[:], band_d[bh], writes=[r_band])
                    cur_band = bh
                for qi in range(8):
                    qsb, r_q = q_ring[cnt["q"] % 2]
                    cnt["q"] += 1
                    S.dma("sp", qsb[:], self.qt[u["qid"], :, qi * 512:(qi + 1) * 512], writes=[r_q])
                    for ki in range(NT):
                        pss, r_pss = pss_ring[cnt["pss"] % 3]
                        cnt["pss"] += 1
                        S.op("pe", lambda e, pss=pss, kt=kt, qsb=qsb, ki=ki: e.matmul(pss[:], kt[:, ki * 128:(ki + 1) * 128], qsb[:], start=True, stop=True),
                             reads=[r_kt, r_q], writes=[r_pss])
                        pt, r_pt = pt_ring[cnt["pt"] % 3]
                        cnt["pt"] += 1
                        if bias is None:
                            S.op("act", lambda e, pt=pt, pss=pss: e.activation(pt[:], pss[:], AF.Exp, scale=SCALE), reads=[r_pss], writes=[r_pt])
                        else:
                            delta = ki * 128 - qi * 512
                            if delta >= 1070:
                                S.op("act", lambda e, pt=pt, pss=pss, bh=bh: e.activation(pt[:], pss[:], AF.Exp, bias=cb[:, 2 * bh + 1:2 * bh + 2], scale=SCALE),
                                     reads=[r_pss, r_cb], writes=[r_pt])
                            elif delta <= -686:
                                S.op("act", lambda e, pt=pt, pss=pss, bh=bh: e.activation(pt[:], pss[:], AF.Exp, bias=cb[:, 2 * bh:2 * bh + 1], scale=SCALE),
                                     reads=[r_pss, r_cb], writes=[r_pt])
                            else:
                                s0 = 1024 - delta
                                tmp, r_tmp = tmp_ring[cnt["tmp"] % 2]
                                cnt["tmp"] += 1
                                S.op("dve", lambda e, tmp=tmp, pss=pss, band=band, s0=s0: e.scalar_tensor_tensor(tmp[:], pss[:], SCALE, band[:, s0:s0 + 512], ALU.mult, ALU.add),
                                     reads=[r_pss, r_band], writes=[r_tmp])
                                S.op("act", lambda e, pt=pt, tmp=tmp: e.activation(pt[:], tmp[:], AF.Exp), reads=[r_tmp], writes=[r_pt])
                        for qs in range(4):
                            acc, r_acc = acc_ring[qs]
                            S.op("pe", lambda e, acc=acc, pt=pt, vt=vt, qs=qs, ki=ki: e.matmul(acc[:], pt[:, qs * 128:(qs + 1) * 128], vt[:, ki, :], start=(ki == 0), stop=(ki == NT - 1)),
                                 reads=[r_pt, r_vt], writes=[r_acc])
                    finish(st, ctx, u, qi, acc_ring)
            return self.phase_end("attn_full")

    def attn_B(self, v1b, band_d, constb_d, lq1, lk1, lq2, lk2, subln, lambda_init):
        S = self.S
        units = []
        for h in range(4):
            for m in range(2):
                units.append(dict(qid=48 + h * 2 + m, kid=56 + h * 2 + m, v=(v1b, h), bias_h=h, h=h, m=m))

        def setup(st):
            c = {}
            (lv, r_lv), = self.sb(st, "lv", [128, 4, 128], F32)
            for i, a in enumerate((lq1, lk1, lq2, lk2)):
                S.dma("sp", lv[:, i, :], a.partition_broadcast(128), writes=[r_lv])
            (pr, r_pr), = self.sb(st, "lpr", [128, 2, 128], F32)
            (sv, r_sv), = self.sb(st, "lsv", [128, 2], F32)
            (nlam, r_nlam), = self.sb(st, "nlam", [128, 1], F32)
            S.op("dve", lambda e: e.tensor_tensor(pr[:, 0, :], lv[:, 0, :], lv[:, 1, :], ALU.mult), reads=[r_lv], writes=[r_pr])
            S.op("dve", lambda e: e.tensor_tensor(pr[:, 1, :], lv[:, 2, :], lv[:, 3, :], ALU.mult), reads=[r_lv], writes=[r_pr])
            S.op("dve", lambda e: e.reduce_sum(sv[:], pr[:], AX.X), reads=[r_pr], writes=[r_sv])
            S.op("act", lambda e: e.activation(sv[:], sv[:], AF.Exp), reads=[r_sv], writes=[r_sv])
            S.op("dve", lambda e: e.tensor_tensor(nlam[:], sv[:, 1:2], sv[:, 0:1], ALU.subtract), reads=[r_sv], writes=[r_nlam])
            S.op("dve", lambda e: e.tensor_scalar(nlam[:], nlam[:], -lambda_init, None, ALU.add), reads=[r_nlam], writes=[r_nlam])
            (sg, r_sg), = self.sb(st, "subg", [128, 256], F32)
            S.dma("sp", sg[:], subln.partition_broadcast(128), writes=[r_sg])
            S.op("dve", lambda e: e.tensor_scalar(sg[:], sg[:], 1.0 - lambda_init, None, ALU.mult), reads=[r_sg], writes=[r_sg])
            (o0, r_o0), = self.sb(st, "o0", [128, 32, 256], F32)
            c["nlam"] = (nlam, r_nlam)
            c["sg"] = (sg, r_sg)
            c["o0"] = (o0, r_o0)
            c["rec"] = self.sb(st, "rec", [128, 1], F32, 4)
            c["o1"] = self.sb(st, "o1", [128, 256], F32, 2)
            c["dd"] = self.sb(st, "dd", [128, 256], F32, 2)
            c["jk"] = self.sb(st, "jkb", [128, 256], BF16, 1)[0]
            c["ss"] = self.sb(st, "ssb", [128, 1], F32, 4)
            c["ost"] = self.sb(st, "ostb", [128, 4, 256], BF16, 2)
            c["n"] = 0
            return c

        def finish(st, c, u, qi, acc_ring):
            h, m = u["h"], u["m"]
            o0, r_o0 = c["o0"]
            if m == 1:
                ost, r_ost = c["ost"][qi % 2]
            for qs in range(4):
                acc, r_acc = acc_ring[qs]
                c["n"] += 1
                rec, r_rec = c["rec"][c["n"] % 4]
                S.op("dve", lambda e, rec=rec, acc=acc: e.reciprocal(rec[:], acc[:, 256:257]), reads=[r_acc], writes=[r_rec])
                if m == 0:
                    S.op("dve", lambda e, acc=acc, rec=rec, qi=qi, qs=qs: e.tensor_scalar(o0[:, qi * 4 + qs, :], acc[:, 0:256], rec[:, 0:1], None, ALU.mult),
                         reads=[r_acc, r_rec], writes=[r_o0])
                else:
                    o1, r_o1 = c["o1"][c["n"] % 2]
                    dd, r_dd = c["dd"][c["n"] % 2]
                    ss, r_ss = c["ss"][c["n"] % 4]
                    jk, r_jk = c["jk"]
                    nlam, r_nlam = c["nlam"]
                    sg, r_sg = c["sg"]
                    S.op("dve", lambda e, o1=o1, acc=acc, rec=rec: e.tensor_scalar(o1[:], acc[:, 0:256], rec[:, 0:1], None, ALU.mult), reads=[r_acc, r_rec], writes=[r_o1])
                    S.op("dve", lambda e, dd=dd, o1=o1, qi=qi, qs=qs: e.scalar_tensor_tensor(dd[:], o1[:], nlam[:, 0:1], o0[:, qi * 4 + qs, :], ALU.mult, ALU.add),
                         reads=[r_o1, r_nlam, r_o0], writes=[r_dd])
                    S.op("act", lambda e, jk=jk, dd=dd, ss=ss: e.activation(jk[:], dd[:], AF.Square, scale=1.0 / 16.0, accum_out=ss[:]), reads=[r_dd], writes=[r_jk, r_ss])
                    S.op("dve", lambda e, ss=ss: e.tensor_scalar(ss[:], ss[:], EPS, None, ALU.add), reads=[r_ss], writes=[r_ss])
                    S.op("act", lambda e, ss=ss: e.activation(ss[:], ss[:], AF.Sqrt), reads=[r_ss], writes=[r_ss])
                    S.op("dve", lambda e, ss=ss: e.reciprocal(ss[:], ss[:]), reads=[r_ss], writes=[r_ss])
                    S.op("dve", lambda e, ost=ost, dd=dd, ss=ss, qs=qs: e.scalar_tensor_tensor(ost[:, qs, :], dd[:], ss[:, 0:1], sg[:], ALU.mult, ALU.mult),
                         reads=[r_dd, r_ss, r_sg], writes=[r_ost])
            if m == 1:
                S.dma("sp", self.mixed[qi * 512:(qi + 1) * 512, 1024 + h * 256:1024 + (h + 1) * 256].rearrange("(s p) c -> p s c", p=128),
                      ost[:], reads=[r_ost])

        return self.attn_full(units, 256, bias=(band_d, constb_d), finish=finish, extra_setup=setup)

    def attn_C(self, v1c):
        S = self.S
        units = []
        for n in range(2):
            for g in range(4):
                h = n * 4 + g
                units.append(dict(qid=h, kid=8 + n, v=(v1c, n), bias_h=None, h=h))

        def setup(st):
            c = {}
            c["rec"] = self.sb(st, "rec", [128, 1], F32, 4)
            c["ost"] = self.sb(st, "ostc", [128, 4, 128], BF16, 2)
            c["n"] = 0
            return c

        def finish(st, c, u, qi, acc_ring):
            h = u["h"]
            ost, r_ost = c["ost"][qi % 2]
            for qs in range(4):
                acc, r_acc = acc_ring[qs]
                c["n"] += 1
                rec, r_rec = c["rec"][c["n"] % 4]
                S.op("dve", lambda e, rec=rec, acc=acc: e.reciprocal(rec[:], acc[:, 128:129]), reads=[r_acc], writes=[r_rec])
                S.op("dve", lambda e, acc=acc, rec=rec, ost=ost, qs=qs: e.tensor_scalar(ost[:, qs, :], acc[:, 0:128], rec[:, 0:1], None, ALU.mult),
                     reads=[r_acc, r_rec], writes=[r_ost])
            S.dma("sp", self.mixed[qi * 512:(qi + 1) * 512, h * 128:(h + 1) * 128].rearrange("(s p) c -> p s c", p=128),
                  ost[:], reads=[r_ost])

        return self.attn_full(units, 128, bias=None, finish=finish, extra_setup=setup)

    def attn_A(self, v1a, mba_d, acca):
        S = self.S
        with ExitStack() as st:
            qn_ring = self.sb(st, "qn", [128, S_LEN], BF16, 2)
            kn_ring = self.sb(st, "kn", [128, S_LEN], BF16, 2)
            qp_ring = self.sb(st, "qp", [128, S_LEN], BF16, 2)
            kp_ring = self.sb(st, "kp", [128, 6144], BF16, 2)
            mb_ring = self.sb(st, "mb", [128, 2, 128], F32, 2)
            vt_ring = self.sb(st, "vta", [128, 33, 129], BF16, 3)
            tmp_ring = self.sb(st, "tmpa", [128, 2, 128], F32, 3)
            pt_ring = self.sb(st, "pta", [128, 2, 128], BF16, 3)
            ost_ring = self.sb(st, "osta", [128, 32, 129], F32, 2)
            pss_ring = self.ps(st, "pssa", [128, 2, 128], F32, 3)
            acc_ring = self.ps(st, "acca", [128, 129], F32, 3)
            n_u = 0
            n_v = 0
            n_b = 0
            for g in range(3):
                r = DIL[g]
                L = S_LEN // r
                nb = L // 128
                for h in range(8):
                    qn, r_qn = qn_ring[n_u % 2]
                    kn, r_kn = kn_ring[n_u % 2]
                    qp, r_qp = qp_ring[n_u % 2]
                    kp, r_kp = kp_ring[n_u % 2]
                    mb, r_mb = mb_ring[n_u % 2]
                    n_u += 1
                    S.dma("sp", qn[:], self.qt[g * 16 + h], writes=[r_qn])
                    S.dma("sp", kn[:], self.qt[g * 16 + 8 + h], writes=[r_kn])
                    S.dma("sp", mb[:], mba_d[g * 8 + h], writes=[r_mb])
                    qpv = qp[:].rearrange("d (r n) -> d r n", r=r)
                    kpv = kp[:, 0:r * (L + 128)].rearrange("d (r n) -> d r n", r=r)
                    S.op("pool", lambda e, qpv=qpv, qn=qn, r=r: e.tensor_copy(qpv, qn[:].rearrange("d (n r) -> d r n", r=r)), reads=[r_qn], writes=[r_qp])

                    def kperm(e, kpv=kpv, kn=kn, r=r, L=L):
                        e.memset(kpv[:, :, 0:64], 0.0)
                        e.memset(kpv[:, :, 64 + L:128 + L], 0.0)
                        return e.tensor_copy(kpv[:, :, 64:64 + L], kn[:].rearrange("d (n r) -> d r n", r=r))
                    S.op("pool", kperm, reads=[r_kn], writes=[r_kp])
                    vsrc = v1a[g, :, h, :].rearrange("(n r) c -> r n c", r=r)
                    for rho in range(r):
                        vt, r_vt = vt_ring[n_v % 3]
                        ost, r_ost = ost_ring[n_v % 2]
                        n_v += 1

                        def vz(e, vt=vt, nb=nb):
                            e.memset(vt[0:64, 0, :], 0.0)
                            return e.memset(vt[64:128, nb, :], 0.0)
                        S.op("pool", vz, writes=[r_vt])
                        S.dma("pool", vt[64:128, 0, :], vsrc[rho, 0:64, :], writes=[r_vt])
                        S.dma("pool", vt[:, 1:nb, :], vsrc[rho, 64:L - 64, :].rearrange("(i p) c -> p i c", p=128), writes=[r_vt])
                        S.dma("pool", vt[0:64, nb, :], vsrc[rho, L - 64:L, :], writes=[r_vt])
                        for b in range(nb):
                            pss, r_pss = pss_ring[n_b % 3]
                            tmp, r_tmp = tmp_ring[n_b % 3]
                            pt, r_pt = pt_ring[n_b % 3]
                            acc, r_acc = acc_ring[n_b % 3]
                            n_b += 1

                            def qk(e, pss=pss, kpv=kpv, qpv=qpv, rho=rho, b=b):
                                e.matmul(pss[:, 0, :], kpv[:, rho, b * 128:(b + 1) * 128], qpv[:, rho, b * 128:(b + 1) * 128], start=True, stop=True)
                                return e.matmul(pss[:, 1, :], kpv[:, rho, (b + 1) * 128:(b + 2) * 128], qpv[:, rho, b * 128:(b + 1) * 128], start=True, stop=True)
                            S.op("pe", qk, reads=[r_kp, r_qp], writes=[r_pss])
                            S.op("dve", lambda e, tmp=tmp, pss=pss, mb=mb: e.scalar_tensor_tensor(tmp[:], pss[:], SCALE, mb[:], ALU.mult, ALU.add),
                                 reads=[r_pss, r_mb], writes=[r_tmp])
                            S.op("act", lambda e, pt=pt, tmp=tmp: e.activation(pt[:], tmp[:], AF.Exp), reads=[r_tmp], writes=[r_pt])

                            def pv(e, acc=acc, pt=pt, vt=vt, b=b):
                                e.matmul(acc[:], pt[:, 0, :], vt[:, b, :], start=True, stop=False)
                                return e.matmul(acc[:], pt[:, 1, :], vt[:, b + 1, :], start=False, stop=True)
                            S.op("pe", pv, reads=[r_pt, r_vt], writes=[r_acc])
                            if n_b % 2:
                                S.op("act", lambda e, ost=ost, acc=acc, b=b: e.activation(ost[:, b, :], acc[:], AF.Copy), reads=[r_acc], writes=[r_ost])
                            else:
                                S.op("dve", lambda e, ost=ost, acc=acc, b=b: e.tensor_copy(ost[:, b, :], acc[:]), reads=[r_acc], writes=[r_ost])
                        dst = acca[g].rearrange("(b p r) h c -> r p b h c", p=128, r=r)[rho][:, :, h, :]
                        S.dma("sp", dst, ost[:, 0:nb, :], reads=[r_ost])
            if self.phase_end("attn_a"):
                return True
        with ExitStack() as st:
            a_ring = self.sb(st, "ca", [128, 3, 8, 129], F32, 2)
            s_ring = self.sb(st, "cs_", [128, 8, 129], F32, 2)
            rc_ring = self.sb(st, "crc", [128, 8], F32, 2)
            o_ring = self.sb(st, "co", [128, 8, 128], BF16, 2)
            for t in range(NT):
                a, r_a = a_ring[t % 2]
                sm, r_sm = s_ring[t % 2]
                rc, r_rc = rc_ring[t % 2]
                o, r_o = o_ring[t % 2]
                for g in range(3):
                    S.dma("sp", a[:, g, :, :], acca[g, t * 128:(t + 1) * 128, :, :], writes=[r_a])
                S.op("dve", lambda e, sm=sm, a=a: e.tensor_tensor(sm[:], a[:, 0, :, :], a[:, 1, :, :], ALU.add), reads=[r_a], writes=[r_sm])
                S.op("dve", lambda e, sm=sm, a=a: e.tensor_tensor(sm[:], sm[:], a[:, 2, :, :], ALU.add), reads=[r_a, r_sm], writes=[r_sm])
                S.op("dve", lambda e, sm=sm, rc=rc: e.reciprocal(rc[:], sm[:, :, 128]), reads=[r_sm], writes=[r_rc])
                for h in range(8):
                    eng = "dve" if h % 2 else "pool"
                    S.op(eng, lambda e, o=o, sm=sm, rc=rc, h=h: e.tensor_scalar(o[:, h, :], sm[:, h, 0:128], rc[:, h:h + 1], None, ALU.mult),
                         reads=[r_sm, r_rc], writes=[r_o])
                S.dma("sp", self.mixed[t * 128:(t + 1) * 128, 0:1024], o[:].rearrange("p h d -> p (h d)"), reads=[r_o])
            return self.phase_end("comb_a")

    def outproj_phase(self, h_src, wb_out, h_dst):
        S = self.S
        with ExitStack() as st:
            bufs = self.load_T_bufs(st, False, BF16)
            (mT, r_mT), = self.sb(st, "mT", [128, 16, 2048], BF16)
            wt_ring = self.sb(st, "wto", [128, 16, 512], BF16, 2)
            po_ring = self.ps(st, "poo", [128, 512], F32, 4)
            hr_ring = self.sb(st, "hr", [128, 512], F32, 3)
            n_w = 0
            n_p = 0
            for c in range(2):
                tok0 = c * 2048
                self.load_T(st, self.mixed, tok0, 2048, mT, r_mT, 0, bufs=bufs)
                for s in range(4):
                    wt, r_wt = wt_ring[n_w % 2]
                    n_w += 1
                    S.dma("sp", wt[:], wb_out[s], writes=[r_wt])
                    for t in range(16):
                        po, r_po = po_ring[n_p % 4]
                        hr, r_hr = hr_ring[n_p % 3]
                        n_p += 1
                        rows = slice(tok0 + t * 128, tok0 + (t + 1) * 128)
                        S.dma("act", hr[:], h_src[rows, s * 512:(s + 1) * 512], writes=[r_hr])

                        def mm(e, po=po, wt=wt, t=t):
                            for k in range(16):
                                i = e.matmul(po[:], mT[:, k, t * 128:(t + 1) * 128], wt[:, k, :], start=(k == 0), stop=(k == 15))
                            return i
                        S.op("pe", mm, reads=[r_wt, r_mT], writes=[r_po])
                        S.op("dve", lambda e, hr=hr, po=po: e.tensor_tensor(hr[:], hr[:], po[:], ALU.add), reads=[r_po, r_hr], writes=[r_hr])
                        S.dma("sp", h_dst[rows, s * 512:(s + 1) * 512], hr[:], reads=[r_hr])
            return self.phase_end("outproj")

    def ffn_phase(self, h_src, gain_ap, wb_up, wb_dn, conv_w, conv_b, h_dst):
        S = self.S
        NFC = DFF // 128
        with ExitStack() as st:
            bufs = self.load_T_bufs(st, True, F32)
            gb, r_gb = self.load_gain(st, gain_ap)
            yT_ring = self.sb(st, "y2T", [128, 16, 514], BF16, 1)
            guT_ring = self.sb(st, "guT", [128, NFC, 512], BF16, 1)
            wg_ring = self.sb(st, "wg", [128, 16, 256], BF16, 2)
            wu_ring = self.sb(st, "wu", [128, 16, 256], BF16, 2)
            wd_ring = self.sb(st, "wd", [128, NFC, 256], BF16, 2)
            (cw, r_cw), = self.sb(st, "cw", [128, 4, NFC], F32)
            for i in range(3):
                S.dma("sp", cw[:, i, :], conv_w[i].rearrange("(c p) -> p c", p=128), writes=[r_cw], allow_slow_non_contiguous=True)
            S.dma("sp", cw[:, 3, :], conv_b.rearrange("(c p) -> p c", p=128), writes=[r_cw], allow_slow_non_contiguous=True)
            gs_ring = self.sb(st, "gs", [128, 516], F32, 2)
            t1_ring = self.sb(st, "t1", [128, 512], F32, 2)
            t2_ring = self.sb(st, "t2", [128, 512], F32, 2)
            gl_ring = self.sb(st, "gl", [128, 512], F32, 2)
            hr_ring = self.sb(st, "hrf", [128, 256], F32, 3)
            psg_ring = self.ps(st, "psg", [128, 512], F32, 2)
            psu_ring = self.ps(st, "psu", [128, 512], F32, 2)
            psh_ring = self.ps(st, "psh", [128, 2], F32, 1)
            psd_ring = self.ps(st, "psd", [128, 256], F32, 1)
            n = {"w": 0, "f": 0, "wd": 0, "d": 0}
            for c in range(S_LEN // 512):
                c0 = c * 512
                yT, r_yT = yT_ring[0]
                guT, r_guT = guT_ring[0]
                self.load_T(st, h_src, c0, 512, yT, r_yT, 0, gain_b=gb, r_gain=r_gb, bufs=bufs)
                self.load_T(st, h_src, c0 - 1, 1, yT, r_yT, 512, gain_b=gb, r_gain=r_gb, bufs=bufs)
                self.load_T(st, h_src, c0 + 512, 1, yT, r_yT, 513, gain_b=gb, r_gain=r_gb, bufs=bufs)
                for sl in range(22):
                    wg, r_wg = wg_ring[n["w"] % 2]
                    wu, r_wu = wu_ring[n["w"] % 2]
                    n["w"] += 1
                    S.dma("sp", wg[:], wb_up[sl], writes=[r_wg])
                    S.dma("act", wu[:], wb_up[22 + sl], writes=[r_wu])
                    for j in range(2):
                        fc = sl * 2 + j
                        psg, r_psg = psg_ring[n["f"] % 2]
                        psu, r_psu = psu_ring[n["f"] % 2]
                        psh, r_psh = psh_ring[0]
                        gs, r_gs = gs_ring[n["f"] % 2]
                        t1, r_t1 = t1_ring[n["f"] % 2]
                        t2, r_t2 = t2_ring[n["f"] % 2]
                        gl, r_gl = gl_ring[n["f"] % 2]
                        n["f"] += 1

                        def mmg(e, psg=psg, psh=psh, wg=wg, j=j):
                            for k in range(16):
                                e.matmul(psg[:], wg[:, k, j * 128:(j + 1) * 128], yT[:, k, 0:512], start=(k == 0), stop=(k == 15))
                            for k in range(16):
                                i = e.matmul(psh[:], wg[:, k, j * 128:(j + 1) * 128], yT[:, k, 512:514], start=(k == 0), stop=(k == 15))
                            return i
                        S.op("pe", mmg, reads=[r_wg, r_yT], writes=[r_psg, r_psh])

                        def mmu(e, psu=psu, wu=wu, j=j):
                            for k in range(16):
                                i = e.matmul(psu[:], wu[:, k, j * 128:(j + 1) * 128], yT[:, k, 0:512], start=(k == 0), stop=(k == 15))
                            return i
                        S.op("pe", mmu, reads=[r_wu, r_yT], writes=[r_psu])
                        S.op("act", lambda e, gs=gs, psg=psg: e.activation(gs[:, 1:513], psg[:], AF.Copy), reads=[r_psg], writes=[r_gs])

                        def halo(e, gs=gs, psh=psh):
                            e.tensor_copy(gs[:, 0:1], psh[:, 0:1])
                            return e.tensor_copy(gs[:, 513:514], psh[:, 1:2])
                        S.op("dve", halo, reads=[r_psh], writes=[r_gs])
                        S.op("act", lambda e, t1=t1, gs=gs, fc=fc: e.activation(t1[:], gs[:, 1:513], AF.Identity, bias=cw[:, 3, fc:fc + 1], scale=cw[:, 1, fc:fc + 1]),
                             reads=[r_gs, r_cw], writes=[r_t1])
                        S.op("dve", lambda e, t2=t2, gs=gs, t1=t1, fc=fc: e.scalar_tensor_tensor(t2[:], gs[:, 0:512], cw[:, 0, fc:fc + 1], t1[:], ALU.mult, ALU.add),
                             reads=[r_gs, r_t1, r_cw], writes=[r_t2])
                        S.op("dve", lambda e, t1=t1, gs=gs, t2=t2, fc=fc: e.scalar_tensor_tensor(t1[:], gs[:, 2:514], cw[:, 2, fc:fc + 1], t2[:], ALU.mult, ALU.add),
                             reads=[r_gs, r_t2, r_cw], writes=[r_t1])
                        S.op("act", lambda e, gl=gl, t1=t1: e.activation(gl[:], t1[:], AF.Gelu_apprx_tanh), reads=[r_t1], writes=[r_gl])
                        S.op("dve", lambda e, gl=gl, psu=psu, fc=fc: e.tensor_tensor(guT[:, fc, :], gl[:], psu[:], ALU.mult), reads=[r_gl, r_psu], writes=[r_guT])
                for ds in range(8):
                    wd, r_wd = wd_ring[n["wd"] % 2]
                    n["wd"] += 1
                    S.dma("sp", wd[:, 0:22, :], wb_dn[ds, :, 0:22, :], writes=[r_wd])
                    S.dma("act", wd[:, 22:44, :], wb_dn[ds, :, 22:44, :], writes=[r_wd])
                    for tt in range(4):
                        psd, r_psd = psd_ring[0]
                        hr, r_hr = hr_ring[n["d"] % 3]
                        n["d"] += 1
                        rows = slice(c0 + tt * 128, c0 + (tt + 1) * 128)
                        S.dma("sp", hr[:], h_src[rows, ds * 256:(ds + 1) * 256], writes=[r_hr])

                        def mmd(e, psd=psd, wd=wd, tt=tt):
                            for k in range(NFC):
                                i = e.matmul(psd[:], guT[:, k, tt * 128:(tt + 1) * 128], wd[:, k, :], start=(k == 0), stop=(k == NFC - 1))
                            return i
                        S.op("pe", mmd, reads=[r_wd, r_guT], writes=[r_psd])
                        S.op("dve", lambda e, hr=hr, psd=psd: e.tensor_tensor(hr[:], hr[:], psd[:], ALU.add), reads=[r_psd, r_hr], writes=[r_hr])
                        S.dma("sp", h_dst[rows, ds * 256:(ds + 1) * 256], hr[:], reads=[r_hr])
            return self.phase_end("ffn")

    def final_norm(self, h_src, gain_ap, out):
        S = self.S
        with ExitStack() as st:
            gb, r_gb = self.load_gain(st, gain_ap)
            ht_ring = self.sb(st, "fht", [128, D], F32, 3)
            junk = self.sb(st, "fjk", [128, D], BF16, 1)[0]
            ssr = self.sb(st, "fss", [128, 1], F32, 4)
            for t in range(NT):
                ht, r_ht = ht_ring[t % 3]
                ss, r_ss = ssr[t % 4]
                jk, r_jk = junk
                S.dma("sp", ht[:], h_src[t * 128:(t + 1) * 128, :], writes=[r_ht])
                S.op("act", lambda e, ht=ht, jk=jk, ss=ss: e.activation(jk[:], ht[:], AF.Square, scale=D ** -0.5, accum_out=ss[:]), reads=[r_ht], writes=[r_jk, r_ss])
                S.op("dve", lambda e, ss=ss: e.tensor_scalar(ss[:], ss[:], EPS, None, ALU.add), reads=[r_ss], writes=[r_ss])
                S.op("act", lambda e, ss=ss: e.activation(ss[:], ss[:], AF.Sqrt), reads=[r_ss], writes=[r_ss])
                S.op("dve", lambda e, ss=ss: e.reciprocal(ss[:], ss[:]), reads=[r_ss], writes=[r_ss])
                S.op("dve", lambda e, ht=ht, ss=ss: e.scalar_tensor_tensor(ht[:], ht[:], ss[:, 0:1], gb[:], ALU.mult, ALU.mult), reads=[r_ht, r_ss, r_gb], writes=[r_ht])
                S.dma("act", out[t * 128:(t + 1) * 128, :], ht[:], reads=[r_ht])
            return self.phase_end("final")


def _t5_bucket_np(rel):
    nb = 16
    max_exact = 8
    rel = np.asarray(rel, dtype=np.int64)
    ret = np.where(rel > 0, nb, 0)
    n = np.abs(rel)
    n_f = np.maximum(n, 1).astype(np.float32)
    large = max_exact + (np.log(n_f / np.float32(max_exact)) / np.float32(math.log(1024 / max_exact)) * np.float32(nb - max_exact)).astype(np.int32)
    large = np.minimum(large, nb - 1)
    return ret + np.where(n < max_exact, n, large)


def host_tables(t5_table, na_rpb):
    t5 = np.asarray(t5_table, dtype=np.float32)
    out = {}
    i = np.arange(128)[:, None]
    c = np.arange(2176)[None, :]
    bk = _t5_bucket_np(i - c + 1024)
    out["bandB"] = np.ascontiguousarray(np.stack([t5[bk, 24 + h] for h in range(4)]).astype(np.float32))
    cb = np.zeros((128, 8), np.float32)
    for h in range(4):
        cb[:, 2 * h] = t5[15, 24 + h]
        cb[:, 2 * h + 1] = t5[31, 24 + h]
    out["constB"] = cb
    p = np.arange(128)[:, None, None]
    t = np.arange(2)[None, :, None]
    j = np.arange(128)[None, None, :]
    rel_sub = (p - 64 + 128 * t) - j
    valid = np.abs(rel_sub) <= 64
    mba = np.zeros((24, 128, 2, 128), np.float32)
    for g, r in enumerate((1, 4, 16)):
        bk = _t5_bucket_np(r * rel_sub)
        for h in range(8):
            mba[g * 8 + h] = np.where(valid, t5[bk, g * 8 + h], np.float32(NEG))
    out["mbA"] = mba
    rpb = np.asarray(na_rpb, dtype=np.float32)[0]
    e = (np.arange(128) // 64)[:, None, None]
    bp = (np.arange(128) % 64)[:, None, None]
    a = np.arange(4)[None, :, None]
    jj = np.arange(64)[None, None, :]
    cs = np.clip(jj - 8, 0, 48)
    validc = (bp >= cs) & (bp < cs + 16)
    dc = np.clip(bp - jj + 15, 0, 30)
    mbd = np.zeros((8, 8, 128, 4, 64), np.float32)
    for di in range(8):
        delta = -di
        dr = delta + 2 * a + e + 7
        drc = np.clip(dr, 0, 14) + 0 * jj
        for h in range(8):
            vals = rpb[h][drc, dc + 0 * a]
            mbd[h, di] = np.where(validc & (dr >= 0) & (dr <= 14), vals, np.float32(NEG))
    out["mbD"] = mbd
    tpos = np.arange(S_LEN)
    row = (tpos // 64).astype(np.float32)
    col = (tpos % 64).astype(np.float32)
    inv_freq = (np.float32(10000.0) ** (-(np.arange(0, 64, 2, dtype=np.float32) / np.float32(64)))).astype(np.float32)
    ang = np.concatenate([row[:, None] * inv_freq[None], col[:, None] * inv_freq[None]], axis=-1).astype(np.float32)
    out["ropec"] = np.cos(ang).astype(np.float32)
    out["ropes"] = np.sin(ang).astype(np.float32)
    return out


def _attn_D(self, v1d, mbd_d):
    S = self.S
    with ExitStack() as st:
        qn_ring = self.sb(st, "qd", [128, S_LEN], BF16, 2)
        kn_ring = self.sb(st, "kd", [128, S_LEN], BF16, 2)
        v0_ring = self.sb(st, "vd0", [128, 32, 129], BF16, 2)
        v1_ring = self.sb(st, "vd1", [128, 31, 129], BF16, 2)
        mb_ring = self.sb(st, "mbd", [128, 8, 4, 64], F32, 2)
        tmp_ring = self.sb(st, "tmpd", [128, 4, 64], F32, 3)
        pt_ring = self.sb(st, "ptd", [128, 4, 64], BF16, 3)
        rec_ring = self.sb(st, "recd", [64, 1], F32, 4)
        ost_ring = self.sb(st, "ostd", [64, 64, 128], BF16, 2)
        pss_ring = self.ps(st, "pssd", [128, 4, 64], F32, 3)
        acc_ring = self.ps(st, "accd", [64, 129], F32, 3)
        n_b = 0
        for h in range(8):
            qn, r_qn = qn_ring[h % 2]
            kn, r_kn = kn_ring[h % 2]
            v0, r_v0 = v0_ring[h % 2]
            v1, r_v1 = v1_ring[h % 2]
            mb, r_mb = mb_ring[h % 2]
            ost, r_ost = ost_ring[h % 2]
            S.dma("sp", qn[:], self.qt[10 + h], writes=[r_qn])
            S.dma("sp", kn[:], self.qt[18 + h], writes=[r_kn])
            S.dma("sp", mb[:], mbd_d[h].rearrange("d p a j -> p d a j"), writes=[r_mb])
            for part in range(4):
                S.dma("pool", v0[:, part * 8:(part + 1) * 8, :], v1d[part * 1024:(part + 1) * 1024, h, :].rearrange("(t p) c -> p t c", p=128), writes=[r_v0])
            S.dma("pool", v1[:, 0:16, :], v1d[64:64 + 2048, h, :].rearrange("(t p) c -> p t c", p=128), writes=[r_v1])
            S.dma("pool", v1[:, 16:31, :], v1d[64 + 2048:64 + 2048 + 1920, h, :].rearrange("(t p) c -> p t c", p=128), writes=[r_v1])
            for i in range(64):
                rs = min(max(i - 4, 0), 56)
                di = i - rs
                pss, r_pss = pss_ring[n_b % 3]
                tmp, r_tmp = tmp_ring[n_b % 3]
                pt, r_pt = pt_ring[n_b % 3]
                acc, r_acc = acc_ring[n_b % 3]
                rec, r_rec = rec_ring[n_b % 4]
                n_b += 1

                def qk(e, pss=pss, kn=kn, qn=qn, rs=rs, i=i):
                    for a in range(4):
                        k0 = 64 * rs + 128 * a
                        ins = e.matmul(pss[:, a, :], kn[:, k0:k0 + 128], qn[:, 64 * i:64 * i + 64], start=True, stop=True)
                    return ins
                S.op("pe", qk, reads=[r_kn, r_qn], writes=[r_pss])
                S.op("dve", lambda e, tmp=tmp, pss=pss, mb=mb, di=di: e.scalar_tensor_tensor(tmp[:], pss[:], SCALE, mb[:, di, :, :], ALU.mult, ALU.add),
                     reads=[r_pss, r_mb], writes=[r_tmp])
                S.op("act", lambda e, pt=pt, tmp=tmp: e.activation(pt[:], tmp[:], AF.Exp), reads=[r_tmp], writes=[r_pt])
                if rs % 2 == 0:
                    vv, r_vv, tb = v0, r_v0, rs // 2
                else:
                    vv, r_vv, tb = v1, r_v1, (rs - 1) // 2

                def pv(e, acc=acc, pt=pt, vv=vv, tb=tb):
                    for a in range(4):
                        ins = e.matmul(acc[:], pt[:, a, :], vv[:, tb + a, :], start=(a == 0), stop=(a == 3))
                    return ins
                S.op("pe", pv, reads=[r_pt, r_vv], writes=[r_acc])
                S.op("dve", lambda e, rec=rec, acc=acc: e.reciprocal(rec[:], acc[:, 128:129]), reads=[r_acc], writes=[r_rec])
                S.op("dve", lambda e, ost=ost, acc=acc, rec=rec, i=i: e.tensor_scalar(ost[:, i, :], acc[:, 0:128], rec[:, 0:1], None, ALU.mult),
                     reads=[r_acc, r_rec], writes=[r_ost])
            S.dma("sp", self.mixed[:, 1024 + h * 128:1024 + (h + 1) * 128].rearrange("(i p) c -> p i c", p=64), ost[:], reads=[r_ost])
        return self.phase_end("attn_d")


KB.attn_D = _attn_D


def build(debug=(), stop_after=None, l1_only=False):
    kb = KB(debug, stop_after)
    nc = kb.nc
    S = kb.S
    I = {}
    I["x"] = kb.inp("x", [S_LEN, D])
    I["ln_mix"] = kb.inp("ln_mix", [2, D])
    I["ln_ffn"] = kb.inp("ln_ffn", [2, D])
    I["ln_final"] = kb.inp("ln_final", [D])
    I["ev_w_in"] = kb.inp("ev_w_in", [D, 12288])
    I["ev_w_out"] = kb.inp("ev_w_out", [D, D])
    for nme in ("diff_lq1", "diff_lk1", "diff_lq2", "diff_lk2"):
        I[nme] = kb.inp(nme, [128])
    I["diff_subln"] = kb.inp("diff_subln", [256])
    I["od_w_in"] = kb.inp("od_w_in", [D, 4608])
    I["od_w_out"] = kb.inp("od_w_out", [D, D])
    I["gqa_q_norm"] = kb.inp("gqa_q_norm", [128])
    I["gqa_k_norm"] = kb.inp("gqa_k_norm", [128])
    I["ffn_w_up"] = kb.inp("ffn_w_up", [2, D, 2 * DFF])
    I["ffn_conv_w"] = kb.inp("ffn_conv_w", [2, 3, DFF])
    I["ffn_conv_b"] = kb.inp("ffn_conv_b", [2, DFF])
    I["ffn_w_down"] = kb.inp("ffn_w_down", [2, DFF, D])
    I["bandB"] = kb.inp("bandB", [4, 128, 2176])
    I["constB"] = kb.inp("constB", [128, 8])
    I["mbA"] = kb.inp("mbA", [24, 128, 2, 128])
    I["mbD"] = kb.inp("mbD", [8, 8, 128, 4, 64])
    I["ropec"] = kb.inp("ropec", [S_LEN, 64])
    I["ropes"] = kb.inp("ropes", [S_LEN, 64])
    out = nc.dram_tensor("out", [S_LEN, D], F32, kind="ExternalOutput").ap()

    kb.qt = kb.dram("qt", [64, 128, S_LEN], BF16)
    kb.mixed = kb.dram("mixed", [S_LEN, D], BF16)
    v1a = kb.dram("v1a", [3, S_LEN, 8, 129], BF16)
    v1b = kb.dram("v1b", [S_LEN, 4, 257], BF16)
    v1c = kb.dram("v1c", [S_LEN, 2, 129], BF16)
    v1d = kb.dram("v1d", [S_LEN, 8, 129], BF16)
    acca = kb.dram("acca", [3, S_LEN, 8, 129], F32)
    hA = kb.dram("hA", [S_LEN, D], F32)
    hB = kb.dram("hB", [S_LEN, D], F32)

    def done():
        kb.es.close()
        return nc

    kb.setup_consts()
    if l1_only:
        hB = kb.inp("hB_in", [S_LEN, D])
    if not l1_only:
        wb_in0 = kb.precast(I["ev_w_in"], "wb_in0", D, 12288, 512)
        wb_out0 = kb.precast(I["ev_w_out"], "wb_out0", D, D, 512)
        wb_up0 = kb.precast(I["ffn_w_up"][0], "wb_up0", D, 2 * DFF, 256)
        wb_dn0 = kb.precast(I["ffn_w_down"][0], "wb_dn0", DFF, D, 256, kstep=11)
    wb_in1 = kb.precast(I["od_w_in"], "wb_in1", D, 4608, 512)
    wb_out1 = kb.precast(I["od_w_out"], "wb_out1", D, D, 512)
    wb_up1 = kb.precast(I["ffn_w_up"][1], "wb_up1", D, 2 * DFF, 256)
    wb_dn1 = kb.precast(I["ffn_w_down"][1], "wb_dn1", DFF, D, 256, kstep=11)
    if kb.phase_end("precast"):
        return done()

    if l1_only:
        return _layer1(kb, I, hA, hB, v1c, v1d, wb_in1, wb_out1, wb_up1, wb_dn1, out, done)
    slabs0 = []
    for g in range(3):
        slabs0.append(("fm", [g * 16 + h for h in range(0, 4)]))
        slabs0.append(("fm", [g * 16 + h for h in range(4, 8)]))
        slabs0.append(("fm", [g * 16 + 8 + h for h in range(0, 4)]))
        slabs0.append(("fm", [g * 16 + 8 + h for h in range(4, 8)]))
        slabs0.append(("tmv", v1a[g], 0, 4, 128))
        slabs0.append(("tmv", v1a[g], 4, 4, 128))
    slabs0.append(("fm", [48 + i for i in range(0, 4)]))
    slabs0.append(("fm", [48 + i for i in range(4, 8)]))
    slabs0.append(("fm", [56 + i for i in range(0, 4)]))
    slabs0.append(("fm", [56 + i for i in range(4, 8)]))
    slabs0.append(("tmv", v1b, 0, 2, 256))
    slabs0.append(("tmv", v1b, 2, 2, 256))
    if kb.proj_phase(I["x"], I["ln_mix"][0], wb_in0, slabs0):
        return done()
    if kb.attn_A(v1a, I["mbA"], acca):
        return done()
    lambda_init0 = 0.8 - 0.6 * math.exp(-0.3 * 0)
    if kb.attn_B(v1b, I["bandB"], I["constB"], I["diff_lq1"], I["diff_lk1"], I["diff_lq2"], I["diff_lk2"], I["diff_subln"], lambda_init0):
        return done()
    if kb.outproj_phase(I["x"], wb_out0, hA):
        return done()
    if kb.ffn_phase(hA, I["ln_ffn"][0], wb_up0, wb_dn0, I["ffn_conv_w"][0], I["ffn_conv_b"][0], hB):
        return done()
    return _layer1(kb, I, hA, hB, v1c, v1d, wb_in1, wb_out1, wb_up1, wb_dn1, out, done)


def _layer1(kb, I, hA, hB, v1c, v1d, wb_in1, wb_out1, wb_up1, wb_dn1, out, done):
    slabs1 = [
        ("tmc", [("q", 0), ("q", 1), ("q", 2), ("q", 3)]),
        ("tmc", [("q", 4), ("q", 5), ("q", 6), ("q", 7)]),
        ("tmc", [("k", 8), ("k", 9), ("v", v1c, 0), ("v", v1c, 1)]),
        ("fm", [10, 11, 12, 13]), ("fm", [14, 15, 16, 17]),
        ("fm", [18, 19, 20, 21]), ("fm", [22, 23, 24, 25]),
        ("tmv", v1d, 0, 4, 128), ("tmv", v1d, 4, 4, 128),
    ]
    if kb.proj_phase(hB, I["ln_mix"][1], wb_in1, slabs1, rope=(I["ropec"], I["ropes"], I["gqa_q_norm"], I["gqa_k_norm"])):
        return done()
    if kb.attn_C(v1c):
        return done()
    if kb.attn_D(v1d, I["mbD"]):
        return done()
    if kb.outproj_phase(hB, wb_out1, hA):
        return done()
    if kb.ffn_phase(hA, I["ln_ffn"][1], wb_up1, wb_dn1, I["ffn_conv_w"][1], I["ffn_conv_b"][1], hB):
        return done()
    kb.final_norm(hB, I["ln_final"], out)
    return done()


_AUX_CACHE = {}


def make_in_maps(inputs, ncores=8):
    aux = host_tables(inputs["t5_table"], inputs["na_rpb"])
    f = lambda a: np.ascontiguousarray(np.asarray(a, dtype=np.float32))
    shared = {
        "ln_mix": f(inputs["ln_mix"]), "ln_ffn": f(inputs["ln_ffn"]), "ln_final": f(inputs["ln_final"]),
        "ev_w_in": f(inputs["ev_w_in"][0]), "ev_w_out": f(inputs["ev_w_out"][0]),
        "diff_lq1": f(inputs["diff_lq1"][0]), "diff_lk1": f(inputs["diff_lk1"][0]),
        "diff_lq2": f(inputs["diff_lq2"][0]), "diff_lk2": f(inputs["diff_lk2"][0]),
        "diff_subln": f(inputs["diff_subln"][0]),
        "od_w_in": f(inputs["od_w_in"][0]), "od_w_out": f(inputs["od_w_out"][0]),
        "gqa_q_norm": f(inputs["gqa_q_norm"][0]), "gqa_k_norm": f(inputs["gqa_k_norm"][0]),
        "ffn_w_up": f(inputs["ffn_w_up"]), "ffn_conv_w": f(inputs["ffn_conv_w"]), "ffn_conv_b": f(inputs["ffn_conv_b"]),
        "ffn_w_down": f(inputs["ffn_w_down"]),
    }
    shared.update(aux)
    x = np.asarray(inputs["x"], dtype=np.float32)
    maps = []
    for c in range(ncores):
        m = dict(shared)
        m["x"] = np.ascontiguousarray(x[c % 4])
        maps.append(m)
    return maps


def kernel(**inputs):
    nc = build()
    maps = make_in_maps(inputs, 8)
    res = run_bass_kernel_spmd(nc, maps, core_ids=list(range(8)))
    out = np.stack([np.asarray(res.results[c]["out"], dtype=np.float32) for c in range(4)], axis=0)
    return out
```

```python
import numpy as np
import concourse.bass as bass
import concourse.mybir as mybir

F32 = mybir.dt.float32
BF16 = mybir.dt.bfloat16
AF = mybir.ActivationFunctionType
ALU = mybir.AluOpType
AX = mybir.AxisListType

ENGS = ("pe", "act", "dve", "pool", "sp")
N_DMA_SEMS = 24
N_BG_SEMS = 12
SYNC_SAME_ENGINE = True


class Res:
    __slots__ = ("lw", "rd", "name", "excl")

    def __init__(self, name="", excl=False):
        self.lw = None
        self.rd = []
        self.name = name
        self.excl = excl


class Op:
    __slots__ = ("eng", "fn", "deps", "signaled", "count", "is_dma", "dsem", "dtarget", "prev_same_sem", "idx", "epoch", "bg")


class Sched:
    def __init__(self, nc, es):
        self.nc = nc
        self.ops = {e: [] for e in ENGS}
        self.emitted = {e: 0 for e in ENGS}
        self.sig_count = {e: 0 for e in ENGS}
        self.esem = {e: es.enter_context(nc.semaphore("s_" + e)) for e in ENGS if e != "sp"}
        self.dsems = [es.enter_context(nc.semaphore("d%d" % i)) for i in range(N_DMA_SEMS)]
        self.dsem_count = [0] * N_DMA_SEMS
        self.dsem_last = [None] * N_DMA_SEMS
        self.dma_rr = 0
        self.bsems = [es.enter_context(nc.semaphore("b%d" % i)) for i in range(N_BG_SEMS)]
        self.bsem_count = [0] * N_BG_SEMS
        self.bsem_last = [None] * N_BG_SEMS
        self.bg_rr = 0
        self.waited = {e: {} for e in ENGS}
        self.last_op = {e: None for e in ENGS}
        self.all_dma = []
        self.epoch = 0

    def engine(self, e):
        nc = self.nc
        return {"pe": nc.tensor, "act": nc.scalar, "dve": nc.vector, "pool": nc.gpsimd, "sp": nc.sync}[e]

    def _mk(self, eng, fn, reads, writes, is_dma):
        op = Op()
        op.eng = eng
        op.fn = fn
        op.is_dma = is_dma
        op.signaled = False
        op.count = None
        op.dsem = None
        op.prev_same_sem = None
        op.bg = False
        deps = []
        xr = [r for r in reads if r.excl]
        if xr:
            reads = [r for r in reads if not r.excl]
            writes = list(writes) + [r for r in xr if r not in writes]
        for r in reads:
            if r.lw is not None:
                deps.append(r.lw)
        for w in writes:
            if w.lw is not None:
                deps.append(w.lw)
            deps.extend(w.rd)
        for r in reads:
            r.rd.append(op)
        for w in writes:
            w.lw = op
            w.rd = []
        seen = set()
        ud = []
        for d in deps:
            if id(d) in seen or d is op:
                continue
            seen.add(id(d))
            ud.append(d)
        op.deps = ud
        op.epoch = self.epoch
        op.idx = len(self.ops[eng])
        self.ops[eng].append(op)
        self.last_op[eng] = op
        return op

    def op(self, eng, fn, reads=(), writes=()):
        return self._mk(eng, fn, reads, writes, False)

    def dma(self, eng, out, in_, reads=(), writes=(), bg=False, **kw):
        def fn(e):
            return e.dma_start(out=out, in_=in_, **kw)
        op = self._mk(eng, fn, reads, writes, True)
        if bg:
            op.bg = True
            s = self.bg_rr
            self.bg_rr = (self.bg_rr + 1) % N_BG_SEMS
            op.dsem = s
            self.bsem_count[s] += 16
            op.dtarget = self.bsem_count[s]
            op.prev_same_sem = self.bsem_last[s]
            self.bsem_last[s] = op
            return op
        s = self.dma_rr
        self.dma_rr = (self.dma_rr + 1) % N_DMA_SEMS
        op.dsem = s
        self.dsem_count[s] += 16
        op.dtarget = self.dsem_count[s]
        op.prev_same_sem = self.dsem_last[s]
        self.dsem_last[s] = op
        self.all_dma.append(op)
        return op

    def barrier(self):
        lasts = [self.last_op[e] for e in ENGS if self.last_op[e] is not None and not self.last_op[e].is_dma]
        dl = [d for d in self.dsem_last if d is not None]
        lastc = []
        for e in ENGS:
            for o in reversed(self.ops[e]):
                if not o.is_dma and o.fn is not None:
                    lastc.append(o)
                    break
        for e in ENGS:
            op = Op()
            op.eng = e
            op.fn = None
            op.is_dma = False
            op.signaled = False
            op.count = None
            op.dsem = None
            op.prev_same_sem = None
            op.bg = False
            op.deps = [d for d in (lastc + dl)]
            op.epoch = self.epoch
            op.idx = len(self.ops[e])
            self.ops[e].append(op)
        self.epoch += 1

    def flush(self):
        nc = self.nc
        self.barrier()
        for e in ENGS:
            for o in self.ops[e][self.emitted[e]:]:
                for d in o.deps:
                    if d.epoch < o.epoch and not d.bg:
                        continue
                    if not d.is_dma:
                        if d.eng == o.eng and (d.eng == "pe" or not SYNC_SAME_ENGINE):
                            continue
                        d.signaled = True
        for e in ENGS:
            for o in self.ops[e][self.emitted[e]:]:
                if o.signaled and o.count is None and not o.is_dma:
                    self.sig_count[e] += 1
                    o.count = self.sig_count[e]
        sched = self

        def emit_engine(e, engobj):
            waited = sched.waited[e]
            for o in sched.ops[e][sched.emitted[e]:]:
                deps = list(o.deps)
                if o.is_dma and o.prev_same_sem is not None:
                    deps.append(o.prev_same_sem)
                for d in deps:
                    if d.epoch < o.epoch and not d.bg:
                        continue
                    if d.is_dma and d.bg:
                        key = ("b", d.dsem)
                        val = d.dtarget
                        sem = sched.bsems[d.dsem]
                    elif d.is_dma:
                        key = ("d", d.dsem)
                        val = d.dtarget
                        sem = sched.dsems[d.dsem]
                    else:
                        if d.eng == e and (e == "pe" or not SYNC_SAME_ENGINE):
                            continue
                        assert d.count is not None, "dep on unsignaled op"
                        key = ("e", d.eng)
                        val = d.count
                        sem = sched.esem[d.eng]
                    if waited.get(key, 0) >= val:
                        continue
                    waited[key] = val
                    engobj.wait_ge(sem, val)
                if o.fn is None:
                    continue
                ins = o.fn(engobj)
                if o.is_dma and o.bg:
                    ins.then_inc(sched.bsems[o.dsem], 16)
                elif o.is_dma:
                    ins.then_inc(sched.dsems[o.dsem], 16)
                elif o.signaled:
                    ins.then_inc(sched.esem[e], 1)
            sched.emitted[e] = len(sched.ops[e])

        with nc.Block() as block:
            @block.tensor
            def _(eng):
                emit_engine("pe", eng)

            @block.scalar
            def _(eng):
                emit_engine("act", eng)

            @block.vector
            def _(eng):
                emit_engine("dve", eng)

            @block.gpsimd
            def _(eng):
                emit_engine("pool", eng)

            @block.sync
            def _(eng):
                emit_engine("sp", eng)

import math
from contextlib import ExitStack
from concourse.bass_utils import run_bass_kernel_spmd

S_LEN = 4096
D = 2048
DFF = 5632
NT = S_LEN // 128
EPS = 1e-6
NEG = -30000.0
SCALE = 128 ** -0.5
DIL = (1, 4, 16)


class KB:
    def __init__(self, debug=(), stop_after=None):
        self.nc = bass.Bass("TRN2", target_bir_lowering=False)
        self.debug = set(debug)
        self.stop_after = stop_after
        self.es = ExitStack()
        self.S = Sched(self.nc, self.es)
        self.rid = 0
        self.wres = {}

    def dram(self, name, shape, dtype, kind=None):
        if kind is None:
            kind = "ExternalOutput" if name in self.debug else "Internal"
        return self.nc.dram_tensor(name, list(shape), dtype, kind=kind).ap()

    def inp(self, name, shape, dtype=F32):
        return self.nc.dram_tensor(name, list(shape), dtype, kind="ExternalInput").ap()

    def res(self, name=""):
        self.rid += 1
        return Res(name + str(self.rid))

    def sb(self, st, name, shape, dtype, n=1):
        out = []
        for i in range(n):
            t = st.enter_context(self.nc.sbuf_tensor("%s_%d_%d" % (name, i, self.rid), list(shape), dtype))
            self.rid += 1
            out.append((t, self.res(name)))
        return out

    def ps(self, st, name, shape, dtype, n=1):
        out = []
        for i in range(n):
            t = st.enter_context(self.nc.psum_tensor("%s_%d_%d" % (name, i, self.rid), list(shape), dtype))
            self.rid += 1
            r = self.res(name)
            r.excl = True
            out.append((t, r))
        return out

    def phase_end(self, name):
        self.S.flush()
        return self.stop_after == name

    def precast(self, w, name, K, N, sc, kstep=16):
        KC = K // 128
        ns = N // sc
        wb = self.dram(name, [ns, 128, KC, sc], BF16)
        rs = []
        for s in range(ns):
            parts = []
            for k0 in range(0, KC, kstep):
                k1 = min(KC, k0 + kstep)
                r = self.res(name)
                parts.append(r)
                self.S.dma("pool", wb[s, :, k0:k1, :],
                           w[k0 * 128:k1 * 128, s * sc:(s + 1) * sc].rearrange("(k p) n -> p k n", p=128),
                           writes=[r], bg=True)
            rs.append(parts)
        self.wres[name] = rs
        return wb

    def setup_consts(self):
        st = self.es
        S = self.S
        (identf, r_if), = self.sb(st, "identf", [128, 128], F32)
        (ident, r_id), = self.sb(st, "ident", [128, 128], BF16)

        def mk_ident(e):
            e.memset(identf[:], 0.0)
            return e.affine_select(identf[:], identf[:], pattern=[[-1, 128]], compare_op=ALU.not_equal,
                                   fill=1.0, base=0, channel_multiplier=1)
        S.op("pool", mk_ident, writes=[r_if])
        S.op("dve", lambda e: e.tensor_copy(ident[:], identf[:]), reads=[r_if], writes=[r_id])
        self.ident = ident
        self.r_ident = r_id

    def load_T(self, st, src, t0, ntok, xT, r_xT, col0, gain_b=None, r_gain=None, dt_in=F32, bufs=None):
        S = self.S
        ht_ring, yb_ring, junk, ssr, pT_ring = bufs
        ntile = (ntok + 127) // 128
        for ti in range(ntile):
            r0 = t0 + ti * 128
            n = min(128, ntok - ti * 128)
            ht, r_ht = ht_ring[self.cnt_ht % len(ht_ring)]
            self.cnt_ht += 1
            lo = max(r0, 0)
            hi = min(r0 + n, S_LEN)
            if lo > r0 or hi < r0 + n:
                S.op("pool", lambda e, ht=ht, n=n: e.memset(ht[0:n, :], 0.0), writes=[r_ht])
            if hi > lo:
                S.dma("sp", ht[lo - r0:hi - r0, :], src[lo:hi, :], writes=[r_ht])
            if gain_b is not None:
                yb, r_yb = yb_ring[self.cnt_yb % len(yb_ring)]
                self.cnt_yb += 1
                (jk, r_jk) = junk
                (ss, r_ss) = ssr[self.cnt_yb % len(ssr)]
                S.op("act", lambda e, ht=ht, jk=jk, ss=ss, n=n: e.activation(jk[0:n, :], ht[0:n, :], AF.Square, scale=D ** -0.5, accum_out=ss[0:n, :]),
                     reads=[r_ht], writes=[r_jk, r_ss])
                S.op("dve", lambda e, ss=ss, n=n: e.tensor_scalar(ss[0:n, :], ss[0:n, :], EPS, None, ALU.add), reads=[r_ss], writes=[r_ss])
                S.op("act", lambda e, ss=ss, n=n: e.activation(ss[0:n, :], ss[0:n, :], AF.Sqrt), reads=[r_ss], writes=[r_ss])
                S.op("dve", lambda e, ss=ss, n=n: e.reciprocal(ss[0:n, :], ss[0:n, :]), reads=[r_ss], writes=[r_ss])
                S.op("dve", lambda e, yb=yb, ht=ht, ss=ss, n=n: e.scalar_tensor_tensor(yb[0:n, :], ht[0:n, :], ss[0:n, 0:1], gain_b[0:n, :], ALU.mult, ALU.mult),
                     reads=[r_ht, r_ss, r_gain], writes=[r_yb])
                srcT, r_srcT = yb, r_yb
            else:
                srcT, r_srcT = ht, r_ht
            for g in range(2):
                pT, r_pT = pT_ring[self.cnt_pT % len(pT_ring)]
                self.cnt_pT += 1

                def tr(e, pT=pT, srcT=srcT, g=g, n=n):
                    for j in range(8):
                        k = g * 8 + j
                        i = e.transpose(pT[:, j, 0:n], srcT[0:n, k * 128:(k + 1) * 128], self.ident[0:n, 0:n])
                    return i
                S.op("pe", tr, reads=[r_srcT, self.r_ident], writes=[r_pT])
                c0 = col0 + ti * 128
                eng = "act" if (self.cnt_pT % 2) else "dve"
                if eng == "act":
                    S.op("act", lambda e, pT=pT, g=g, c0=c0, n=n: e.activation(xT[:, g * 8:(g + 1) * 8, c0:c0 + n], pT[:, :, 0:n], AF.Copy),
                         reads=[r_pT], writes=[r_xT])
                else:
                    S.op("dve", lambda e, pT=pT, g=g, c0=c0, n=n: e.tensor_copy(xT[:, g * 8:(g + 1) * 8, c0:c0 + n], pT[:, :, 0:n]),
                         reads=[r_pT], writes=[r_xT])

    def load_T_bufs(self, st, norm, dt_in, npT=2):
        ht_ring = self.sb(st, "ht", [128, D], dt_in, 2)
        if norm:
            yb_ring = self.sb(st, "yb", [128, D], BF16, 2)
            junk = self.sb(st, "junk", [128, D], BF16, 1)[0]
            ssr = self.sb(st, "ss", [128, 1], F32, 4)
        else:
            yb_ring = junk = ssr = None
        pT_ring = self.ps(st, "pT", [128, 8, 128], BF16, npT)
        self.cnt_ht = self.cnt_yb = self.cnt_pT = 0
        return (ht_ring, yb_ring, junk, ssr, pT_ring)

    def load_gain(self, st, g_ap, n=D):
        (gb, r_gb), = self.sb(st, "gb", [128, n], F32)
        self.S.dma("sp", gb[:], g_ap.partition_broadcast(128), writes=[r_gb])
        return gb, r_gb

    def proj_phase(self, h_src, gain_ap, wb, slabs, rope=None, wname=None):
        S = self.S
        nc = self.nc
        wres = self.wres[wname]
        with ExitStack() as st:
            bufs = self.load_T_bufs(st, True, F32)
            gb, r_gb = self.load_gain(st, gain_ap)
            (yT, r_yT), = self.sb(st, "yT", [128, 16, 2048], BF16)
            wt_ring = self.sb(st, "wt", [128, 16, 512], BF16, 2)
            po_ring = self.ps(st, "po", [128, 512], F32, 4)
            qst_ring = self.sb(st, "qst", [128, 2048], BF16, 3)
            vst_ring = {}
            cnt = {"wt": 0, "po": 0, "qst": 0, "ev": 0}
            if rope is not None:
                cos_d, sin_d, gq_ap, gk_ap = rope
                (gqb, r_gqb), = self.sb(st, "gqb", [128, 128], F32)
                (gkb, r_gkb), = self.sb(st, "gkb", [128, 128], F32)
                S.dma("sp", gqb[:], gq_ap.partition_broadcast(128), writes=[r_gqb])
                S.dma("sp", gkb[:], gk_ap.partition_broadcast(128), writes=[r_gkb])
                cs_ring = self.sb(st, "cs", [128, 2, 64], F32, 3)
                sq_ring = self.sb(st, "sq", [128, 512], F32, 2)
                ss4_ring = self.sb(st, "ss4", [128, 4], F32, 3)
                xn_ring = self.sb(st, "xn", [128, 4, 128], F32, 2)
                tt_ring = self.sb(st, "tt", [128, 4, 4, 64], F32, 2)
                xr_ring = self.sb(st, "xr", [128, 4, 128], BF16, 2)
                hst = self.sb(st, "hst", [128, 2048], BF16, 8)
                pTc_ring = self.ps(st, "pTc", [128, 4, 128], BF16, 2)
                cnt.update({"cs": 0, "sq": 0, "xn": 0, "hst": 0, "pTc": 0})

            def get_vst(nh, dv):
                key = (nh, dv)
                if key not in vst_ring:
                    ring = self.sb(st, "vst", [128, nh, dv + 1], BF16, 3)
                    for (t, r) in ring:
                        S.op("dve", lambda e, t=t: e.memset(t[:], 1.0), writes=[r])
                    vst_ring[key] = [ring, 0]
                ent = vst_ring[key]
                t, r = ent[0][ent[1] % 3]
                ent[1] += 1
                return t, r

            def evac(out_ap, in_ap, reads, writes):
                cnt["ev"] += 1
                if cnt["ev"] % 2:
                    S.op("act", lambda e: e.activation(out_ap, in_ap, AF.Copy), reads=reads, writes=writes)
                else:
                    S.op("dve", lambda e: e.tensor_copy(out_ap, in_ap), reads=reads, writes=writes)

            for c in range(2):
                tok0 = c * 2048
                self.load_T(st, h_src, tok0, 2048, yT, r_yT, 0, gain_b=gb, r_gain=r_gb, bufs=bufs)
                for si, spec in enumerate(slabs):
                    wt, r_wt = wt_ring[cnt["wt"] % 2]
                    cnt["wt"] += 1
                    S.dma("sp", wt[:], wb[si], reads=wres[si], writes=[r_wt])
                    kind = spec[0]
                    if kind == "fm":
                        for cc in range(4):
                            qid = spec[1][cc]
                            qst, r_qst = qst_ring[cnt["qst"] % 3]
                            cnt["qst"] += 1
                            for tg in range(4):
                                po, r_po = po_ring[cnt["po"] % 4]
                                cnt["po"] += 1

                                def mm(e, po=po, wt=wt, cc=cc, tg=tg):
                                    for k in range(16):
                                        i = e.matmul(po[:], wt[:, k, cc * 128:(cc + 1) * 128], yT[:, k, tg * 512:(tg + 1) * 512],
                                                     start=(k == 0), stop=(k == 15))
                                    return i
                                S.op("pe", mm, reads=[r_wt, r_yT], writes=[r_po])
                                evac(qst[:, tg * 512:(tg + 1) * 512], po[:], [r_po], [r_qst])
                            S.dma("sp", self.qt[qid, :, tok0:tok0 + 2048], qst[:], reads=[r_qst])
                    elif kind == "tmv":
                        _, v1, h0, nh, dv = spec
                        for t in range(16):
                            po, r_po = po_ring[cnt["po"] % 4]
                            cnt["po"] += 1

                            def mm(e, po=po, wt=wt, t=t):
                                for k in range(16):
                                    i = e.matmul(po[:], yT[:, k, t * 128:(t + 1) * 128], wt[:, k, :], start=(k == 0), stop=(k == 15))
                                return i
                            S.op("pe", mm, reads=[r_wt, r_yT], writes=[r_po])
                            vst, r_vst = get_vst(nh, dv)
                            evac(vst[:, :, 0:dv], po[:].rearrange("p (h d) -> p h d", h=nh), [r_po], [r_vst])
                            S.dma("sp", v1[tok0 + t * 128:tok0 + (t + 1) * 128, h0:h0 + nh, :], vst[:], reads=[r_vst])
                    elif kind == "tmc":
                        heads = spec[1]
                        hsts = []
                        for hh in range(4):
                            if heads[hh][0] in ("q", "k"):
                                hsts.append(hst[cnt["hst"] % 8])
                                cnt["hst"] += 1
                            else:
                                hsts.append(None)
                        nqk = sum(1 for x in heads if x[0] in ("q", "k"))
                        vheads = [x for x in heads if x[0] == "v"]
                        for t in range(16):
                            po, r_po = po_ring[cnt["po"] % 4]
                            cnt["po"] += 1

                            def mm(e, po=po, wt=wt, t=t):
                                for k in range(16):
                                    i = e.matmul(po[:], yT[:, k, t * 128:(t + 1) * 128], wt[:, k, :], start=(k == 0), stop=(k == 15))
                                return i
                            S.op("pe", mm, reads=[r_wt, r_yT], writes=[r_po])
                            if vheads:
                                nv = len(vheads)
                                vst, r_vst = get_vst(nv, 128)
                                evac(vst[:, :, 0:128], po[:, nqk * 128:512].rearrange("p (h d) -> p h d", h=nv), [r_po], [r_vst])
                                v1 = vheads[0][1]
                                S.dma("sp", v1[tok0 + t * 128:tok0 + (t + 1) * 128, vheads[0][2]:vheads[0][2] + nv, :], vst[:], reads=[r_vst])
                            W = nqk * 128
                            sq, r_sq = sq_ring[cnt["sq"] % 2]
                            ss4, r_ss4 = ss4_ring[cnt["sq"] % 3]
                            cnt["sq"] += 1
                            S.op("act", lambda e, sq=sq, po=po, W=W: e.activation(sq[:, 0:W], po[:, 0:W], AF.Square, scale=128 ** -0.5),
                                 reads=[r_po], writes=[r_sq])
                            S.op("dve", lambda e, sq=sq, ss4=ss4, nqk=nqk, W=W: e.reduce_sum(ss4[:, 0:nqk], sq[:, 0:W].rearrange("p (h d) -> p h d", h=nqk), AX.X),
                                 reads=[r_sq], writes=[r_ss4])
                            S.op("dve", lambda e, ss4=ss4, nqk=nqk: e.tensor_scalar(ss4[:, 0:nqk], ss4[:, 0:nqk], EPS, None, ALU.add), reads=[r_ss4], writes=[r_ss4])
                            S.op("act", lambda e, ss4=ss4, nqk=nqk: e.activation(ss4[:, 0:nqk], ss4[:, 0:nqk], AF.Sqrt), reads=[r_ss4], writes=[r_ss4])
                            S.op("dve", lambda e, ss4=ss4, nqk=nqk: e.reciprocal(ss4[:, 0:nqk], ss4[:, 0:nqk]), reads=[r_ss4], writes=[r_ss4])
                            xn, r_xn = xn_ring[cnt["xn"] % 2]
                            tt, r_tt = tt_ring[cnt["xn"] % 2]
                            xr, r_xr = xr_ring[cnt["xn"] % 2]
                            cnt["xn"] += 1
                            for hh in range(nqk):
                                gbx, r_gbx = (gqb, r_gqb) if heads[hh][0] == "q" else (gkb, r_gkb)
                                S.op("dve", lambda e, xn=xn, po=po, ss4=ss4, hh=hh, gbx=gbx: e.scalar_tensor_tensor(
                                    xn[:, hh, :], po[:, hh * 128:(hh + 1) * 128], ss4[:, hh:hh + 1], gbx[:], ALU.mult, ALU.mult),
                                    reads=[r_po, r_ss4, r_gbx], writes=[r_xn])
                            cs, r_cs = cs_ring[cnt["cs"] % 3]
                            cnt["cs"] += 1
                            S.dma("sp", cs[:, 0, :], cos_d[tok0 + t * 128:tok0 + (t + 1) * 128, :], writes=[r_cs])
                            S.dma("sp", cs[:, 1, :], sin_d[tok0 + t * 128:tok0 + (t + 1) * 128, :], writes=[r_cs])
                            x0 = xn[:, 0:nqk, 0:128:2]
                            x1 = xn[:, 0:nqk, 1:128:2]
                            cb = cs[:, 0:1, :].broadcast_to([128, nqk, 64])
                            sb_ = cs[:, 1:2, :].broadcast_to([128, nqk, 64])
                            S.op("dve", lambda e, tt=tt, x0=x0, cb=cb, nqk=nqk: e.tensor_tensor(tt[:, 0, 0:nqk, :], x0, cb, ALU.mult), reads=[r_xn, r_cs], writes=[r_tt])
                            S.op("dve", lambda e, tt=tt, x1=x1, sb_=sb_, nqk=nqk: e.tensor_tensor(tt[:, 1, 0:nqk, :], x1, sb_, ALU.mult), reads=[r_xn, r_cs], writes=[r_tt])
                            S.op("dve", lambda e, tt=tt, x0=x0, sb_=sb_, nqk=nqk: e.tensor_tensor(tt[:, 2, 0:nqk, :], x0, sb_, ALU.mult), reads=[r_xn, r_cs], writes=[r_tt])
                            S.op("dve", lambda e, tt=tt, x1=x1, cb=cb, nqk=nqk: e.tensor_tensor(tt[:, 3, 0:nqk, :], x1, cb, ALU.mult), reads=[r_xn, r_cs], writes=[r_tt])
                            S.op("dve", lambda e, tt=tt, xr=xr, nqk=nqk: e.tensor_tensor(xr[:, 0:nqk, 0:128:2], tt[:, 0, 0:nqk, :], tt[:, 1, 0:nqk, :], ALU.subtract), reads=[r_tt], writes=[r_xr])
                            S.op("dve", lambda e, tt=tt, xr=xr, nqk=nqk: e.tensor_tensor(xr[:, 0:nqk, 1:128:2], tt[:, 2, 0:nqk, :], tt[:, 3, 0:nqk, :], ALU.add), reads=[r_tt], writes=[r_xr])
                            pTc, r_pTc = pTc_ring[cnt["pTc"] % 2]
                            cnt["pTc"] += 1

                            def tr(e, pTc=pTc, xr=xr, nqk=nqk):
                                for hh in range(nqk):
                                    i = e.transpose(pTc[:, hh, :], xr[:, hh, :], self.ident[:])
                                return i
                            S.op("pe", tr, reads=[r_xr, self.r_ident], writes=[r_pTc])
                            for hh in range(nqk):
                                evac(hsts[hh][0][:, t * 128:(t + 1) * 128], pTc[:, hh, :], [r_pTc], [hsts[hh][1]])
                        for hh in range(nqk):
                            S.dma("sp", self.qt[heads[hh][1], :, tok0:tok0 + 2048], hsts[hh][0][:], reads=[hsts[hh][1]])
            return self.phase_end("proj")

    def attn_full(self, units, dv, bias=None, finish=None, extra_setup=None):
        S = self.S
        with ExitStack() as st:
            vt_ring = self.sb(st, "vt", [128, NT, dv + 1], BF16, 2)
            kt_ring = self.sb(st, "kt", [128, S_LEN], BF16, 2)
            q_ring = self.sb(st, "qsb", [128, 512], BF16, 2)
            pt_ring = self.sb(st, "pt", [128, 512], BF16, 3)
            pss_ring = self.ps(st, "pss", [128, 512], F32, 3)
            acc_ring = self.ps(st, "acc", [128, dv + 1], F32, 4)
            if bias is not None:
                band_d, constb_d = bias
                band_ring = self.sb(st, "band", [128, 2176], F32, 2)
                tmp_ring = self.sb(st, "tmpb", [128, 512], F32, 2)
                nhb = band_d.shape[0]
                (cb, r_cb), = self.sb(st, "constb", [128, nhb * 2], F32)
                S.dma("sp", cb[:], constb_d, writes=[r_cb])
            ctx = extra_setup(st) if extra_setup is not None else None
            cnt = {"vt": 0, "kt": 0, "q": 0, "pt": 0, "pss": 0, "band": 0, "tmp": 0}
            cur_v = None
            cur_band = None
            for u in units:
                vkey = (id(u["v"][0]), u["v"][1])
                if vkey != cur_v:
                    vt, r_vt = vt_ring[cnt["vt"] % 2]
                    cnt["vt"] += 1
                    v1, hidx = u["v"]
                    for part in range(4):
                        S.dma("pool", vt[:, part * 8:(part + 1) * 8, :],
                              v1[part * 1024:(part + 1) * 1024, hidx, :].rearrange("(t p) c -> p t c", p=128), writes=[r_vt])
                    cur_v = vkey
                kt, r_kt = kt_ring[cnt["kt"] % 2]
                cnt["kt"] += 1
                S.dma("sp", kt[:], self.qt[u["kid"]], writes=[r_kt])
                bh = u.get("bias_h")
                if bias is not None and bh != cur_band:
                    band, r_band = band_ring[cnt["band"] % 2]
                    cnt["band"] += 1
                    S.dma("sp", band[:], band_d[bh], writes=[r_band])
                    cur_band = bh
                for qi in range(8):
                    qsb, r_q = q_ring[cnt["q"] % 2]
                    cnt["q"] += 1
                    S.dma("sp", qsb[:], self.qt[u["qid"], :, qi * 512:(qi + 1) * 512], writes=[r_q])
                    for ki in range(NT):
                        pss, r_pss = pss_ring[cnt["pss"] % 3]
                        cnt["pss"] += 1
                        S.op("pe", lambda e, pss=pss, kt=kt, qsb=qsb, ki=ki: e.matmul(pss[:], kt[:, ki * 128:(ki + 1) * 128], qsb[:], start=True, stop=True),
                             reads=[r_kt, r_q], writes=[r_pss])
                        pt, r_pt = pt_ring[cnt["pt"] % 3]
                        cnt["pt"] += 1
                        if bias is None:
                            S.op("act", lambda e, pt=pt, pss=pss: e.activation(pt[:], pss[:], AF.Exp, scale=SCALE), reads=[r_pss], writes=[r_pt])
                        else:
                            delta = ki * 128 - qi * 512
                            if delta >= 1070:
                                S.op("act", lambda e, pt=pt, pss=pss, bh=bh: e.activation(pt[:], pss[:], AF.Exp, bias=cb[:, 2 * bh + 1:2 * bh + 2], scale=SCALE),
                                     reads=[r_pss, r_cb], writes=[r_pt])
                            elif delta <= -686:
                                S.op("act", lambda e, pt=pt, pss=pss, bh=bh: e.activation(pt[:], pss[:], AF.Exp, bias=cb[:, 2 * bh:2 * bh + 1], scale=SCALE),
                                     reads=[r_pss, r_cb], writes=[r_pt])
                            else:
                                s0 = 1024 - delta
                                tmp, r_tmp = tmp_ring[cnt["tmp"] % 2]
                                cnt["tmp"] += 1
                                S.op("dve", lambda e, tmp=tmp, pss=pss, band=band, s0=s0: e.scalar_tensor_tensor(tmp[:], pss[:], SCALE, band[:, s0:s0 + 512], ALU.mult, ALU.add),
                                     reads=[r_pss, r_band], writes=[r_tmp])
                                S.op("act", lambda e, pt=pt, tmp=tmp: e.activation(pt[:], tmp[:], AF.Exp), reads=[r_tmp], writes=[r_pt])
                        for qs in range(4):
                            acc, r_acc = acc_ring[qs]
                            S.op("pe", lambda e, acc=acc, pt=pt, vt=vt, qs=qs, ki=ki: e.matmul(acc[:], pt[:, qs * 128:(qs + 1) * 128], vt[:, ki, :], start=(ki == 0), stop=(ki == NT - 1)),
                                 reads=[r_pt, r_vt], writes=[r_acc])
                    finish(st, ctx, u, qi, acc_ring)
            return self.phase_end("attn_full")

    def attn_B(self, v1b, band_d, constb_d, lq1, lk1, lq2, lk2, subln, lambda_init):
        S = self.S
        units = []
        for h in range(4):
            for m in range(2):
                units.append(dict(qid=48 + h * 2 + m, kid=56 + h * 2 + m, v=(v1b, h), bias_h=h, h=h, m=m))

        def setup(st):
            c = {}
            (lv, r_lv), = self.sb(st, "lv", [128, 4, 128], F32)
            for i, a in enumerate((lq1, lk1, lq2, lk2)):
                S.dma("sp", lv[:, i, :], a.partition_broadcast(128), writes=[r_lv])
            (pr, r_pr), = self.sb(st, "lpr", [128, 2, 128], F32)
            (sv, r_sv), = self.sb(st, "lsv", [128, 2], F32)
            (nlam, r_nlam), = self.sb(st, "nlam", [128, 1], F32)
            S.op("dve", lambda e: e.tensor_tensor(pr[:, 0, :], lv[:, 0, :], lv[:, 1, :], ALU.mult), reads=[r_lv], writes=[r_pr])
            S.op("dve", lambda e: e.tensor_tensor(pr[:, 1, :], lv[:, 2, :], lv[:, 3, :], ALU.mult), reads=[r_lv], writes=[r_pr])
            S.op("dve", lambda e: e.reduce_sum(sv[:], pr[:], AX.X), reads=[r_pr], writes=[r_sv])
            S.op("act", lambda e: e.activation(sv[:], sv[:], AF.Exp), reads=[r_sv], writes=[r_sv])
            S.op("dve", lambda e: e.tensor_tensor(nlam[:], sv[:, 1:2], sv[:, 0:1], ALU.subtract), reads=[r_sv], writes=[r_nlam])
            S.op("dve", lambda e: e.tensor_scalar(nlam[:], nlam[:], -lambda_init, None, ALU.add), reads=[r_nlam], writes=[r_nlam])
            (sg, r_sg), = self.sb(st, "subg", [128, 256], F32)
            S.dma("sp", sg[:], subln.partition_broadcast(128), writes=[r_sg])
            S.op("dve", lambda e: e.tensor_scalar(sg[:], sg[:], 1.0 - lambda_init, None, ALU.mult), reads=[r_sg], writes=[r_sg])
            (o0, r_o0), = self.sb(st, "o0", [128, 32, 256], F32)
            c["nlam"] = (nlam, r_nlam)
            c["sg"] = (sg, r_sg)
            c["o0"] = (o0, r_o0)
            c["rec"] = self.sb(st, "rec", [128, 1], F32, 4)
            c["o1"] = self.sb(st, "o1", [128, 256], F32, 2)
            c["dd"] = self.sb(st, "dd", [128, 256], F32, 2)
            c["jk"] = self.sb(st, "jkb", [128, 256], BF16, 1)[0]
            c["ss"] = self.sb(st, "ssb", [128, 1], F32, 4)
            c["ost"] = self.sb(st, "ostb", [128, 4, 256], BF16, 2)
            c["n"] = 0
            return c

        def finish(st, c, u, qi, acc_ring):
            h, m = u["h"], u["m"]
            o0, r_o0 = c["o0"]
            if m == 1:
                ost, r_ost = c["ost"][qi % 2]
            for qs in range(4):
                acc, r_acc = acc_ring[qs]
                c["n"] += 1
                rec, r_rec = c["rec"][c["n"] % 4]
                S.op("dve", lambda e, rec=rec, acc=acc: e.reciprocal(rec[:], acc[:, 256:257]), reads=[r_acc], writes=[r_rec])
                if m == 0:
                    S.op("dve", lambda e, acc=acc, rec=rec, qi=qi, qs=qs: e.tensor_scalar(o0[:, qi * 4 + qs, :], acc[:, 0:256], rec[:, 0:1], None, ALU.mult),
                         reads=[r_acc, r_rec], writes=[r_o0])
                else:
                    o1, r_o1 = c["o1"][c["n"] % 2]
                    dd, r_dd = c["dd"][c["n"] % 2]
                    ss, r_ss = c["ss"][c["n"] % 4]
                    jk, r_jk = c["jk"]
                    nlam, r_nlam = c["nlam"]
                    sg, r_sg = c["sg"]
                    S.op("dve", lambda e, o1=o1, acc=acc, rec=rec: e.tensor_scalar(o1[:], acc[:, 0:256], rec[:, 0:1], None, ALU.mult), reads=[r_acc, r_rec], writes=[r_o1])
                    S.op("dve", lambda e, dd=dd, o1=o1, qi=qi, qs=qs: e.scalar_tensor_tensor(dd[:], o1[:], nlam[:, 0:1], o0[:, qi * 4 + qs, :], ALU.mult, ALU.add),
                         reads=[r_o1, r_nlam, r_o0], writes=[r_dd])
                    S.op("act", lambda e, jk=jk, dd=dd, ss=ss: e.activation(jk[:], dd[:], AF.Square, scale=1.0 / 16.0, accum_out=ss[:]), reads=[r_dd], writes=[r_jk, r_ss])
                    S.op("dve", lambda e, ss=ss: e.tensor_scalar(ss[:], ss[:], EPS, None, ALU.add), reads=[r_ss], writes=[r_ss])
                    S.op("act", lambda e, ss=ss: e.activation(ss[:], ss[:], AF.Sqrt), reads=[r_ss], writes=[r_ss])
                    S.op("dve", lambda e, ss=ss: e.reciprocal(ss[:], ss[:]), reads=[r_ss], writes=[r_ss])
                    S.op("dve", lambda e, ost=ost, dd=dd, ss=ss, qs=qs: e.scalar_tensor_tensor(ost[:, qs, :], dd[:], ss[:, 0:1], sg[:], ALU.mult, ALU.mult),
                         reads=[r_dd, r_ss, r_sg], writes=[r_ost])
            if m == 1:
                S.dma("sp", self.mixed[qi * 512:(qi + 1) * 512, 1024 + h * 256:1024 + (h + 1) * 256].rearrange("(s p) c -> p s c", p=128),
                      ost[:], reads=[r_ost])

        return self.attn_full(units, 256, bias=(band_d, constb_d), finish=finish, extra_setup=setup)

    def attn_C(self, v1c):
        S = self.S
        units = []
        for n in range(2):
            for g in range(4):
                h = n * 4 + g
                units.append(dict(qid=h, kid=8 + n, v=(v1c, n), bias_h=None, h=h))

        def setup(st):
            c = {}
            c["rec"] = self.sb(st, "rec", [128, 1], F32, 4)
            c["ost"] = self.sb(st, "ostc", [128, 4, 128], BF16, 2)
            c["n"] = 0
            return c

        def finish(st, c, u, qi, acc_ring):
            h = u["h"]
            ost, r_ost = c["ost"][qi % 2]
            for qs in range(4):
                acc, r_acc = acc_ring[qs]
                c["n"] += 1
                rec, r_rec = c["rec"][c["n"] % 4]
                S.op("dve", lambda e, rec=rec, acc=acc: e.reciprocal(rec[:], acc[:, 128:129]), reads=[r_acc], writes=[r_rec])
                S.op("dve", lambda e, acc=acc, rec=rec, ost=ost, qs=qs: e.tensor_scalar(ost[:, qs, :], acc[:, 0:128], rec[:, 0:1], None, ALU.mult),
                     reads=[r_acc, r_rec], writes=[r_ost])
            S.dma("sp", self.mixed[qi * 512:(qi + 1) * 512, h * 128:(h + 1) * 128].rearrange("(s p) c -> p s c", p=128),
                  ost[:], reads=[r_ost])

        return self.attn_full(units, 128, bias=None, finish=finish, extra_setup=setup)

    def attn_A(self, v1a, mba_d, acca):
        S = self.S
        with ExitStack() as st:
            qn_ring = self.sb(st, "qn", [128, S_LEN], BF16, 2)
            kn_ring = self.sb(st, "kn", [128, S_LEN], BF16, 2)
            qp_ring = self.sb(st, "qp", [128, S_LEN], BF16, 2)
            kp_ring = self.sb(st, "kp", [128, 6144], BF16, 2)
            mb_ring = self.sb(st, "mb", [128, 2, 128], F32, 2)
            vt_ring = self.sb(st, "vta", [128, 33, 129], BF16, 3)
            tmp_ring = self.sb(st, "tmpa", [128, 2, 128], F32, 3)
            pt_ring = self.sb(st, "pta", [128, 2, 128], BF16, 3)
            ost_ring = self.sb(st, "osta", [128, 32, 129], F32, 2)
            pss_ring = self.ps(st, "pssa", [128, 2, 128], F32, 3)
            acc_ring = self.ps(st, "acca", [128, 129], F32, 3)
            n_u = 0
            n_v = 0
            n_b = 0
            for g in range(3):
                r = DIL[g]
                L = S_LEN // r
                nb = L // 128
                for h in range(8):
                    qn, r_qn = qn_ring[n_u % 2]
                    kn, r_kn = kn_ring[n_u % 2]
                    qp, r_qp = qp_ring[n_u % 2]
                    kp, r_kp = kp_ring[n_u % 2]
                    mb, r_mb = mb_ring[n_u % 2]
                    n_u += 1
                    S.dma("sp", qn[:], self.qt[g * 16 + h], writes=[r_qn])
                    S.dma("sp", kn[:], self.qt[g * 16 + 8 + h], writes=[r_kn])
                    S.dma("sp", mb[:], mba_d[g * 8 + h], writes=[r_mb])
                    qpv = qp[:].rearrange("d (r n) -> d r n", r=r)
                    kpv = kp[:, 0:r * (L + 128)].rearrange("d (r n) -> d r n", r=r)
                    S.op("pool", lambda e, qpv=qpv, qn=qn, r=r: e.tensor_copy(qpv, qn[:].rearrange("d (n r) -> d r n", r=r)), reads=[r_qn], writes=[r_qp])

                    def kperm(e, kpv=kpv, kn=kn, r=r, L=L):
                        e.memset(kpv[:, :, 0:64], 0.0)
                        e.memset(kpv[:, :, 64 + L:128 + L], 0.0)
                        return e.tensor_copy(kpv[:, :, 64:64 + L], kn[:].rearrange("d (n r) -> d r n", r=r))
                    S.op("pool", kperm, reads=[r_kn], writes=[r_kp])
                    vsrc = v1a[g, :, h, :].rearrange("(n r) c -> r n c", r=r)
                    for rho in range(r):
                        vt, r_vt = vt_ring[n_v % 3]
                        ost, r_ost = ost_ring[n_v % 2]
                        n_v += 1

                        def vz(e, vt=vt, nb=nb):
                            e.memset(vt[0:64, 0, :], 0.0)
                            return e.memset(vt[64:128, nb, :], 0.0)
                        S.op("pool", vz, writes=[r_vt])
                        S.dma("pool", vt[64:128, 0, :], vsrc[rho, 0:64, :], writes=[r_vt])
                        S.dma("pool", vt[:, 1:nb, :], vsrc[rho, 64:L - 64, :].rearrange("(i p) c -> p i c", p=128), writes=[r_vt])
                        S.dma("pool", vt[0:64, nb, :], vsrc[rho, L - 64:L, :], writes=[r_vt])
                        for b in range(nb):
                            pss, r_pss = pss_ring[n_b % 3]
                            tmp, r_tmp = tmp_ring[n_b % 3]
                            pt, r_pt = pt_ring[n_b % 3]
                            acc, r_acc = acc_ring[n_b % 3]
                            n_b += 1

                            def qk(e, pss=pss, kpv=kpv, qpv=qpv, rho=rho, b=b):
                                e.matmul(pss[:, 0, :], kpv[:, rho, b * 128:(b + 1) * 128], qpv[:, rho, b * 128:(b + 1) * 128], start=True, stop=True)
                                return e.matmul(pss[:, 1, :], kpv[:, rho, (b + 1) * 128:(b + 2) * 128], qpv[:, rho, b * 128:(b + 1) * 128], start=True, stop=True)
                            S.op("pe", qk, reads=[r_kp, r_qp], writes=[r_pss])
                            S.op("dve", lambda e, tmp=tmp, pss=pss, mb=mb: e.scalar_tensor_tensor(tmp[:], pss[:], SCALE, mb[:], ALU.mult, ALU.add),
                                 reads=[r_pss, r_mb], writes=[r_tmp])
                            S.op("act", lambda e, pt=pt, tmp=tmp: e.activation(pt[:], tmp[:], AF.Exp), reads=[r_tmp], writes=[r_pt])

                            def pv(e, acc=acc, pt=pt, vt=vt, b=b):
                                e.matmul(acc[:], pt[:, 0, :], vt[:, b, :], start=True, stop=False)
                                return e.matmul(acc[:], pt[:, 1, :], vt[:, b + 1, :], start=False, stop=True)
                            S.op("pe", pv, reads=[r_pt, r_vt], writes=[r_acc])
                            if n_b % 2:
                                S.op("act", lambda e, ost=ost, acc=acc, b=b: e.activation(ost[:, b, :], acc[:], AF.Copy), reads=[r_acc], writes=[r_ost])
                            else:
                                S.op("dve", lambda e, ost=ost, acc=acc, b=b: e.tensor_copy(ost[:, b, :], acc[:]), reads=[r_acc], writes=[r_ost])
                        dst = acca[g].rearrange("(b p r) h c -> r p b h c", p=128, r=r)[rho][:, :, h, :]
                        S.dma("sp", dst, ost[:, 0:nb, :], reads=[r_ost])
            if self.phase_end("attn_a"):
                return True
        with ExitStack() as st:
            a_ring = self.sb(st, "ca", [128, 3, 8, 129], F32, 2)
            s_ring = self.sb(st, "cs_", [128, 8, 129], F32, 2)
            rc_ring = self.sb(st, "crc", [128, 8], F32, 2)
            o_ring = self.sb(st, "co", [128, 8, 128], BF16, 2)
            for t in range(NT):
                a, r_a = a_ring[t % 2]
                sm, r_sm = s_ring[t % 2]
                rc, r_rc = rc_ring[t % 2]
                o, r_o = o_ring[t % 2]
                for g in range(3):
                    S.dma("sp", a[:, g, :, :], acca[g, t * 128:(t + 1) * 128, :, :], writes=[r_a])
                S.op("dve", lambda e, sm=sm, a=a: e.tensor_tensor(sm[:], a[:, 0, :, :], a[:, 1, :, :], ALU.add), reads=[r_a], writes=[r_sm])
                S.op("dve", lambda e, sm=sm, a=a: e.tensor_tensor(sm[:], sm[:], a[:, 2, :, :], ALU.add), reads=[r_a, r_sm], writes=[r_sm])
                S.op("dve", lambda e, sm=sm, rc=rc: e.reciprocal(rc[:], sm[:, :, 128]), reads=[r_sm], writes=[r_rc])
                for h in range(8):
                    eng = "dve" if h % 2 else "pool"
                    S.op(eng, lambda e, o=o, sm=sm, rc=rc, h=h: e.tensor_scalar(o[:, h, :], sm[:, h, 0:128], rc[:, h:h + 1], None, ALU.mult),
                         reads=[r_sm, r_rc], writes=[r_o])
                S.dma("sp", self.mixed[t * 128:(t + 1) * 128, 0:1024], o[:].rearrange("p h d -> p (h d)"), reads=[r_o])
            return self.phase_end("comb_a")

    def outproj_phase(self, h_src, wb_out, h_dst, wname=None):
        S = self.S
        wres = self.wres[wname]
        with ExitStack() as st:
            bufs = self.load_T_bufs(st, False, BF16)
            (mT, r_mT), = self.sb(st, "mT", [128, 16, 2048], BF16)
            wt_ring = self.sb(st, "wto", [128, 16, 512], BF16, 2)
            po_ring = self.ps(st, "poo", [128, 512], F32, 4)
            hr_ring = self.sb(st, "hr", [128, 512], F32, 3)
            n_w = 0
            n_p = 0
            for c in range(2):
                tok0 = c * 2048
                self.load_T(st, self.mixed, tok0, 2048, mT, r_mT, 0, bufs=bufs)
                for s in range(4):
                    wt, r_wt = wt_ring[n_w % 2]
                    n_w += 1
                    S.dma("sp", wt[:], wb_out[s], reads=wres[s], writes=[r_wt])
                    for t in range(16):
                        po, r_po = po_ring[n_p % 4]
                        hr, r_hr = hr_ring[n_p % 3]
                        n_p += 1
                        rows = slice(tok0 + t * 128, tok0 + (t + 1) * 128)
                        S.dma("act", hr[:], h_src[rows, s * 512:(s + 1) * 512], writes=[r_hr])

                        def mm(e, po=po, wt=wt, t=t):
                            for k in range(16):
                                i = e.matmul(po[:], mT[:, k, t * 128:(t + 1) * 128], wt[:, k, :], start=(k == 0), stop=(k == 15))
                            return i
                        S.op("pe", mm, reads=[r_wt, r_mT], writes=[r_po])
                        S.op("dve", lambda e, hr=hr, po=po: e.tensor_tensor(hr[:], hr[:], po[:], ALU.add), reads=[r_po, r_hr], writes=[r_hr])
                        S.dma("sp", h_dst[rows, s * 512:(s + 1) * 512], hr[:], reads=[r_hr])
            return self.phase_end("outproj")

    def ffn_phase(self, h_src, gain_ap, wb_up, wb_dn, conv_w, conv_b, h_dst, wn_up=None, wn_dn=None):
        S = self.S
        NFC = DFF // 128
        wres_up = self.wres[wn_up]
        wres_dn = self.wres[wn_dn]
        with ExitStack() as st:
            bufs = self.load_T_bufs(st, True, F32, npT=1)
            gb, r_gb = self.load_gain(st, gain_ap)
            yT_ring = self.sb(st, "y2T", [128, 16, 514], BF16, 1)
            guT_ring = self.sb(st, "guT", [128, NFC, 512], BF16, 1)
            wg_ring = self.sb(st, "wg", [128, 16, 256], BF16, 2)
            wu_ring = self.sb(st, "wu", [128, 16, 256], BF16, 2)
            wd_ring = self.sb(st, "wd", [128, NFC, 256], BF16, 2)
            (cw, r_cw), = self.sb(st, "cw", [128, 4, NFC], F32)
            for i in range(3):
                S.dma("sp", cw[:, i, :], conv_w[i].rearrange("(c p) -> p c", p=128), writes=[r_cw], allow_slow_non_contiguous=True)
            S.dma("sp", cw[:, 3, :], conv_b.rearrange("(c p) -> p c", p=128), writes=[r_cw], allow_slow_non_contiguous=True)
            gs_ring = self.sb(st, "gs", [128, 516], F32, 2)
            t1_ring = self.sb(st, "t1", [128, 512], F32, 2)
            t2_ring = self.sb(st, "t2", [128, 512], F32, 2)
            gl_ring = self.sb(st, "gl", [128, 512], F32, 2)
            hr_ring = self.sb(st, "hrf", [128, 256], F32, 3)
            psg_ring = self.ps(st, "psg", [128, 512], F32, 2)
            psu_ring = self.ps(st, "psu", [128, 512], F32, 2)
            psh_ring = self.ps(st, "psh", [128, 2], F32, 1)
            psd_ring = self.ps(st, "psd", [128, 256], F32, 2)
            n = {"w": 0, "f": 0, "wd": 0, "d": 0}
            for c in range(S_LEN // 512):
                c0 = c * 512
                yT, r_yT = yT_ring[0]
                guT, r_guT = guT_ring[0]
                self.load_T(st, h_src, c0, 512, yT, r_yT, 0, gain_b=gb, r_gain=r_gb, bufs=bufs)
                self.load_T(st, h_src, c0 - 1, 1, yT, r_yT, 512, gain_b=gb, r_gain=r_gb, bufs=bufs)
                self.load_T(st, h_src, c0 + 512, 1, yT, r_yT, 513, gain_b=gb, r_gain=r_gb, bufs=bufs)
                for sl in range(22):
                    wg, r_wg = wg_ring[n["w"] % 2]
                    wu, r_wu = wu_ring[n["w"] % 2]
                    n["w"] += 1
                    S.dma("sp", wg[:], wb_up[sl], reads=wres_up[sl], writes=[r_wg])
                    S.dma("act", wu[:], wb_up[22 + sl], reads=wres_up[22 + sl], writes=[r_wu])
                    for j in range(2):
                        fc = sl * 2 + j
                        psg, r_psg = psg_ring[n["f"] % 2]
                        psu, r_psu = psu_ring[n["f"] % 2]
                        psh, r_psh = psh_ring[0]
                        gs, r_gs = gs_ring[n["f"] % 2]
                        t1, r_t1 = t1_ring[n["f"] % 2]
                        t2, r_t2 = t2_ring[n["f"] % 2]
                        gl, r_gl = gl_ring[n["f"] % 2]
                        n["f"] += 1

                        def mmg(e, psg=psg, psh=psh, wg=wg, j=j):
                            for k in range(16):
                                e.matmul(psg[:], wg[:, k, j * 128:(j + 1) * 128], yT[:, k, 0:512], start=(k == 0), stop=(k == 15))
                            for k in range(16):
                                i = e.matmul(psh[:], wg[:, k, j * 128:(j + 1) * 128], yT[:, k, 512:514], start=(k == 0), stop=(k == 15))
                            return i
                        S.op("pe", mmg, reads=[r_wg, r_yT], writes=[r_psg, r_psh])

                        def mmu(e, psu=psu, wu=wu, j=j):
                            for k in range(16):
                                i = e.matmul(psu[:], wu[:, k, j * 128:(j + 1) * 128], yT[:, k, 0:512], start=(k == 0), stop=(k == 15))
                            return i
                        S.op("pe", mmu, reads=[r_wu, r_yT], writes=[r_psu])
                        S.op("act", lambda e, gs=gs, psg=psg: e.activation(gs[:, 1:513], psg[:], AF.Copy), reads=[r_psg], writes=[r_gs])

                        def halo(e, gs=gs, psh=psh):
                            e.tensor_copy(gs[:, 0:1], psh[:, 0:1])
                            return e.tensor_copy(gs[:, 513:514], psh[:, 1:2])
                        S.op("dve", halo, reads=[r_psh], writes=[r_gs])
                        S.op("act", lambda e, t1=t1, gs=gs, fc=fc: e.activation(t1[:], gs[:, 1:513], AF.Identity, bias=cw[:, 3, fc:fc + 1], scale=cw[:, 1, fc:fc + 1]),
                             reads=[r_gs, r_cw], writes=[r_t1])
                        S.op("dve", lambda e, t2=t2, gs=gs, t1=t1, fc=fc: e.scalar_tensor_tensor(t2[:], gs[:, 0:512], cw[:, 0, fc:fc + 1], t1[:], ALU.mult, ALU.add),
                             reads=[r_gs, r_t1, r_cw], writes=[r_t2])
                        S.op("dve", lambda e, t1=t1, gs=gs, t2=t2, fc=fc: e.scalar_tensor_tensor(t1[:], gs[:, 2:514], cw[:, 2, fc:fc + 1], t2[:], ALU.mult, ALU.add),
                             reads=[r_gs, r_t2, r_cw], writes=[r_t1])
                        S.op("act", lambda e, gl=gl, t1=t1: e.activation(gl[:], t1[:], AF.Gelu_apprx_tanh), reads=[r_t1], writes=[r_gl])
                        S.op("dve", lambda e, gl=gl, psu=psu, fc=fc: e.tensor_tensor(guT[:, fc, :], gl[:], psu[:], ALU.mult), reads=[r_gl, r_psu], writes=[r_guT])
                for ds in range(8):
                    wd, r_wd = wd_ring[n["wd"] % 2]
                    n["wd"] += 1
                    S.dma("sp", wd[:, 0:22, :], wb_dn[ds, :, 0:22, :], reads=wres_dn[ds], writes=[r_wd])
                    S.dma("act", wd[:, 22:44, :], wb_dn[ds, :, 22:44, :], reads=wres_dn[ds], writes=[r_wd])
                    for tt in range(4):
                        psd, r_psd = psd_ring[n["d"] % 2]
                        hr, r_hr = hr_ring[n["d"] % 3]
                        n["d"] += 1
                        rows = slice(c0 + tt * 128, c0 + (tt + 1) * 128)
                        S.dma("sp", hr[:], h_src[rows, ds * 256:(ds + 1) * 256], writes=[r_hr])

                        def mmd(e, psd=psd, wd=wd, tt=tt):
                            for k in range(NFC):
                                i = e.matmul(psd[:], guT[:, k, tt * 128:(tt + 1) * 128], wd[:, k, :], start=(k == 0), stop=(k == NFC - 1))
                            return i
                        S.op("pe", mmd, reads=[r_wd, r_guT], writes=[r_psd])
                        S.op("dve", lambda e, hr=hr, psd=psd: e.tensor_tensor(hr[:], hr[:], psd[:], ALU.add), reads=[r_psd, r_hr], writes=[r_hr])
                        S.dma("sp", h_dst[rows, ds * 256:(ds + 1) * 256], hr[:], reads=[r_hr])
            return self.phase_end("ffn")

    def final_norm(self, h_src, gain_ap, out):
        S = self.S
        with ExitStack() as st:
            gb, r_gb = self.load_gain(st, gain_ap)
            ht_ring = self.sb(st, "fht", [128, D], F32, 3)
            junk = self.sb(st, "fjk", [128, D], BF16, 1)[0]
            ssr = self.sb(st, "fss", [128, 1], F32, 4)
            for t in range(NT):
                ht, r_ht = ht_ring[t % 3]
                ss, r_ss = ssr[t % 4]
                jk, r_jk = junk
                S.dma("sp", ht[:], h_src[t * 128:(t + 1) * 128, :], writes=[r_ht])
                S.op("act", lambda e, ht=ht, jk=jk, ss=ss: e.activation(jk[:], ht[:], AF.Square, scale=D ** -0.5, accum_out=ss[:]), reads=[r_ht], writes=[r_jk, r_ss])
                S.op("dve", lambda e, ss=ss: e.tensor_scalar(ss[:], ss[:], EPS, None, ALU.add), reads=[r_ss], writes=[r_ss])
                S.op("act", lambda e, ss=ss: e.activation(ss[:], ss[:], AF.Sqrt), reads=[r_ss], writes=[r_ss])
                S.op("dve", lambda e, ss=ss: e.reciprocal(ss[:], ss[:]), reads=[r_ss], writes=[r_ss])
                S.op("dve", lambda e, ht=ht, ss=ss: e.scalar_tensor_tensor(ht[:], ht[:], ss[:, 0:1], gb[:], ALU.mult, ALU.mult), reads=[r_ht, r_ss, r_gb], writes=[r_ht])
                S.dma("act", out[t * 128:(t + 1) * 128, :], ht[:], reads=[r_ht])
            return self.phase_end("final")


def _t5_bucket_np(rel):
    nb = 16
    max_exact = 8
    rel = np.asarray(rel, dtype=np.int64)
    ret = np.where(rel > 0, nb, 0)
    n = np.abs(rel)
    n_f = np.maximum(n, 1).astype(np.float32)
    large = max_exact + (np.log(n_f / np.float32(max_exact)) / np.float32(math.log(1024 / max_exact)) * np.float32(nb - max_exact)).astype(np.int32)
    large = np.minimum(large, nb - 1)
    return ret + np.where(n < max_exact, n, large)


def host_tables(t5_table, na_rpb):
    t5 = np.asarray(t5_table, dtype=np.float32)
    out = {}
    i = np.arange(128)[:, None]
    c = np.arange(2176)[None, :]
    bk = _t5_bucket_np(i - c + 1024)
    out["bandB"] = np.ascontiguousarray(np.stack([t5[bk, 24 + h] for h in range(4)]).astype(np.float32))
    cb = np.zeros((128, 8), np.float32)
    for h in range(4):
        cb[:, 2 * h] = t5[15, 24 + h]
        cb[:, 2 * h + 1] = t5[31, 24 + h]
    out["constB"] = cb
    p = np.arange(128)[:, None, None]
    t = np.arange(2)[None, :, None]
    j = np.arange(128)[None, None, :]
    rel_sub = (p - 64 + 128 * t) - j
    valid = np.abs(rel_sub) <= 64
    mba = np.zeros((24, 128, 2, 128), np.float32)
    for g, r in enumerate((1, 4, 16)):
        bk = _t5_bucket_np(r * rel_sub)
        for h in range(8):
            mba[g * 8 + h] = np.where(valid, t5[bk, g * 8 + h], np.float32(NEG))
    out["mbA"] = mba
    rpb = np.asarray(na_rpb, dtype=np.float32)[0]
    e = (np.arange(128) // 64)[:, None, None]
    bp = (np.arange(128) % 64)[:, None, None]
    a = np.arange(4)[None, :, None]
    jj = np.arange(64)[None, None, :]
    cs = np.clip(jj - 8, 0, 48)
    validc = (bp >= cs) & (bp < cs + 16)
    dc = np.clip(bp - jj + 15, 0, 30)
    mbd = np.zeros((8, 8, 128, 4, 64), np.float32)
    for di in range(8):
        delta = -di
        dr = delta + 2 * a + e + 7
        drc = np.clip(dr, 0, 14) + 0 * jj
        for h in range(8):
            vals = rpb[h][drc, dc + 0 * a]
            mbd[h, di] = np.where(validc & (dr >= 0) & (dr <= 14), vals, np.float32(NEG))
    out["mbD"] = mbd
    tpos = np.arange(S_LEN)
    row = (tpos // 64).astype(np.float32)
    col = (tpos % 64).astype(np.float32)
    inv_freq = (np.float32(10000.0) ** (-(np.arange(0, 64, 2, dtype=np.float32) / np.float32(64)))).astype(np.float32)
    ang = np.concatenate([row[:, None] * inv_freq[None], col[:, None] * inv_freq[None]], axis=-1).astype(np.float32)
    out["ropec"] = np.cos(ang).astype(np.float32)
    out["ropes"] = np.sin(ang).astype(np.float32)
    return out


def _attn_D(self, v1d, mbd_d):
    S = self.S
    with ExitStack() as st:
        qn_ring = self.sb(st, "qd", [128, S_LEN], BF16, 2)
        kn_ring = self.sb(st, "kd", [128, S_LEN], BF16, 2)
        v0_ring = self.sb(st, "vd0", [128, 32, 129], BF16, 2)
        v1_ring = self.sb(st, "vd1", [128, 31, 129], BF16, 2)
        mb_ring = self.sb(st, "mbd", [128, 8, 4, 64], F32, 2)
        tmp_ring = self.sb(st, "tmpd", [128, 4, 64], F32, 3)
        pt_ring = self.sb(st, "ptd", [128, 4, 64], BF16, 3)
        rec_ring = self.sb(st, "recd", [64, 1], F32, 4)
        ost_ring = self.sb(st, "ostd", [64, 64, 128], BF16, 2)
        pss_ring = self.ps(st, "pssd", [128, 4, 64], F32, 3)
        acc_ring = self.ps(st, "accd", [64, 129], F32, 3)
        n_b = 0
        for h in range(8):
            qn, r_qn = qn_ring[h % 2]
            kn, r_kn = kn_ring[h % 2]
            v0, r_v0 = v0_ring[h % 2]
            v1, r_v1 = v1_ring[h % 2]
            mb, r_mb = mb_ring[h % 2]
            ost, r_ost = ost_ring[h % 2]
            S.dma("sp", qn[:], self.qt[10 + h], writes=[r_qn])
            S.dma("sp", kn[:], self.qt[18 + h], writes=[r_kn])
            S.dma("sp", mb[:], mbd_d[h].rearrange("d p a j -> p d a j"), writes=[r_mb])
            for part in range(4):
                S.dma("pool", v0[:, part * 8:(part + 1) * 8, :], v1d[part * 1024:(part + 1) * 1024, h, :].rearrange("(t p) c -> p t c", p=128), writes=[r_v0])
            S.dma("pool", v1[:, 0:16, :], v1d[64:64 + 2048, h, :].rearrange("(t p) c -> p t c", p=128), writes=[r_v1])
            S.dma("pool", v1[:, 16:31, :], v1d[64 + 2048:64 + 2048 + 1920, h, :].rearrange("(t p) c -> p t c", p=128), writes=[r_v1])
            for i in range(64):
                rs = min(max(i - 4, 0), 56)
                di = i - rs
                pss, r_pss = pss_ring[n_b % 3]
                tmp, r_tmp = tmp_ring[n_b % 3]
                pt, r_pt = pt_ring[n_b % 3]
                acc, r_acc = acc_ring[n_b % 3]
                rec, r_rec = rec_ring[n_b % 4]
                n_b += 1

                def qk(e, pss=pss, kn=kn, qn=qn, rs=rs, i=i):
                    for a in range(4):
                        k0 = 64 * rs + 128 * a
                        ins = e.matmul(pss[:, a, :], kn[:, k0:k0 + 128], qn[:, 64 * i:64 * i + 64], start=True, stop=True)
                    return ins
                S.op("pe", qk, reads=[r_kn, r_qn], writes=[r_pss])
                S.op("dve", lambda e, tmp=tmp, pss=pss, mb=mb, di=di: e.scalar_tensor_tensor(tmp[:], pss[:], SCALE, mb[:, di, :, :], ALU.mult, ALU.add),
                     reads=[r_pss, r_mb], writes=[r_tmp])
                S.op("act", lambda e, pt=pt, tmp=tmp: e.activation(pt[:], tmp[:], AF.Exp), reads=[r_tmp], writes=[r_pt])
                if rs % 2 == 0:
                    vv, r_vv, tb = v0, r_v0, rs // 2
                else:
                    vv, r_vv, tb = v1, r_v1, (rs - 1) // 2

                def pv(e, acc=acc, pt=pt, vv=vv, tb=tb):
                    for a in range(4):
                        ins = e.matmul(acc[:], pt[:, a, :], vv[:, tb + a, :], start=(a == 0), stop=(a == 3))
                    return ins
                S.op("pe", pv, reads=[r_pt, r_vv], writes=[r_acc])
                S.op("dve", lambda e, rec=rec, acc=acc: e.reciprocal(rec[:], acc[:, 128:129]), reads=[r_acc], writes=[r_rec])
                S.op("dve", lambda e, ost=ost, acc=acc, rec=rec, i=i: e.tensor_scalar(ost[:, i, :], acc[:, 0:128], rec[:, 0:1], None, ALU.mult),
                     reads=[r_acc, r_rec], writes=[r_ost])
            S.dma("sp", self.mixed[:, 1024 + h * 128:1024 + (h + 1) * 128].rearrange("(i p) c -> p i c", p=64), ost[:], reads=[r_ost])
        return self.phase_end("attn_d")


KB.attn_D = _attn_D


def build(debug=(), stop_after=None, l1_only=False):
    kb = KB(debug, stop_after)
    nc = kb.nc
    S = kb.S
    I = {}
    I["x"] = kb.inp("x", [S_LEN, D])
    I["ln_mix"] = kb.inp("ln_mix", [2, D])
    I["ln_ffn"] = kb.inp("ln_ffn", [2, D])
    I["ln_final"] = kb.inp("ln_final", [D])
    I["ev_w_in"] = kb.inp("ev_w_in", [D, 12288])
    I["ev_w_out"] = kb.inp("ev_w_out", [D, D])
    for nme in ("diff_lq1", "diff_lk1", "diff_lq2", "diff_lk2"):
        I[nme] = kb.inp(nme, [128])
    I["diff_subln"] = kb.inp("diff_subln", [256])
    I["od_w_in"] = kb.inp("od_w_in", [D, 4608])
    I["od_w_out"] = kb.inp("od_w_out", [D, D])
    I["gqa_q_norm"] = kb.inp("gqa_q_norm", [128])
    I["gqa_k_norm"] = kb.inp("gqa_k_norm", [128])
    I["ffn_w_up"] = kb.inp("ffn_w_up", [2, D, 2 * DFF])
    I["ffn_conv_w"] = kb.inp("ffn_conv_w", [2, 3, DFF])
    I["ffn_conv_b"] = kb.inp("ffn_conv_b", [2, DFF])
    I["ffn_w_down"] = kb.inp("ffn_w_down", [2, DFF, D])
    I["bandB"] = kb.inp("bandB", [4, 128, 2176])
    I["constB"] = kb.inp("constB", [128, 8])
    I["mbA"] = kb.inp("mbA", [24, 128, 2, 128])
    I["mbD"] = kb.inp("mbD", [8, 8, 128, 4, 64])
    I["ropec"] = kb.inp("ropec", [S_LEN, 64])
    I["ropes"] = kb.inp("ropes", [S_LEN, 64])
    out = nc.dram_tensor("out", [S_LEN, D], F32, kind="ExternalOutput").ap()

    kb.qt = kb.dram("qt", [64, 128, S_LEN], BF16)
    kb.mixed = kb.dram("mixed", [S_LEN, D], BF16)
    v1a = kb.dram("v1a", [3, S_LEN, 8, 129], BF16)
    v1b = kb.dram("v1b", [S_LEN, 4, 257], BF16)
    v1c = kb.dram("v1c", [S_LEN, 2, 129], BF16)
    v1d = kb.dram("v1d", [S_LEN, 8, 129], BF16)
    acca = kb.dram("acca", [3, S_LEN, 8, 129], F32)
    hA = kb.dram("hA", [S_LEN, D], F32)
    hB = kb.dram("hB", [S_LEN, D], F32)

    def done():
        kb.es.close()
        return nc

    kb.setup_consts()
    if l1_only:
        hB = kb.inp("hB_in", [S_LEN, D])
    if not l1_only:
        wb_in0 = kb.precast(I["ev_w_in"], "wb_in0", D, 12288, 512)
        wb_out0 = kb.precast(I["ev_w_out"], "wb_out0", D, D, 512)
        wb_up0 = kb.precast(I["ffn_w_up"][0], "wb_up0", D, 2 * DFF, 256)
        wb_dn0 = kb.precast(I["ffn_w_down"][0], "wb_dn0", DFF, D, 256, kstep=11)
    wb_in1 = kb.precast(I["od_w_in"], "wb_in1", D, 4608, 512)
    wb_out1 = kb.precast(I["od_w_out"], "wb_out1", D, D, 512)
    wb_up1 = kb.precast(I["ffn_w_up"][1], "wb_up1", D, 2 * DFF, 256)
    wb_dn1 = kb.precast(I["ffn_w_down"][1], "wb_dn1", DFF, D, 256, kstep=11)
    if kb.stop_after == "precast":
        kb.phase_end("precast")
        return done()

    if l1_only:
        return _layer1(kb, I, hA, hB, v1c, v1d, wb_in1, wb_out1, wb_up1, wb_dn1, out, done)
    slabs0 = []
    for g in range(3):
        slabs0.append(("fm", [g * 16 + h for h in range(0, 4)]))
        slabs0.append(("fm", [g * 16 + h for h in range(4, 8)]))
        slabs0.append(("fm", [g * 16 + 8 + h for h in range(0, 4)]))
        slabs0.append(("fm", [g * 16 + 8 + h for h in range(4, 8)]))
        slabs0.append(("tmv", v1a[g], 0, 4, 128))
        slabs0.append(("tmv", v1a[g], 4, 4, 128))
    slabs0.append(("fm", [48 + i for i in range(0, 4)]))
    slabs0.append(("fm", [48 + i for i in range(4, 8)]))
    slabs0.append(("fm", [56 + i for i in range(0, 4)]))
    slabs0.append(("fm", [56 + i for i in range(4, 8)]))
    slabs0.append(("tmv", v1b, 0, 2, 256))
    slabs0.append(("tmv", v1b, 2, 2, 256))
    if kb.proj_phase(I["x"], I["ln_mix"][0], wb_in0, slabs0, wname="wb_in0"):
        return done()
    if kb.attn_A(v1a, I["mbA"], acca):
        return done()
    lambda_init0 = 0.8 - 0.6 * math.exp(-0.3 * 0)
    if kb.attn_B(v1b, I["bandB"], I["constB"], I["diff_lq1"], I["diff_lk1"], I["diff_lq2"], I["diff_lk2"], I["diff_subln"], lambda_init0):
        return done()
    if kb.outproj_phase(I["x"], wb_out0, hA, wname="wb_out0"):
        return done()
    if kb.ffn_phase(hA, I["ln_ffn"][0], wb_up0, wb_dn0, I["ffn_conv_w"][0], I["ffn_conv_b"][0], hB, wn_up="wb_up0", wn_dn="wb_dn0"):
        return done()
    return _layer1(kb, I, hA, hB, v1c, v1d, wb_in1, wb_out1, wb_up1, wb_dn1, out, done)


def _layer1(kb, I, hA, hB, v1c, v1d, wb_in1, wb_out1, wb_up1, wb_dn1, out, done):
    slabs1 = [
        ("tmc", [("q", 0), ("q", 1), ("q", 2), ("q", 3)]),
        ("tmc", [("q", 4), ("q", 5), ("q", 6), ("q", 7)]),
        ("tmc", [("k", 8), ("k", 9), ("v", v1c, 0), ("v", v1c, 1)]),
        ("fm", [10, 11, 12, 13]), ("fm", [14, 15, 16, 17]),
        ("fm", [18, 19, 20, 21]), ("fm", [22, 23, 24, 25]),
        ("tmv", v1d, 0, 4, 128), ("tmv", v1d, 4, 4, 128),
    ]
    if kb.proj_phase(hB, I["ln_mix"][1], wb_in1, slabs1, rope=(I["ropec"], I["ropes"], I["gqa_q_norm"], I["gqa_k_norm"]), wname="wb_in1"):
        return done()
    if kb.attn_C(v1c):
        return done()
    if kb.attn_D(v1d, I["mbD"]):
        return done()
    if kb.outproj_phase(hB, wb_out1, hA, wname="wb_out1"):
        return done()
    if kb.ffn_phase(hA, I["ln_ffn"][1], wb_up1, wb_dn1, I["ffn_conv_w"][1], I["ffn_conv_b"][1], hB, wn_up="wb_up1", wn_dn="wb_dn1"):
        return done()
    kb.final_norm(hB, I["ln_final"], out)
    return done()


ACTIVE = [0, 1, 4, 5]


def make_in_maps(inputs, ncores=8):
    aux = host_tables(inputs["t5_table"], inputs["na_rpb"])
    f = lambda a: np.ascontiguousarray(np.asarray(a, dtype=np.float32))
    shared = {
        "ln_mix": f(inputs["ln_mix"]), "ln_ffn": f(inputs["ln_ffn"]), "ln_final": f(inputs["ln_final"]),
        "ev_w_in": f(inputs["ev_w_in"][0]), "ev_w_out": f(inputs["ev_w_out"][0]),
        "diff_lq1": f(inputs["diff_lq1"][0]), "diff_lk1": f(inputs["diff_lk1"][0]),
        "diff_lq2": f(inputs["diff_lq2"][0]), "diff_lk2": f(inputs["diff_lk2"][0]),
        "diff_subln": f(inputs["diff_subln"][0]),
        "od_w_in": f(inputs["od_w_in"][0]), "od_w_out": f(inputs["od_w_out"][0]),
        "gqa_q_norm": f(inputs["gqa_q_norm"][0]), "gqa_k_norm": f(inputs["gqa_k_norm"][0]),
        "ffn_w_up": f(inputs["ffn_w_up"]), "ffn_conv_w": f(inputs["ffn_conv_w"]), "ffn_conv_b": f(inputs["ffn_conv_b"]),
        "ffn_w_down": f(inputs["ffn_w_down"]),
    }
    shared.update(aux)
    x = np.asarray(inputs["x"], dtype=np.float32)
    maps = []
    if ncores == 1:
        m = dict(shared)
        m["x"] = np.ascontiguousarray(x[0])
        return [m]
    zero = {k_: np.zeros_like(v) for k_, v in shared.items()}
    zero["x"] = np.zeros_like(x[0])
    for c in range(ncores):
        if c in ACTIVE:
            m = dict(shared)
            m["x"] = np.ascontiguousarray(x[ACTIVE.index(c)])
        else:
            m = zero
        maps.append(m)
    return maps


def kernel(**inputs):
    nc = build()
    maps = make_in_maps(inputs, 8)
    res = run_bass_kernel_spmd(nc, maps, core_ids=list(range(8)))
    out = np.stack([np.asarray(res.results[c]["out"], dtype=np.float32) for c in ACTIVE], axis=0)
    return out
```

```python
import numpy as np
import concourse.bass as bass
import concourse.mybir as mybir

F32 = mybir.dt.float32
BF16 = mybir.dt.bfloat16
AF = mybir.ActivationFunctionType
ALU = mybir.AluOpType
AX = mybir.AxisListType

ENGS = ("pe", "act", "dve", "pool", "sp")
N_DMA_SEMS = 24
N_BG_SEMS = 12
SYNC_SAME_ENGINE = True


class Res:
    __slots__ = ("lw", "rd", "name", "excl")

    def __init__(self, name="", excl=False):
        self.lw = None
        self.rd = []
        self.name = name
        self.excl = excl


class Op:
    __slots__ = ("eng", "fn", "deps", "signaled", "count", "is_dma", "dsem", "dtarget", "prev_same_sem", "idx", "epoch", "bg")


class Sched:
    def __init__(self, nc, es):
        self.nc = nc
        self.ops = {e: [] for e in ENGS}
        self.emitted = {e: 0 for e in ENGS}
        self.sig_count = {e: 0 for e in ENGS}
        self.esem = {e: es.enter_context(nc.semaphore("s_" + e)) for e in ENGS if e != "sp"}
        self.dsems = [es.enter_context(nc.semaphore("d%d" % i)) for i in range(N_DMA_SEMS)]
        self.dsem_count = [0] * N_DMA_SEMS
        self.dsem_last = [None] * N_DMA_SEMS
        self.dma_rr = 0
        self.bsems = [es.enter_context(nc.semaphore("b%d" % i)) for i in range(N_BG_SEMS)]
        self.bsem_count = [0] * N_BG_SEMS
        self.bsem_last = [None] * N_BG_SEMS
        self.bg_rr = 0
        self.waited = {e: {} for e in ENGS}
        self.last_op = {e: None for e in ENGS}
        self.all_dma = []
        self.epoch = 0

    def engine(self, e):
        nc = self.nc
        return {"pe": nc.tensor, "act": nc.scalar, "dve": nc.vector, "pool": nc.gpsimd, "sp": nc.sync}[e]

    def _mk(self, eng, fn, reads, writes, is_dma):
        op = Op()
        op.eng = eng
        op.fn = fn
        op.is_dma = is_dma
        op.signaled = False
        op.count = None
        op.dsem = None
        op.prev_same_sem = None
        op.bg = False
        deps = []
        xr = [r for r in reads if r.excl]
        if xr:
            reads = [r for r in reads if not r.excl]
            writes = list(writes) + [r for r in xr if r not in writes]
        for r in reads:
            if r.lw is not None:
                deps.append(r.lw)
        for w in writes:
            if w.lw is not None:
                deps.append(w.lw)
            deps.extend(w.rd)
        for r in reads:
            r.rd.append(op)
        for w in writes:
            w.lw = op
            w.rd = []
        seen = set()
        ud = []
        for d in deps:
            if id(d) in seen or d is op:
                continue
            seen.add(id(d))
            ud.append(d)
        op.deps = ud
        op.epoch = self.epoch
        op.idx = len(self.ops[eng])
        self.ops[eng].append(op)
        self.last_op[eng] = op
        return op

    def op(self, eng, fn, reads=(), writes=()):
        return self._mk(eng, fn, reads, writes, False)

    def dma(self, eng, out, in_, reads=(), writes=(), bg=False, **kw):
        def fn(e):
            return e.dma_start(out=out, in_=in_, **kw)
        op = self._mk(eng, fn, reads, writes, True)
        if bg:
            op.bg = True
            s = self.bg_rr
            self.bg_rr = (self.bg_rr + 1) % N_BG_SEMS
            op.dsem = s
            self.bsem_count[s] += 16
            op.dtarget = self.bsem_count[s]
            op.prev_same_sem = self.bsem_last[s]
            self.bsem_last[s] = op
            return op
        s = self.dma_rr
        self.dma_rr = (self.dma_rr + 1) % N_DMA_SEMS
        op.dsem = s
        self.dsem_count[s] += 16
        op.dtarget = self.dsem_count[s]
        op.prev_same_sem = self.dsem_last[s]
        self.dsem_last[s] = op
        self.all_dma.append(op)
        return op

    def barrier(self):
        lasts = [self.last_op[e] for e in ENGS if self.last_op[e] is not None and not self.last_op[e].is_dma]
        dl = [d for d in self.dsem_last if d is not None]
        lastc = []
        for e in ENGS:
            for o in reversed(self.ops[e]):
                if not o.is_dma and o.fn is not None:
                    lastc.append(o)
                    break
        for e in ENGS:
            op = Op()
            op.eng = e
            op.fn = None
            op.is_dma = False
            op.signaled = False
            op.count = None
            op.dsem = None
            op.prev_same_sem = None
            op.bg = False
            op.deps = [d for d in (lastc + dl)]
            op.epoch = self.epoch
            op.idx = len(self.ops[e])
            self.ops[e].append(op)
        self.epoch += 1

    def flush(self):
        nc = self.nc
        self.barrier()
        for e in ENGS:
            for o in self.ops[e][self.emitted[e]:]:
                for d in o.deps:
                    if d.epoch < o.epoch and not d.bg:
                        continue
                    if not d.is_dma:
                        if d.eng == o.eng and (d.eng == "pe" or not SYNC_SAME_ENGINE):
                            continue
                        d.signaled = True
        for e in ENGS:
            for o in self.ops[e][self.emitted[e]:]:
                if o.signaled and o.count is None and not o.is_dma:
                    self.sig_count[e] += 1
                    o.count = self.sig_count[e]
        sched = self

        def emit_engine(e, engobj):
            waited = sched.waited[e]
            for o in sched.ops[e][sched.emitted[e]:]:
                deps = list(o.deps)
                if o.is_dma and o.prev_same_sem is not None:
                    deps.append(o.prev_same_sem)
                for d in deps:
                    if d.epoch < o.epoch and not d.bg:
                        continue
                    if d.is_dma and d.bg:
                        key = ("b", d.dsem)
                        val = d.dtarget
                        sem = sched.bsems[d.dsem]
                    elif d.is_dma:
                        key = ("d", d.dsem)
                        val = d.dtarget
                        sem = sched.dsems[d.dsem]
                    else:
                        if d.eng == e and (e == "pe" or not SYNC_SAME_ENGINE):
                            continue
                        assert d.count is not None, "dep on unsignaled op"
                        key = ("e", d.eng)
                        val = d.count
                        sem = sched.esem[d.eng]
                    if waited.get(key, 0) >= val:
                        continue
                    waited[key] = val
                    engobj.wait_ge(sem, val)
                if o.fn is None:
                    continue
                ins = o.fn(engobj)
                if o.is_dma and o.bg:
                    ins.then_inc(sched.bsems[o.dsem], 16)
                elif o.is_dma:
                    ins.then_inc(sched.dsems[o.dsem], 16)
                elif o.signaled:
                    ins.then_inc(sched.esem[e], 1)
            sched.emitted[e] = len(sched.ops[e])

        with nc.Block() as block:
            @block.tensor
            def _(eng):
                emit_engine("pe", eng)

            @block.scalar
            def _(eng):
                emit_engine("act", eng)

            @block.vector
            def _(eng):
                emit_engine("dve", eng)

            @block.gpsimd
            def _(eng):
                emit_engine("pool", eng)

            @block.sync
            def _(eng):
                emit_engine("sp", eng)

import math
from contextlib import ExitStack
from concourse.bass_utils import run_bass_kernel_spmd

S_LEN = 4096
D = 2048
DFF = 5632
NT = S_LEN // 128
EPS = 1e-6
NEG = -30000.0
SCALE = 128 ** -0.5
DIL = (1, 4, 16)


class KB:
    def __init__(self, debug=(), stop_after=None):
        self.nc = bass.Bass("TRN2", target_bir_lowering=False)
        self.debug = set(debug)
        self.stop_after = stop_after
        self.es = ExitStack()
        self.S = Sched(self.nc, self.es)
        self.rid = 0
        self.wres = {}

    def dram(self, name, shape, dtype, kind=None):
        if kind is None:
            kind = "ExternalOutput" if name in self.debug else "Internal"
        return self.nc.dram_tensor(name, list(shape), dtype, kind=kind).ap()

    def inp(self, name, shape, dtype=F32):
        return self.nc.dram_tensor(name, list(shape), dtype, kind="ExternalInput").ap()

    def res(self, name=""):
        self.rid += 1
        return Res(name + str(self.rid))

    def sb(self, st, name, shape, dtype, n=1):
        out = []
        for i in range(n):
            t = st.enter_context(self.nc.sbuf_tensor("%s_%d_%d" % (name, i, self.rid), list(shape), dtype))
            self.rid += 1
            out.append((t, self.res(name)))
        return out

    def ps(self, st, name, shape, dtype, n=1):
        out = []
        for i in range(n):
            t = st.enter_context(self.nc.psum_tensor("%s_%d_%d" % (name, i, self.rid), list(shape), dtype))
            self.rid += 1
            r = self.res(name)
            r.excl = True
            out.append((t, r))
        return out

    def phase_end(self, name):
        self.S.flush()
        return self.stop_after == name

    def precast(self, w, name, K, N, sc, kstep=16):
        KC = K // 128
        ns = N // sc
        wb = self.dram(name, [ns, 128, KC, sc], BF16)
        rs = []
        for s in range(ns):
            parts = []
            for k0 in range(0, KC, kstep):
                k1 = min(KC, k0 + kstep)
                r = self.res(name)
                parts.append(r)
                self.S.dma("pool", wb[s, :, k0:k1, :],
                           w[k0 * 128:k1 * 128, s * sc:(s + 1) * sc].rearrange("(k p) n -> p k n", p=128),
                           writes=[r], bg=True)
            rs.append(parts)
        self.wres[name] = rs
        return wb

    def setup_consts(self):
        st = self.es
        S = self.S
        (identf, r_if), = self.sb(st, "identf", [128, 128], F32)
        (ident, r_id), = self.sb(st, "ident", [128, 128], BF16)

        def mk_ident(e):
            e.memset(identf[:], 0.0)
            return e.affine_select(identf[:], identf[:], pattern=[[-1, 128]], compare_op=ALU.not_equal,
                                   fill=1.0, base=0, channel_multiplier=1)
        S.op("pool", mk_ident, writes=[r_if])
        S.op("dve", lambda e: e.tensor_copy(ident[:], identf[:]), reads=[r_if], writes=[r_id])
        self.ident = ident
        self.r_ident = r_id

    def load_T(self, st, src, t0, ntok, xT, r_xT, col0, gain_b=None, r_gain=None, dt_in=F32, bufs=None):
        S = self.S
        ht_ring, yb_ring, junk, ssr, pT_ring = bufs
        ntile = (ntok + 127) // 128
        for ti in range(ntile):
            r0 = t0 + ti * 128
            n = min(128, ntok - ti * 128)
            ht, r_ht = ht_ring[self.cnt_ht % len(ht_ring)]
            self.cnt_ht += 1
            lo = max(r0, 0)
            hi = min(r0 + n, S_LEN)
            if lo > r0 or hi < r0 + n:
                S.op("pool", lambda e, ht=ht, n=n: e.memset(ht[0:n, :], 0.0), writes=[r_ht])
            if hi > lo:
                S.dma("sp", ht[lo - r0:hi - r0, :], src[lo:hi, :], writes=[r_ht])
            if gain_b is not None:
                yb, r_yb = yb_ring[self.cnt_yb % len(yb_ring)]
                self.cnt_yb += 1
                (jk, r_jk) = junk
                (ss, r_ss) = ssr[self.cnt_yb % len(ssr)]
                S.op("act", lambda e, ht=ht, jk=jk, ss=ss, n=n: e.activation(jk[0:n, :], ht[0:n, :], AF.Square, scale=D ** -0.5, accum_out=ss[0:n, :]),
                     reads=[r_ht], writes=[r_jk, r_ss])
                S.op("dve", lambda e, ss=ss, n=n: e.tensor_scalar(ss[0:n, :], ss[0:n, :], EPS, None, ALU.add), reads=[r_ss], writes=[r_ss])
                S.op("act", lambda e, ss=ss, n=n: e.activation(ss[0:n, :], ss[0:n, :], AF.Sqrt), reads=[r_ss], writes=[r_ss])
                S.op("dve", lambda e, ss=ss, n=n: e.reciprocal(ss[0:n, :], ss[0:n, :]), reads=[r_ss], writes=[r_ss])
                S.op("dve", lambda e, yb=yb, ht=ht, ss=ss, n=n: e.scalar_tensor_tensor(yb[0:n, :], ht[0:n, :], ss[0:n, 0:1], gain_b[0:n, :], ALU.mult, ALU.mult),
                     reads=[r_ht, r_ss, r_gain], writes=[r_yb])
                srcT, r_srcT = yb, r_yb
            else:
                srcT, r_srcT = ht, r_ht
            for g in range(2):
                pT, r_pT = pT_ring[self.cnt_pT % len(pT_ring)]
                self.cnt_pT += 1

                def tr(e, pT=pT, srcT=srcT, g=g, n=n):
                    for j in range(8):
                        k = g * 8 + j
                        i = e.transpose(pT[:, j, 0:n], srcT[0:n, k * 128:(k + 1) * 128], self.ident[0:n, 0:n])
                    return i
                S.op("pe", tr, reads=[r_srcT, self.r_ident], writes=[r_pT])
                c0 = col0 + ti * 128
                eng = "act" if (self.cnt_pT % 2) else "dve"
                if eng == "act":
                    S.op("act", lambda e, pT=pT, g=g, c0=c0, n=n: e.activation(xT[:, g * 8:(g + 1) * 8, c0:c0 + n], pT[:, :, 0:n], AF.Copy),
                         reads=[r_pT], writes=[r_xT])
                else:
                    S.op("dve", lambda e, pT=pT, g=g, c0=c0, n=n: e.tensor_copy(xT[:, g * 8:(g + 1) * 8, c0:c0 + n], pT[:, :, 0:n]),
                         reads=[r_pT], writes=[r_xT])

    def load_T_bufs(self, st, norm, dt_in, npT=2):
        ht_ring = self.sb(st, "ht", [128, D], dt_in, 2)
        if norm:
            yb_ring = self.sb(st, "yb", [128, D], BF16, 2)
            junk = self.sb(st, "junk", [128, D], BF16, 1)[0]
            ssr = self.sb(st, "ss", [128, 1], F32, 4)
        else:
            yb_ring = junk = ssr = None
        pT_ring = self.ps(st, "pT", [128, 8, 128], BF16, npT)
        self.cnt_ht = self.cnt_yb = self.cnt_pT = 0
        return (ht_ring, yb_ring, junk, ssr, pT_ring)

    def load_gain(self, st, g_ap, n=D):
        (gb, r_gb), = self.sb(st, "gb", [128, n], F32)
        self.S.dma("sp", gb[:], g_ap.partition_broadcast(128), writes=[r_gb])
        return gb, r_gb

    def proj_phase(self, h_src, gain_ap, wb, slabs, rope=None, wname=None):
        S = self.S
        nc = self.nc
        wres = self.wres[wname]
        with ExitStack() as st:
            bufs = self.load_T_bufs(st, True, F32)
            gb, r_gb = self.load_gain(st, gain_ap)
            (yT, r_yT), = self.sb(st, "yT", [128, 16, 2048], BF16)
            wt_ring = self.sb(st, "wt", [128, 16, 512], BF16, 2)
            po_ring = self.ps(st, "po", [128, 512], F32, 4)
            qst_ring = self.sb(st, "qst", [128, 2048], BF16, 3)
            vst_ring = {}
            cnt = {"wt": 0, "po": 0, "qst": 0, "ev": 0}
            if rope is not None:
                cos_d, sin_d, gq_ap, gk_ap = rope
                (gqb, r_gqb), = self.sb(st, "gqb", [128, 128], F32)
                (gkb, r_gkb), = self.sb(st, "gkb", [128, 128], F32)
                S.dma("sp", gqb[:], gq_ap.partition_broadcast(128), writes=[r_gqb])
                S.dma("sp", gkb[:], gk_ap.partition_broadcast(128), writes=[r_gkb])
                cs_ring = self.sb(st, "cs", [128, 2, 64], F32, 3)
                sq_ring = self.sb(st, "sq", [128, 512], F32, 2)
                ss4_ring = self.sb(st, "ss4", [128, 4], F32, 3)
                xn_ring = self.sb(st, "xn", [128, 4, 128], F32, 2)
                tt_ring = self.sb(st, "tt", [128, 4, 4, 64], F32, 2)
                xr_ring = self.sb(st, "xr", [128, 4, 128], BF16, 2)
                hst = self.sb(st, "hst", [128, 2048], BF16, 8)
                pTc_ring = self.ps(st, "pTc", [128, 4, 128], BF16, 2)
                cnt.update({"cs": 0, "sq": 0, "xn": 0, "hst": 0, "pTc": 0})

            def get_vst(nh, dv):
                key = (nh, dv)
                if key not in vst_ring:
                    ring = self.sb(st, "vst", [128, nh, dv + 1], BF16, 3)
                    for (t, r) in ring:
                        S.op("dve", lambda e, t=t: e.memset(t[:], 1.0), writes=[r])
                    vst_ring[key] = [ring, 0]
                ent = vst_ring[key]
                t, r = ent[0][ent[1] % 3]
                ent[1] += 1
                return t, r

            def evac(out_ap, in_ap, reads, writes):
                cnt["ev"] += 1
                if cnt["ev"] % 2:
                    S.op("act", lambda e: e.activation(out_ap, in_ap, AF.Copy), reads=reads, writes=writes)
                else:
                    S.op("dve", lambda e: e.tensor_copy(out_ap, in_ap), reads=reads, writes=writes)

            for c in range(2):
                tok0 = c * 2048
                self.load_T(st, h_src, tok0, 2048, yT, r_yT, 0, gain_b=gb, r_gain=r_gb, bufs=bufs)
                for si, spec in enumerate(slabs):
                    wt, r_wt = wt_ring[cnt["wt"] % 2]
                    cnt["wt"] += 1
                    S.dma("sp", wt[:], wb[si], reads=wres[si], writes=[r_wt])
                    kind = spec[0]
                    if kind == "fm":
                        for cc in range(4):
                            qid = spec[1][cc]
                            qst, r_qst = qst_ring[cnt["qst"] % 3]
                            cnt["qst"] += 1
                            for tg in range(4):
                                po, r_po = po_ring[cnt["po"] % 4]
                                cnt["po"] += 1

                                def mm(e, po=po, wt=wt, cc=cc, tg=tg):
                                    for k in range(16):
                                        i = e.matmul(po[:], wt[:, k, cc * 128:(cc + 1) * 128], yT[:, k, tg * 512:(tg + 1) * 512],
                                                     start=(k == 0), stop=(k == 15))
                                    return i
                                S.op("pe", mm, reads=[r_wt, r_yT], writes=[r_po])
                                evac(qst[:, tg * 512:(tg + 1) * 512], po[:], [r_po], [r_qst])
                            S.dma("act", self.qt[qid, :, tok0:tok0 + 2048], qst[:], reads=[r_qst])
                    elif kind == "tmv":
                        _, v1, h0, nh, dv = spec
                        for t in range(16):
                            po, r_po = po_ring[cnt["po"] % 4]
                            cnt["po"] += 1

                            def mm(e, po=po, wt=wt, t=t):
                                for k in range(16):
                                    i = e.matmul(po[:], yT[:, k, t * 128:(t + 1) * 128], wt[:, k, :], start=(k == 0), stop=(k == 15))
                                return i
                            S.op("pe", mm, reads=[r_wt, r_yT], writes=[r_po])
                            vst, r_vst = get_vst(nh, dv)
                            evac(vst[:, :, 0:dv], po[:].rearrange("p (h d) -> p h d", h=nh), [r_po], [r_vst])
                            S.dma("act", v1[tok0 + t * 128:tok0 + (t + 1) * 128, h0:h0 + nh, :], vst[:], reads=[r_vst])
                    elif kind == "tmc":
                        heads = spec[1]
                        hsts = []
                        for hh in range(4):
                            if heads[hh][0] in ("q", "k"):
                                hsts.append(hst[cnt["hst"] % 8])
                                cnt["hst"] += 1
                            else:
                                hsts.append(None)
                        nqk = sum(1 for x in heads if x[0] in ("q", "k"))
                        vheads = [x for x in heads if x[0] == "v"]
                        for t in range(16):
                            po, r_po = po_ring[cnt["po"] % 4]
                            cnt["po"] += 1

                            def mm(e, po=po, wt=wt, t=t):
                                for k in range(16):
                                    i = e.matmul(po[:], yT[:, k, t * 128:(t + 1) * 128], wt[:, k, :], start=(k == 0), stop=(k == 15))
                                return i
                            S.op("pe", mm, reads=[r_wt, r_yT], writes=[r_po])
                            if vheads:
                                nv = len(vheads)
                                vst, r_vst = get_vst(nv, 128)
                                evac(vst[:, :, 0:128], po[:, nqk * 128:512].rearrange("p (h d) -> p h d", h=nv), [r_po], [r_vst])
                                v1 = vheads[0][1]
                                S.dma("act", v1[tok0 + t * 128:tok0 + (t + 1) * 128, vheads[0][2]:vheads[0][2] + nv, :], vst[:], reads=[r_vst])
                            W = nqk * 128
                            sq, r_sq = sq_ring[cnt["sq"] % 2]
                            ss4, r_ss4 = ss4_ring[cnt["sq"] % 3]
                            cnt["sq"] += 1
                            S.op("act", lambda e, sq=sq, po=po, W=W: e.activation(sq[:, 0:W], po[:, 0:W], AF.Square, scale=128 ** -0.5),
                                 reads=[r_po], writes=[r_sq])
                            S.op("dve", lambda e, sq=sq, ss4=ss4, nqk=nqk, W=W: e.reduce_sum(ss4[:, 0:nqk], sq[:, 0:W].rearrange("p (h d) -> p h d", h=nqk), AX.X),
                                 reads=[r_sq], writes=[r_ss4])
                            S.op("dve", lambda e, ss4=ss4, nqk=nqk: e.tensor_scalar(ss4[:, 0:nqk], ss4[:, 0:nqk], EPS, None, ALU.add), reads=[r_ss4], writes=[r_ss4])
                            S.op("act", lambda e, ss4=ss4, nqk=nqk: e.activation(ss4[:, 0:nqk], ss4[:, 0:nqk], AF.Sqrt), reads=[r_ss4], writes=[r_ss4])
                            S.op("dve", lambda e, ss4=ss4, nqk=nqk: e.reciprocal(ss4[:, 0:nqk], ss4[:, 0:nqk]), reads=[r_ss4], writes=[r_ss4])
                            xn, r_xn = xn_ring[cnt["xn"] % 2]
                            tt, r_tt = tt_ring[cnt["xn"] % 2]
                            xr, r_xr = xr_ring[cnt["xn"] % 2]
                            cnt["xn"] += 1
                            for hh in range(nqk):
                                gbx, r_gbx = (gqb, r_gqb) if heads[hh][0] == "q" else (gkb, r_gkb)
                                S.op("dve", lambda e, xn=xn, po=po, ss4=ss4, hh=hh, gbx=gbx: e.scalar_tensor_tensor(
                                    xn[:, hh, :], po[:, hh * 128:(hh + 1) * 128], ss4[:, hh:hh + 1], gbx[:], ALU.mult, ALU.mult),
                                    reads=[r_po, r_ss4, r_gbx], writes=[r_xn])
                            cs, r_cs = cs_ring[cnt["cs"] % 3]
                            cnt["cs"] += 1
                            S.dma("sp", cs[:, 0, :], cos_d[tok0 + t * 128:tok0 + (t + 1) * 128, :], writes=[r_cs])
                            S.dma("sp", cs[:, 1, :], sin_d[tok0 + t * 128:tok0 + (t + 1) * 128, :], writes=[r_cs])
                            x0 = xn[:, 0:nqk, 0:128:2]
                            x1 = xn[:, 0:nqk, 1:128:2]
                            cb = cs[:, 0:1, :].broadcast_to([128, nqk, 64])
                            sb_ = cs[:, 1:2, :].broadcast_to([128, nqk, 64])
                            S.op("dve", lambda e, tt=tt, x0=x0, cb=cb, nqk=nqk: e.tensor_tensor(tt[:, 0, 0:nqk, :], x0, cb, ALU.mult), reads=[r_xn, r_cs], writes=[r_tt])
                            S.op("dve", lambda e, tt=tt, x1=x1, sb_=sb_, nqk=nqk: e.tensor_tensor(tt[:, 1, 0:nqk, :], x1, sb_, ALU.mult), reads=[r_xn, r_cs], writes=[r_tt])
                            S.op("dve", lambda e, tt=tt, x0=x0, sb_=sb_, nqk=nqk: e.tensor_tensor(tt[:, 2, 0:nqk, :], x0, sb_, ALU.mult), reads=[r_xn, r_cs], writes=[r_tt])
                            S.op("dve", lambda e, tt=tt, x1=x1, cb=cb, nqk=nqk: e.tensor_tensor(tt[:, 3, 0:nqk, :], x1, cb, ALU.mult), reads=[r_xn, r_cs], writes=[r_tt])
                            S.op("dve", lambda e, tt=tt, xr=xr, nqk=nqk: e.tensor_tensor(xr[:, 0:nqk, 0:128:2], tt[:, 0, 0:nqk, :], tt[:, 1, 0:nqk, :], ALU.subtract), reads=[r_tt], writes=[r_xr])
                            S.op("dve", lambda e, tt=tt, xr=xr, nqk=nqk: e.tensor_tensor(xr[:, 0:nqk, 1:128:2], tt[:, 2, 0:nqk, :], tt[:, 3, 0:nqk, :], ALU.add), reads=[r_tt], writes=[r_xr])
                            pTc, r_pTc = pTc_ring[cnt["pTc"] % 2]
                            cnt["pTc"] += 1

                            def tr(e, pTc=pTc, xr=xr, nqk=nqk):
                                for hh in range(nqk):
                                    i = e.transpose(pTc[:, hh, :], xr[:, hh, :], self.ident[:])
                                return i
                            S.op("pe", tr, reads=[r_xr, self.r_ident], writes=[r_pTc])
                            for hh in range(nqk):
                                evac(hsts[hh][0][:, t * 128:(t + 1) * 128], pTc[:, hh, :], [r_pTc], [hsts[hh][1]])
                        for hh in range(nqk):
                            S.dma("act", self.qt[heads[hh][1], :, tok0:tok0 + 2048], hsts[hh][0][:], reads=[hsts[hh][1]])
            return self.phase_end("proj")

    def attn_full(self, units, dv, bias=None, finish=None, extra_setup=None):
        S = self.S
        with ExitStack() as st:
            vt_ring = self.sb(st, "vt", [128, NT, dv + 1], BF16, 2)
            kt_ring = self.sb(st, "kt", [128, S_LEN], BF16, 2)
            q_ring = self.sb(st, "qsb", [128, 512], BF16, 2)
            pt_ring = self.sb(st, "pt", [128, 512], BF16, 3)
            pss_ring = self.ps(st, "pss", [128, 512], F32, 3)
            acc_ring = self.ps(st, "acc", [128, dv + 1], F32, 4)
            if bias is not None:
                band_d, constb_d = bias
                band_ring = self.sb(st, "band", [128, 2176], F32, 2)
                tmp_ring = self.sb(st, "tmpb", [128, 512], F32, 2)
                nhb = band_d.shape[0]
                (cb, r_cb), = self.sb(st, "constb", [128, nhb * 2], F32)
                S.dma("sp", cb[:], constb_d, writes=[r_cb])
            ctx = extra_setup(st) if extra_setup is not None else None
            cnt = {"vt": 0, "kt": 0, "q": 0, "pt": 0, "pss": 0, "band": 0, "tmp": 0}
            cur_v = None
            cur_band = None
            for u in units:
                vkey = (id(u["v"][0]), u["v"][1])
                if vkey != cur_v:
                    vt, r_vt = vt_ring[cnt["vt"] % 2]
                    cnt["vt"] += 1
                    v1, hidx = u["v"]
                    for part in range(4):
                        S.dma("pool", vt[:, part * 8:(part + 1) * 8, :],
                              v1[part * 1024:(part + 1) * 1024, hidx, :].rearrange("(t p) c -> p t c", p=128), writes=[r_vt])
                    cur_v = vkey
                kt, r_kt = kt_ring[cnt["kt"] % 2]
                cnt["kt"] += 1
                S.dma("sp", kt[:], self.qt[u["kid"]], writes=[r_kt])
                bh = u.get("bias_h")
                if bias is not None and bh != cur_band:
                    band, r_band = band_ring[cnt["band"] % 2]
                    cnt["band"] += 1
                    S.dma("sp", band[:], band_d[bh], writes=[r_band])
                    cur_band = bh
                for qi in range(8):
                    qsb, r_q = q_ring[cnt["q"] % 2]
                    cnt["q"] += 1
                    S.dma("sp", qsb[:], self.qt[u["qid"], :, qi * 512:(qi + 1) * 512], writes=[r_q])
                    pss_of = {}

                    def issue_qk(ki, kt=kt, r_kt=r_kt, qsb=qsb, r_q=r_q):
                        pss, r_pss = pss_ring[cnt["pss"] % 3]
                        cnt["pss"] += 1
                        pss_of[ki] = (pss, r_pss)
                        S.op("pe", lambda e, pss=pss, ki=ki: e.matmul(pss[:], kt[:, ki * 128:(ki + 1) * 128], qsb[:], start=True, stop=True),
                             reads=[r_kt, r_q], writes=[r_pss])
                    issue_qk(0)
                    issue_qk(1)
                    for ki in range(NT):
                        pss, r_pss = pss_of.pop(ki)
                        pt, r_pt = pt_ring[cnt["pt"] % 3]
                        cnt["pt"] += 1
                        if bias is None:
                            S.op("act", lambda e, pt=pt, pss=pss: e.activation(pt[:], pss[:], AF.Exp, scale=SCALE), reads=[r_pss], writes=[r_pt])
                        else:
                            delta = ki * 128 - qi * 512
                            if delta >= 1070:
                                S.op("act", lambda e, pt=pt, pss=pss, bh=bh: e.activation(pt[:], pss[:], AF.Exp, bias=cb[:, 2 * bh + 1:2 * bh + 2], scale=SCALE),
                                     reads=[r_pss, r_cb], writes=[r_pt])
                            elif delta <= -686:
                                S.op("act", lambda e, pt=pt, pss=pss, bh=bh: e.activation(pt[:], pss[:], AF.Exp, bias=cb[:, 2 * bh:2 * bh + 1], scale=SCALE),
                                     reads=[r_pss, r_cb], writes=[r_pt])
                            else:
                                s0 = 1024 - delta
                                tmp, r_tmp = tmp_ring[cnt["tmp"] % 2]
                                cnt["tmp"] += 1
                                S.op("dve", lambda e, tmp=tmp, pss=pss, band=band, s0=s0: e.scalar_tensor_tensor(tmp[:], pss[:], SCALE, band[:, s0:s0 + 512], ALU.mult, ALU.add),
                                     reads=[r_pss, r_band], writes=[r_tmp])
                                S.op("act", lambda e, pt=pt, tmp=tmp: e.activation(pt[:], tmp[:], AF.Exp), reads=[r_tmp], writes=[r_pt])
                        if ki + 2 < NT:
                            issue_qk(ki + 2)
                        for qs in range(4):
                            acc, r_acc = acc_ring[qs]
                            S.op("pe", lambda e, acc=acc, pt=pt, vt=vt, qs=qs, ki=ki: e.matmul(acc[:], pt[:, qs * 128:(qs + 1) * 128], vt[:, ki, :], start=(ki == 0), stop=(ki == NT - 1)),
                                 reads=[r_pt, r_vt], writes=[r_acc])
                    finish(st, ctx, u, qi, acc_ring)
            return self.phase_end("attn_full")

    def attn_B(self, v1b, band_d, constb_d, lq1, lk1, lq2, lk2, subln, lambda_init):
        S = self.S
        units = []
        for h in range(4):
            for m in range(2):
                units.append(dict(qid=48 + h * 2 + m, kid=56 + h * 2 + m, v=(v1b, h), bias_h=h, h=h, m=m))

        def setup(st):
            c = {}
            (lv, r_lv), = self.sb(st, "lv", [128, 4, 128], F32)
            for i, a in enumerate((lq1, lk1, lq2, lk2)):
                S.dma("sp", lv[:, i, :], a.partition_broadcast(128), writes=[r_lv])
            (pr, r_pr), = self.sb(st, "lpr", [128, 2, 128], F32)
            (sv, r_sv), = self.sb(st, "lsv", [128, 2], F32)
            (nlam, r_nlam), = self.sb(st, "nlam", [128, 1], F32)
            S.op("dve", lambda e: e.tensor_tensor(pr[:, 0, :], lv[:, 0, :], lv[:, 1, :], ALU.mult), reads=[r_lv], writes=[r_pr])
            S.op("dve", lambda e: e.tensor_tensor(pr[:, 1, :], lv[:, 2, :], lv[:, 3, :], ALU.mult), reads=[r_lv], writes=[r_pr])
            S.op("dve", lambda e: e.reduce_sum(sv[:], pr[:], AX.X), reads=[r_pr], writes=[r_sv])
            S.op("act", lambda e: e.activation(sv[:], sv[:], AF.Exp), reads=[r_sv], writes=[r_sv])
            S.op("dve", lambda e: e.tensor_tensor(nlam[:], sv[:, 1:2], sv[:, 0:1], ALU.subtract), reads=[r_sv], writes=[r_nlam])
            S.op("dve", lambda e: e.tensor_scalar(nlam[:], nlam[:], -lambda_init, None, ALU.add), reads=[r_nlam], writes=[r_nlam])
            (sg, r_sg), = self.sb(st, "subg", [128, 256], F32)
            S.dma("sp", sg[:], subln.partition_broadcast(128), writes=[r_sg])
            S.op("dve", lambda e: e.tensor_scalar(sg[:], sg[:], 1.0 - lambda_init, None, ALU.mult), reads=[r_sg], writes=[r_sg])
            (o0, r_o0), = self.sb(st, "o0", [128, 32, 256], F32)
            c["nlam"] = (nlam, r_nlam)
            c["sg"] = (sg, r_sg)
            c["o0"] = (o0, r_o0)
            c["rec"] = self.sb(st, "rec", [128, 1], F32, 4)
            c["o1"] = self.sb(st, "o1", [128, 256], F32, 2)
            c["dd"] = self.sb(st, "dd", [128, 256], F32, 2)
            c["jk"] = self.sb(st, "jkb", [128, 256], BF16, 1)[0]
            c["ss"] = self.sb(st, "ssb", [128, 1], F32, 4)
            c["ost"] = self.sb(st, "ostb", [128, 4, 256], BF16, 2)
            c["n"] = 0
            return c

        def finish(st, c, u, qi, acc_ring):
            h, m = u["h"], u["m"]
            o0, r_o0 = c["o0"]
            if m == 1:
                ost, r_ost = c["ost"][qi % 2]
            for qs in range(4):
                acc, r_acc = acc_ring[qs]
                c["n"] += 1
                rec, r_rec = c["rec"][c["n"] % 4]
                S.op("dve", lambda e, rec=rec, acc=acc: e.reciprocal(rec[:], acc[:, 256:257]), reads=[r_acc], writes=[r_rec])
                if m == 0:
                    S.op("dve", lambda e, acc=acc, rec=rec, qi=qi, qs=qs: e.tensor_scalar(o0[:, qi * 4 + qs, :], acc[:, 0:256], rec[:, 0:1], None, ALU.mult),
                         reads=[r_acc, r_rec], writes=[r_o0])
                else:
                    o1, r_o1 = c["o1"][c["n"] % 2]
                    dd, r_dd = c["dd"][c["n"] % 2]
                    ss, r_ss = c["ss"][c["n"] % 4]
                    jk, r_jk = c["jk"]
                    nlam, r_nlam = c["nlam"]
                    sg, r_sg = c["sg"]
                    S.op("dve", lambda e, o1=o1, acc=acc, rec=rec: e.tensor_scalar(o1[:], acc[:, 0:256], rec[:, 0:1], None, ALU.mult), reads=[r_acc, r_rec], writes=[r_o1])
                    S.op("dve", lambda e, dd=dd, o1=o1, qi=qi, qs=qs: e.scalar_tensor_tensor(dd[:], o1[:], nlam[:, 0:1], o0[:, qi * 4 + qs, :], ALU.mult, ALU.add),
                         reads=[r_o1, r_nlam, r_o0], writes=[r_dd])
                    S.op("act", lambda e, jk=jk, dd=dd, ss=ss: e.activation(jk[:], dd[:], AF.Square, scale=1.0 / 16.0, accum_out=ss[:]), reads=[r_dd], writes=[r_jk, r_ss])
                    S.op("dve", lambda e, ss=ss: e.tensor_scalar(ss[:], ss[:], EPS, None, ALU.add), reads=[r_ss], writes=[r_ss])
                    S.op("act", lambda e, ss=ss: e.activation(ss[:], ss[:], AF.Sqrt), reads=[r_ss], writes=[r_ss])
                    S.op("dve", lambda e, ss=ss: e.reciprocal(ss[:], ss[:]), reads=[r_ss], writes=[r_ss])
                    S.op("dve", lambda e, ost=ost, dd=dd, ss=ss, qs=qs: e.scalar_tensor_tensor(ost[:, qs, :], dd[:], ss[:, 0:1], sg[:], ALU.mult, ALU.mult),
                         reads=[r_dd, r_ss, r_sg], writes=[r_ost])
            if m == 1:
                S.dma("pool", self.mixed[qi * 512:(qi + 1) * 512, 1024 + h * 256:1024 + (h + 1) * 256].rearrange("(s p) c -> p s c", p=128),
                      ost[:], reads=[r_ost])

        return self.attn_full(units, 256, bias=(band_d, constb_d), finish=finish, extra_setup=setup)

    def attn_C(self, v1c):
        S = self.S
        units = []
        for n in range(2):
            for g in range(4):
                h = n * 4 + g
                units.append(dict(qid=h, kid=8 + n, v=(v1c, n), bias_h=None, h=h))

        def setup(st):
            c = {}
            c["rec"] = self.sb(st, "rec", [128, 1], F32, 4)
            c["ost"] = self.sb(st, "ostc", [128, 4, 128], BF16, 2)
            c["n"] = 0
            return c

        def finish(st, c, u, qi, acc_ring):
            h = u["h"]
            ost, r_ost = c["ost"][qi % 2]
            for qs in range(4):
                acc, r_acc = acc_ring[qs]
                c["n"] += 1
                rec, r_rec = c["rec"][c["n"] % 4]
                S.op("dve", lambda e, rec=rec, acc=acc: e.reciprocal(rec[:], acc[:, 128:129]), reads=[r_acc], writes=[r_rec])
                S.op("dve", lambda e, acc=acc, rec=rec, ost=ost, qs=qs: e.tensor_scalar(ost[:, qs, :], acc[:, 0:128], rec[:, 0:1], None, ALU.mult),
                     reads=[r_acc, r_rec], writes=[r_ost])
            S.dma("pool", self.mixed[qi * 512:(qi + 1) * 512, h * 128:(h + 1) * 128].rearrange("(s p) c -> p s c", p=128),
                  ost[:], reads=[r_ost])

        return self.attn_full(units, 128, bias=None, finish=finish, extra_setup=setup)

    def attn_A(self, v1a, mba_d, acca):
        S = self.S
        with ExitStack() as st:
            qn_ring = self.sb(st, "qn", [128, S_LEN], BF16, 2)
            kn_ring = self.sb(st, "kn", [128, S_LEN], BF16, 2)
            qp_ring = self.sb(st, "qp", [128, S_LEN], BF16, 2)
            kp_ring = self.sb(st, "kp", [128, 6144], BF16, 2)
            mb_ring = self.sb(st, "mb", [128, 2, 128], F32, 2)
            vt_ring = self.sb(st, "vta", [128, 48, 129], BF16, 2)
            tmp_ring = self.sb(st, "tmpa", [128, 2, 128], F32, 3)
            pt_ring = self.sb(st, "pta", [128, 2, 128], BF16, 3)
            ost_ring = self.sb(st, "osta", [128, 32, 129], F32, 2)
            pss_ring = self.ps(st, "pssa", [128, 2, 128], F32, 3)
            acc_ring = self.ps(st, "acca", [128, 129], F32, 3)
            n_u = 0
            cntA = {"pss": 0, "b": 0}
            for g in range(3):
                r = DIL[g]
                L = S_LEN // r
                nb = L // 128
                for h in range(8):
                    qn, r_qn = qn_ring[n_u % 2]
                    kn, r_kn = kn_ring[n_u % 2]
                    qp, r_qp = qp_ring[n_u % 2]
                    kp, r_kp = kp_ring[n_u % 2]
                    mb, r_mb = mb_ring[n_u % 2]
                    vtt, r_vt = vt_ring[n_u % 2]
                    ost, r_ost = ost_ring[n_u % 2]
                    n_u += 1
                    S.dma("sp", qn[:], self.qt[g * 16 + h], writes=[r_qn])
                    S.dma("sp", kn[:], self.qt[g * 16 + 8 + h], writes=[r_kn])
                    S.dma("sp", mb[:], mba_d[g * 8 + h], writes=[r_mb])
                    qpv = qp[:].rearrange("d (r n) -> d r n", r=r)
                    kpv = kp[:, 0:r * (L + 128)].rearrange("d (r n) -> d r n", r=r)
                    S.op("pool", lambda e, qpv=qpv, qn=qn, r=r: e.tensor_copy(qpv, qn[:].rearrange("d (n r) -> d r n", r=r)), reads=[r_qn], writes=[r_qp])

                    def kperm(e, kpv=kpv, kn=kn, r=r, L=L):
                        e.memset(kpv[:, :, 0:64], 0.0)
                        e.memset(kpv[:, :, 64 + L:128 + L], 0.0)
                        return e.tensor_copy(kpv[:, :, 64:64 + L], kn[:].rearrange("d (n r) -> d r n", r=r))
                    S.op("pool", kperm, reads=[r_kn], writes=[r_kp])
                    vsrc = v1a[g, :, h, :].rearrange("(n r) c -> r n c", r=r)
                    vt = vtt[:, 0:r * (nb + 1), :].rearrange("p (r i) c -> p r i c", r=r)

                    def vz(e, vt=vt, nb=nb):
                        e.memset(vt[0:64, :, 0, :], 0.0)
                        return e.memset(vt[64:128, :, nb, :], 0.0)
                    S.op("pool", vz, writes=[r_vt])
                    S.dma("sp", vt[64:128, :, 0, :], vsrc[:, 0:64, :].rearrange("r p c -> p r c"), writes=[r_vt])
                    S.dma("sp", vt[0:64, :, nb, :], vsrc[:, L - 64:L, :].rearrange("r p c -> p r c"), writes=[r_vt])
                    for rho in range(r):
                        S.dma("sp", vt[:, rho, 1:nb, :], vsrc[rho, 64:L - 64, :].rearrange("(i p) c -> p i c", p=128), writes=[r_vt])
                    blocks = [(rho, b) for rho in range(r) for b in range(nb)]
                    pss_of = {}

                    def issue_qk(n, kpv=kpv, qpv=qpv, r_kp=r_kp, r_qp=r_qp, blocks=blocks, pss_of=pss_of):
                        rho, b = blocks[n]
                        pss, r_pss = pss_ring[cntA["pss"] % 3]
                        cntA["pss"] += 1
                        pss_of[n] = (pss, r_pss)

                        def qk(e, pss=pss, rho=rho, b=b):
                            e.matmul(pss[:, 0, :], kpv[:, rho, b * 128:(b + 1) * 128], qpv[:, rho, b * 128:(b + 1) * 128], start=True, stop=True)
                            return e.matmul(pss[:, 1, :], kpv[:, rho, (b + 1) * 128:(b + 2) * 128], qpv[:, rho, b * 128:(b + 1) * 128], start=True, stop=True)
                        S.op("pe", qk, reads=[r_kp, r_qp], writes=[r_pss])
                    issue_qk(0)
                    issue_qk(1)
                    for n, (rho, b) in enumerate(blocks):
                        pss, r_pss = pss_of.pop(n)
                        tmp, r_tmp = tmp_ring[cntA["b"] % 3]
                        pt, r_pt = pt_ring[cntA["b"] % 3]
                        acc, r_acc = acc_ring[cntA["b"] % 3]
                        cntA["b"] += 1
                        S.op("dve", lambda e, tmp=tmp, pss=pss, mb=mb: e.scalar_tensor_tensor(tmp[:], pss[:], SCALE, mb[:], ALU.mult, ALU.add),
                             reads=[r_pss, r_mb], writes=[r_tmp])
                        S.op("act", lambda e, pt=pt, tmp=tmp: e.activation(pt[:], tmp[:], AF.Exp), reads=[r_tmp], writes=[r_pt])
                        if n + 2 < len(blocks):
                            issue_qk(n + 2)

                        def pv(e, acc=acc, pt=pt, vt=vt, b=b, rho=rho):
                            e.matmul(acc[:], pt[:, 0, :], vt[:, rho, b, :], start=True, stop=False)
                            return e.matmul(acc[:], pt[:, 1, :], vt[:, rho, b + 1, :], start=False, stop=True)
                        S.op("pe", pv, reads=[r_pt, r_vt], writes=[r_acc])
                        S.op("dve", lambda e, ost=ost, acc=acc, n=n: e.tensor_copy(ost[:, n, :], acc[:]), reads=[r_acc], writes=[r_ost])
                    for rho in range(r):
                        dst = acca[g].rearrange("(b p r) h c -> r p b h c", p=128, r=r)[rho][:, :, h, :]
                        S.dma("pool", dst, ost[:, rho * nb:(rho + 1) * nb, :], reads=[r_ost])
            if self.phase_end("attn_a"):
                return True
        with ExitStack() as st:
            a_ring = self.sb(st, "ca", [128, 3, 8, 129], F32, 2)
            s_ring = self.sb(st, "cs_", [128, 8, 129], F32, 2)
            rc_ring = self.sb(st, "crc", [128, 8], F32, 2)
            o_ring = self.sb(st, "co", [128, 8, 128], BF16, 2)
            for t in range(NT):
                a, r_a = a_ring[t % 2]
                sm, r_sm = s_ring[t % 2]
                rc, r_rc = rc_ring[t % 2]
                o, r_o = o_ring[t % 2]
                for g in range(3):
                    S.dma("sp", a[:, g, :, :], acca[g, t * 128:(t + 1) * 128, :, :], writes=[r_a])
                S.op("dve", lambda e, sm=sm, a=a: e.tensor_tensor(sm[:], a[:, 0, :, :], a[:, 1, :, :], ALU.add), reads=[r_a], writes=[r_sm])
                S.op("dve", lambda e, sm=sm, a=a: e.tensor_tensor(sm[:], sm[:], a[:, 2, :, :], ALU.add), reads=[r_a, r_sm], writes=[r_sm])
                S.op("dve", lambda e, sm=sm, rc=rc: e.reciprocal(rc[:], sm[:, :, 128]), reads=[r_sm], writes=[r_rc])
                for h in range(8):
                    eng = "dve" if h % 2 else "pool"
                    S.op(eng, lambda e, o=o, sm=sm, rc=rc, h=h: e.tensor_scalar(o[:, h, :], sm[:, h, 0:128], rc[:, h:h + 1], None, ALU.mult),
                         reads=[r_sm, r_rc], writes=[r_o])
                S.dma("act", self.mixed[t * 128:(t + 1) * 128, 0:1024], o[:].rearrange("p h d -> p (h d)"), reads=[r_o])
            return self.phase_end("comb_a")

    def outproj_phase(self, h_src, wb_out, h_dst, wname=None):
        S = self.S
        wres = self.wres[wname]
        with ExitStack() as st:
            bufs = self.load_T_bufs(st, False, BF16)
            (mT, r_mT), = self.sb(st, "mT", [128, 16, 2048], BF16)
            wt_ring = self.sb(st, "wto", [128, 16, 512], BF16, 2)
            po_ring = self.ps(st, "poo", [128, 512], F32, 4)
            hr_ring = self.sb(st, "hr", [128, 512], F32, 3)
            n_w = 0
            n_p = 0
            for c in range(2):
                tok0 = c * 2048
                self.load_T(st, self.mixed, tok0, 2048, mT, r_mT, 0, bufs=bufs)
                for s in range(4):
                    wt, r_wt = wt_ring[n_w % 2]
                    n_w += 1
                    S.dma("sp", wt[:], wb_out[s], reads=wres[s], writes=[r_wt])
                    for t in range(16):
                        po, r_po = po_ring[n_p % 4]
                        hr, r_hr = hr_ring[n_p % 3]
                        n_p += 1
                        rows = slice(tok0 + t * 128, tok0 + (t + 1) * 128)
                        S.dma("sp", hr[:], h_src[rows, s * 512:(s + 1) * 512], writes=[r_hr])

                        def mm(e, po=po, wt=wt, t=t):
                            for k in range(16):
                                i = e.matmul(po[:], mT[:, k, t * 128:(t + 1) * 128], wt[:, k, :], start=(k == 0), stop=(k == 15))
                            return i
                        S.op("pe", mm, reads=[r_wt, r_mT], writes=[r_po])
                        S.op("dve", lambda e, hr=hr, po=po: e.tensor_tensor(hr[:], hr[:], po[:], ALU.add), reads=[r_po, r_hr], writes=[r_hr])
                        S.dma("act", h_dst[rows, s * 512:(s + 1) * 512], hr[:], reads=[r_hr])
            return self.phase_end("outproj")

    def ffn_phase(self, h_src, gain_ap, wb_up, wb_dn, conv_w, conv_b, h_dst, wn_up=None, wn_dn=None):
        S = self.S
        NFC = DFF // 128
        wres_up = self.wres[wn_up]
        wres_dn = self.wres[wn_dn]
        with ExitStack() as st:
            bufs = self.load_T_bufs(st, True, F32, npT=1)
            gb, r_gb = self.load_gain(st, gain_ap)
            yT_ring = self.sb(st, "y2T", [128, 16, 514], BF16, 1)
            guT_ring = self.sb(st, "guT", [128, NFC, 512], BF16, 1)
            wg_ring = self.sb(st, "wg", [128, 16, 256], BF16, 2)
            wu_ring = self.sb(st, "wu", [128, 16, 256], BF16, 2)
            wd_ring = self.sb(st, "wd", [128, NFC, 256], BF16, 2)
            (cw, r_cw), = self.sb(st, "cw", [128, 4, NFC], F32)
            for i in range(3):
                S.dma("sp", cw[:, i, :], conv_w[i].rearrange("(c p) -> p c", p=128), writes=[r_cw], allow_slow_non_contiguous=True)
            S.dma("sp", cw[:, 3, :], conv_b.rearrange("(c p) -> p c", p=128), writes=[r_cw], allow_slow_non_contiguous=True)
            gs_ring = self.sb(st, "gs", [128, 516], F32, 2)
            t1_ring = self.sb(st, "t1", [128, 512], F32, 2)
            t2_ring = self.sb(st, "t2", [128, 512], F32, 2)
            gl_ring = self.sb(st, "gl", [128, 512], F32, 2)
            hr_ring = self.sb(st, "hrf", [128, 256], F32, 3)
            psg_ring = self.ps(st, "psg", [128, 512], F32, 2)
            psu_ring = self.ps(st, "psu", [128, 512], F32, 2)
            psh_ring = self.ps(st, "psh", [128, 2], F32, 1)
            psd_ring = self.ps(st, "psd", [128, 256], F32, 2)
            n = {"w": 0, "f": 0, "wd": 0, "d": 0}
            for c in range(S_LEN // 512):
                c0 = c * 512
                yT, r_yT = yT_ring[0]
                guT, r_guT = guT_ring[0]
                self.load_T(st, h_src, c0, 512, yT, r_yT, 0, gain_b=gb, r_gain=r_gb, bufs=bufs)
                self.load_T(st, h_src, c0 - 1, 1, yT, r_yT, 512, gain_b=gb, r_gain=r_gb, bufs=bufs)
                self.load_T(st, h_src, c0 + 512, 1, yT, r_yT, 513, gain_b=gb, r_gain=r_gb, bufs=bufs)
                for sl in range(22):
                    wg, r_wg = wg_ring[n["w"] % 2]
                    wu, r_wu = wu_ring[n["w"] % 2]
                    n["w"] += 1
                    S.dma("sp", wg[:], wb_up[sl], reads=wres_up[sl], writes=[r_wg])
                    S.dma("sp", wu[:], wb_up[22 + sl], reads=wres_up[22 + sl], writes=[r_wu])
                    for j in range(2):
                        fc = sl * 2 + j
                        psg, r_psg = psg_ring[n["f"] % 2]
                        psu, r_psu = psu_ring[n["f"] % 2]
                        psh, r_psh = psh_ring[0]
                        gs, r_gs = gs_ring[n["f"] % 2]
                        t1, r_t1 = t1_ring[n["f"] % 2]
                        t2, r_t2 = t2_ring[n["f"] % 2]
                        gl, r_gl = gl_ring[n["f"] % 2]
                        n["f"] += 1

                        def mmg(e, psg=psg, psh=psh, wg=wg, j=j):
                            for k in range(16):
                                e.matmul(psg[:], wg[:, k, j * 128:(j + 1) * 128], yT[:, k, 0:512], start=(k == 0), stop=(k == 15))
                            for k in range(16):
                                i = e.matmul(psh[:], wg[:, k, j * 128:(j + 1) * 128], yT[:, k, 512:514], start=(k == 0), stop=(k == 15))
                            return i
                        S.op("pe", mmg, reads=[r_wg, r_yT], writes=[r_psg, r_psh])

                        def mmu(e, psu=psu, wu=wu, j=j):
                            for k in range(16):
                                i = e.matmul(psu[:], wu[:, k, j * 128:(j + 1) * 128], yT[:, k, 0:512], start=(k == 0), stop=(k == 15))
                            return i
                        S.op("pe", mmu, reads=[r_wu, r_yT], writes=[r_psu])
                        S.op("act", lambda e, gs=gs, psg=psg: e.activation(gs[:, 1:513], psg[:], AF.Copy), reads=[r_psg], writes=[r_gs])

                        def halo(e, gs=gs, psh=psh):
                            e.tensor_copy(gs[:, 0:1], psh[:, 0:1])
                            return e.tensor_copy(gs[:, 513:514], psh[:, 1:2])
                        S.op("dve", halo, reads=[r_psh], writes=[r_gs])
                        S.op("act", lambda e, t1=t1, gs=gs, fc=fc: e.activation(t1[:], gs[:, 1:513], AF.Identity, bias=cw[:, 3, fc:fc + 1], scale=cw[:, 1, fc:fc + 1]),
                             reads=[r_gs, r_cw], writes=[r_t1])
                        S.op("dve", lambda e, t2=t2, gs=gs, t1=t1, fc=fc: e.scalar_tensor_tensor(t2[:], gs[:, 0:512], cw[:, 0, fc:fc + 1], t1[:], ALU.mult, ALU.add),
                             reads=[r_gs, r_t1, r_cw], writes=[r_t2])
                        S.op("dve", lambda e, t1=t1, gs=gs, t2=t2, fc=fc: e.scalar_tensor_tensor(t1[:], gs[:, 2:514], cw[:, 2, fc:fc + 1], t2[:], ALU.mult, ALU.add),
                             reads=[r_gs, r_t2, r_cw], writes=[r_t1])
                        S.op("act", lambda e, gl=gl, t1=t1: e.activation(gl[:], t1[:], AF.Gelu_apprx_tanh), reads=[r_t1], writes=[r_gl])
                        S.op("dve", lambda e, gl=gl, psu=psu, fc=fc: e.tensor_tensor(guT[:, fc, :], gl[:], psu[:], ALU.mult), reads=[r_gl, r_psu], writes=[r_guT])
                for ds in range(8):
                    wd, r_wd = wd_ring[n["wd"] % 2]
                    n["wd"] += 1
                    S.dma("sp", wd[:, 0:22, :], wb_dn[ds, :, 0:22, :], reads=wres_dn[ds], writes=[r_wd])
                    S.dma("sp", wd[:, 22:44, :], wb_dn[ds, :, 22:44, :], reads=wres_dn[ds], writes=[r_wd])
                    for tt in range(4):
                        psd, r_psd = psd_ring[n["d"] % 2]
                        hr, r_hr = hr_ring[n["d"] % 3]
                        n["d"] += 1
                        rows = slice(c0 + tt * 128, c0 + (tt + 1) * 128)
                        S.dma("sp", hr[:], h_src[rows, ds * 256:(ds + 1) * 256], writes=[r_hr])

                        def mmd(e, psd=psd, wd=wd, tt=tt):
                            for k in range(NFC):
                                i = e.matmul(psd[:], guT[:, k, tt * 128:(tt + 1) * 128], wd[:, k, :], start=(k == 0), stop=(k == NFC - 1))
                            return i
                        S.op("pe", mmd, reads=[r_wd, r_guT], writes=[r_psd])
                        S.op("dve", lambda e, hr=hr, psd=psd: e.tensor_tensor(hr[:], hr[:], psd[:], ALU.add), reads=[r_psd, r_hr], writes=[r_hr])
                        S.dma("act", h_dst[rows, ds * 256:(ds + 1) * 256], hr[:], reads=[r_hr])
            return self.phase_end("ffn")

    def final_norm(self, h_src, gain_ap, out):
        S = self.S
        with ExitStack() as st:
            gb, r_gb = self.load_gain(st, gain_ap)
            ht_ring = self.sb(st, "fht", [128, D], F32, 3)
            junk = self.sb(st, "fjk", [128, D], BF16, 1)[0]
            ssr = self.sb(st, "fss", [128, 1], F32, 4)
            for t in range(NT):
                ht, r_ht = ht_ring[t % 3]
                ss, r_ss = ssr[t % 4]
                jk, r_jk = junk
                S.dma("sp", ht[:], h_src[t * 128:(t + 1) * 128, :], writes=[r_ht])
                S.op("act", lambda e, ht=ht, jk=jk, ss=ss: e.activation(jk[:], ht[:], AF.Square, scale=D ** -0.5, accum_out=ss[:]), reads=[r_ht], writes=[r_jk, r_ss])
                S.op("dve", lambda e, ss=ss: e.tensor_scalar(ss[:], ss[:], EPS, None, ALU.add), reads=[r_ss], writes=[r_ss])
                S.op("act", lambda e, ss=ss: e.activation(ss[:], ss[:], AF.Sqrt), reads=[r_ss], writes=[r_ss])
                S.op("dve", lambda e, ss=ss: e.reciprocal(ss[:], ss[:]), reads=[r_ss], writes=[r_ss])
                S.op("dve", lambda e, ht=ht, ss=ss: e.scalar_tensor_tensor(ht[:], ht[:], ss[:, 0:1], gb[:], ALU.mult, ALU.mult), reads=[r_ht, r_ss, r_gb], writes=[r_ht])
                S.dma("act", out[t * 128:(t + 1) * 128, :], ht[:], reads=[r_ht])
            return self.phase_end("final")


def _t5_bucket_np(rel):
    nb = 16
    max_exact = 8
    rel = np.asarray(rel, dtype=np.int64)
    ret = np.where(rel > 0, nb, 0)
    n = np.abs(rel)
    n_f = np.maximum(n, 1).astype(np.float32)
    large = max_exact + (np.log(n_f / np.float32(max_exact)) / np.float32(math.log(1024 / max_exact)) * np.float32(nb - max_exact)).astype(np.int32)
    large = np.minimum(large, nb - 1)
    return ret + np.where(n < max_exact, n, large)


def host_tables(t5_table, na_rpb):
    t5 = np.asarray(t5_table, dtype=np.float32)
    out = {}
    i = np.arange(128)[:, None]
    c = np.arange(2176)[None, :]
    bk = _t5_bucket_np(i - c + 1024)
    out["bandB"] = np.ascontiguousarray(np.stack([t5[bk, 24 + h] for h in range(4)]).astype(np.float32))
    cb = np.zeros((128, 8), np.float32)
    for h in range(4):
        cb[:, 2 * h] = t5[15, 24 + h]
        cb[:, 2 * h + 1] = t5[31, 24 + h]
    out["constB"] = cb
    p = np.arange(128)[:, None, None]
    t = np.arange(2)[None, :, None]
    j = np.arange(128)[None, None, :]
    rel_sub = (p - 64 + 128 * t) - j
    valid = np.abs(rel_sub) <= 64
    mba = np.zeros((24, 128, 2, 128), np.float32)
    for g, r in enumerate((1, 4, 16)):
        bk = _t5_bucket_np(r * rel_sub)
        for h in range(8):
            mba[g * 8 + h] = np.where(valid, t5[bk, g * 8 + h], np.float32(NEG))
    out["mbA"] = mba
    rpb = np.asarray(na_rpb, dtype=np.float32)[0]
    e = (np.arange(128) // 64)[:, None, None]
    bp = (np.arange(128) % 64)[:, None, None]
    a = np.arange(4)[None, :, None]
    jj = np.arange(64)[None, None, :]
    cs = np.clip(jj - 8, 0, 48)
    validc = (bp >= cs) & (bp < cs + 16)
    dc = np.clip(bp - jj + 15, 0, 30)
    mbd = np.zeros((8, 8, 128, 4, 64), np.float32)
    for di in range(8):
        delta = -di
        dr = delta + 2 * a + e + 7
        drc = np.clip(dr, 0, 14) + 0 * jj
        for h in range(8):
            vals = rpb[h][drc, dc + 0 * a]
            mbd[h, di] = np.where(validc & (dr >= 0) & (dr <= 14), vals, np.float32(NEG))
    out["mbD"] = mbd
    tpos = np.arange(S_LEN)
    row = (tpos // 64).astype(np.float32)
    col = (tpos % 64).astype(np.float32)
    inv_freq = (np.float32(10000.0) ** (-(np.arange(0, 64, 2, dtype=np.float32) / np.float32(64)))).astype(np.float32)
    ang = np.concatenate([row[:, None] * inv_freq[None], col[:, None] * inv_freq[None]], axis=-1).astype(np.float32)
    out["ropec"] = np.cos(ang).astype(np.float32)
    out["ropes"] = np.sin(ang).astype(np.float32)
    return out


def _attn_D(self, v1d, mbd_d):
    S = self.S
    with ExitStack() as st:
        qn_ring = self.sb(st, "qd", [128, S_LEN], BF16, 2)
        kn_ring = self.sb(st, "kd", [128, S_LEN], BF16, 2)
        v0_ring = self.sb(st, "vd0", [128, 32, 129], BF16, 2)
        v1_ring = self.sb(st, "vd1", [128, 31, 129], BF16, 2)
        mb_ring = self.sb(st, "mbd", [128, 8, 4, 64], F32, 2)
        tmp_ring = self.sb(st, "tmpd", [128, 4, 64], F32, 3)
        pt_ring = self.sb(st, "ptd", [128, 4, 64], BF16, 3)
        rec_ring = self.sb(st, "recd", [64, 1], F32, 4)
        ost_ring = self.sb(st, "ostd", [64, 64, 128], BF16, 2)
        pss_ring = self.ps(st, "pssd", [128, 4, 64], F32, 3)
        acc_ring = self.ps(st, "accd", [64, 129], F32, 3)
        n_b = 0
        cntD = {"pss": 0}
        for h in range(8):
            qn, r_qn = qn_ring[h % 2]
            kn, r_kn = kn_ring[h % 2]
            v0, r_v0 = v0_ring[h % 2]
            v1, r_v1 = v1_ring[h % 2]
            mb, r_mb = mb_ring[h % 2]
            ost, r_ost = ost_ring[h % 2]
            S.dma("sp", qn[:], self.qt[10 + h], writes=[r_qn])
            S.dma("sp", kn[:], self.qt[18 + h], writes=[r_kn])
            S.dma("sp", mb[:], mbd_d[h].rearrange("d p a j -> p d a j"), writes=[r_mb])
            for part in range(4):
                S.dma("pool", v0[:, part * 8:(part + 1) * 8, :], v1d[part * 1024:(part + 1) * 1024, h, :].rearrange("(t p) c -> p t c", p=128), writes=[r_v0])
            S.dma("pool", v1[:, 0:16, :], v1d[64:64 + 2048, h, :].rearrange("(t p) c -> p t c", p=128), writes=[r_v1])
            S.dma("pool", v1[:, 16:31, :], v1d[64 + 2048:64 + 2048 + 1920, h, :].rearrange("(t p) c -> p t c", p=128), writes=[r_v1])
            pss_of = {}

            def issue_qk(i, kn=kn, qn=qn, r_kn=r_kn, r_qn=r_qn, pss_of=pss_of):
                rs = min(max(i - 4, 0), 56)
                pss, r_pss = pss_ring[cntD["pss"] % 3]
                cntD["pss"] += 1
                pss_of[i] = (pss, r_pss)

                def qk(e, pss=pss, rs=rs, i=i):
                    for a in range(4):
                        k0 = 64 * rs + 128 * a
                        ins = e.matmul(pss[:, a, :], kn[:, k0:k0 + 128], qn[:, 64 * i:64 * i + 64], start=True, stop=True)
                    return ins
                S.op("pe", qk, reads=[r_kn, r_qn], writes=[r_pss])
            issue_qk(0)
            issue_qk(1)
            for i in range(64):
                rs = min(max(i - 4, 0), 56)
                di = i - rs
                pss, r_pss = pss_of.pop(i)
                tmp, r_tmp = tmp_ring[n_b % 3]
                pt, r_pt = pt_ring[n_b % 3]
                acc, r_acc = acc_ring[n_b % 3]
                rec, r_rec = rec_ring[n_b % 4]
                n_b += 1
                S.op("dve", lambda e, tmp=tmp, pss=pss, mb=mb, di=di: e.scalar_tensor_tensor(tmp[:], pss[:], SCALE, mb[:, di, :, :], ALU.mult, ALU.add),
                     reads=[r_pss, r_mb], writes=[r_tmp])
                S.op("act", lambda e, pt=pt, tmp=tmp: e.activation(pt[:], tmp[:], AF.Exp), reads=[r_tmp], writes=[r_pt])
                if i + 2 < 64:
                    issue_qk(i + 2)
                if rs % 2 == 0:
                    vv, r_vv, tb = v0, r_v0, rs // 2
                else:
                    vv, r_vv, tb = v1, r_v1, (rs - 1) // 2

                def pv(e, acc=acc, pt=pt, vv=vv, tb=tb):
                    for a in range(4):
                        ins = e.matmul(acc[:], pt[:, a, :], vv[:, tb + a, :], start=(a == 0), stop=(a == 3))
                    return ins
                S.op("pe", pv, reads=[r_pt, r_vv], writes=[r_acc])
                S.op("dve", lambda e, rec=rec, acc=acc: e.reciprocal(rec[:], acc[:, 128:129]), reads=[r_acc], writes=[r_rec])
                S.op("dve", lambda e, ost=ost, acc=acc, rec=rec, i=i: e.tensor_scalar(ost[:, i, :], acc[:, 0:128], rec[:, 0:1], None, ALU.mult),
                     reads=[r_acc, r_rec], writes=[r_ost])
            S.dma("pool", self.mixed[:, 1024 + h * 128:1024 + (h + 1) * 128].rearrange("(i p) c -> p i c", p=64), ost[:], reads=[r_ost])
        return self.phase_end("attn_d")


KB.attn_D = _attn_D


def build(debug=(), stop_after=None, l1_only=False):
    kb = KB(debug, stop_after)
    nc = kb.nc
    S = kb.S
    I = {}
    I["x"] = kb.inp("x", [S_LEN, D])
    I["ln_mix"] = kb.inp("ln_mix", [2, D])
    I["ln_ffn"] = kb.inp("ln_ffn", [2, D])
    I["ln_final"] = kb.inp("ln_final", [D])
    I["ev_w_in"] = kb.inp("ev_w_in", [D, 12288])
    I["ev_w_out"] = kb.inp("ev_w_out", [D, D])
    for nme in ("diff_lq1", "diff_lk1", "diff_lq2", "diff_lk2"):
        I[nme] = kb.inp(nme, [128])
    I["diff_subln"] = kb.inp("diff_subln", [256])
    I["od_w_in"] = kb.inp("od_w_in", [D, 4608])
    I["od_w_out"] = kb.inp("od_w_out", [D, D])
    I["gqa_q_norm"] = kb.inp("gqa_q_norm", [128])
    I["gqa_k_norm"] = kb.inp("gqa_k_norm", [128])
    I["ffn_w_up"] = kb.inp("ffn_w_up", [2, D, 2 * DFF])
    I["ffn_conv_w"] = kb.inp("ffn_conv_w", [2, 3, DFF])
    I["ffn_conv_b"] = kb.inp("ffn_conv_b", [2, DFF])
    I["ffn_w_down"] = kb.inp("ffn_w_down", [2, DFF, D])
    I["bandB"] = kb.inp("bandB", [4, 128, 2176])
    I["constB"] = kb.inp("constB", [128, 8])
    I["mbA"] = kb.inp("mbA", [24, 128, 2, 128])
    I["mbD"] = kb.inp("mbD", [8, 8, 128, 4, 64])
    I["ropec"] = kb.inp("ropec", [S_LEN, 64])
    I["ropes"] = kb.inp("ropes", [S_LEN, 64])
    out = nc.dram_tensor("out", [S_LEN, D], F32, kind="ExternalOutput").ap()

    kb.qt = kb.dram("qt", [64, 128, S_LEN], BF16)
    kb.mixed = kb.dram("mixed", [S_LEN, D], BF16)
    v1a = kb.dram("v1a", [3, S_LEN, 8, 129], BF16)
    v1b = kb.dram("v1b", [S_LEN, 4, 257], BF16)
    v1c = kb.dram("v1c", [S_LEN, 2, 129], BF16)
    v1d = kb.dram("v1d", [S_LEN, 8, 129], BF16)
    acca = kb.dram("acca", [3, S_LEN, 8, 129], F32)
    hA = kb.dram("hA", [S_LEN, D], F32)
    hB = kb.dram("hB", [S_LEN, D], F32)

    def done():
        kb.es.close()
        return nc

    kb.setup_consts()
    if l1_only:
        hB = kb.inp("hB_in", [S_LEN, D])
    if not l1_only:
        wb_in0 = kb.precast(I["ev_w_in"], "wb_in0", D, 12288, 512)
        wb_out0 = kb.precast(I["ev_w_out"], "wb_out0", D, D, 512)
        wb_up0 = kb.precast(I["ffn_w_up"][0], "wb_up0", D, 2 * DFF, 256)
        wb_dn0 = kb.precast(I["ffn_w_down"][0], "wb_dn0", DFF, D, 256, kstep=11)
    wb_in1 = kb.precast(I["od_w_in"], "wb_in1", D, 4608, 512)
    wb_out1 = kb.precast(I["od_w_out"], "wb_out1", D, D, 512)
    wb_up1 = kb.precast(I["ffn_w_up"][1], "wb_up1", D, 2 * DFF, 256)
    wb_dn1 = kb.precast(I["ffn_w_down"][1], "wb_dn1", DFF, D, 256, kstep=11)
    if kb.stop_after == "precast":
        kb.phase_end("precast")
        return done()

    if l1_only:
        return _layer1(kb, I, hA, hB, v1c, v1d, wb_in1, wb_out1, wb_up1, wb_dn1, out, done)
    slabs0 = []
    for g in range(3):
        slabs0.append(("fm", [g * 16 + h for h in range(0, 4)]))
        slabs0.append(("fm", [g * 16 + h for h in range(4, 8)]))
        slabs0.append(("fm", [g * 16 + 8 + h for h in range(0, 4)]))
        slabs0.append(("fm", [g * 16 + 8 + h for h in range(4, 8)]))
        slabs0.append(("tmv", v1a[g], 0, 4, 128))
        slabs0.append(("tmv", v1a[g], 4, 4, 128))
    slabs0.append(("fm", [48 + i for i in range(0, 4)]))
    slabs0.append(("fm", [48 + i for i in range(4, 8)]))
    slabs0.append(("fm", [56 + i for i in range(0, 4)]))
    slabs0.append(("fm", [56 + i for i in range(4, 8)]))
    slabs0.append(("tmv", v1b, 0, 2, 256))
    slabs0.append(("tmv", v1b, 2, 2, 256))
    if kb.proj_phase(I["x"], I["ln_mix"][0], wb_in0, slabs0, wname="wb_in0"):
        return done()
    if kb.attn_A(v1a, I["mbA"], acca):
        return done()
    lambda_init0 = 0.8 - 0.6 * math.exp(-0.3 * 0)
    if kb.attn_B(v1b, I["bandB"], I["constB"], I["diff_lq1"], I["diff_lk1"], I["diff_lq2"], I["diff_lk2"], I["diff_subln"], lambda_init0):
        return done()
    if kb.outproj_phase(I["x"], wb_out0, hA, wname="wb_out0"):
        return done()
    if kb.ffn_phase(hA, I["ln_ffn"][0], wb_up0, wb_dn0, I["ffn_conv_w"][0], I["ffn_conv_b"][0], hB, wn_up="wb_up0", wn_dn="wb_dn0"):
        return done()
    return _layer1(kb, I, hA, hB, v1c, v1d, wb_in1, wb_out1, wb_up1, wb_dn1, out, done)


def _layer1(kb, I, hA, hB, v1c, v1d, wb_in1, wb_out1, wb_up1, wb_dn1, out, done):
    slabs1 = [
        ("tmc", [("q", 0), ("q", 1), ("q", 2), ("q", 3)]),
        ("tmc", [("q", 4), ("q", 5), ("q", 6), ("q", 7)]),
        ("tmc", [("k", 8), ("k", 9), ("v", v1c, 0), ("v", v1c, 1)]),
        ("fm", [10, 11, 12, 13]), ("fm", [14, 15, 16, 17]),
        ("fm", [18, 19, 20, 21]), ("fm", [22, 23, 24, 25]),
        ("tmv", v1d, 0, 4, 128), ("tmv", v1d, 4, 4, 128),
    ]
    if kb.proj_phase(hB, I["ln_mix"][1], wb_in1, slabs1, rope=(I["ropec"], I["ropes"], I["gqa_q_norm"], I["gqa_k_norm"]), wname="wb_in1"):
        return done()
    if kb.attn_C(v1c):
        return done()
    if kb.attn_D(v1d, I["mbD"]):
        return done()
    if kb.outproj_phase(hB, wb_out1, hA, wname="wb_out1"):
        return done()
    if kb.ffn_phase(hA, I["ln_ffn"][1], wb_up1, wb_dn1, I["ffn_conv_w"][1], I["ffn_conv_b"][1], hB, wn_up="wb_up1", wn_dn="wb_dn1"):
        return done()
    kb.final_norm(hB, I["ln_final"], out)
    return done()


ACTIVE = [0, 1, 4, 5]


def make_in_maps(inputs, ncores=8):
    aux = host_tables(inputs["t5_table"], inputs["na_rpb"])
    f = lambda a: np.ascontiguousarray(np.asarray(a, dtype=np.float32))
    shared = {
        "ln_mix": f(inputs["ln_mix"]), "ln_ffn": f(inputs["ln_ffn"]), "ln_final": f(inputs["ln_final"]),
        "ev_w_in": f(inputs["ev_w_in"][0]), "ev_w_out": f(inputs["ev_w_out"][0]),
        "diff_lq1": f(inputs["diff_lq1"][0]), "diff_lk1": f(inputs["diff_lk1"][0]),
        "diff_lq2": f(inputs["diff_lq2"][0]), "diff_lk2": f(inputs["diff_lk2"][0]),
        "diff_subln": f(inputs["diff_subln"][0]),
        "od_w_in": f(inputs["od_w_in"][0]), "od_w_out": f(inputs["od_w_out"][0]),
        "gqa_q_norm": f(inputs["gqa_q_norm"][0]), "gqa_k_norm": f(inputs["gqa_k_norm"][0]),
        "ffn_w_up": f(inputs["ffn_w_up"]), "ffn_conv_w": f(inputs["ffn_conv_w"]), "ffn_conv_b": f(inputs["ffn_conv_b"]),
        "ffn_w_down": f(inputs["ffn_w_down"]),
    }
    shared.update(aux)
    x = np.asarray(inputs["x"], dtype=np.float32)
    maps = []
    if ncores == 1:
        m = dict(shared)
        m["x"] = np.ascontiguousarray(x[0])
        return [m]
    zero = {k_: np.zeros_like(v) for k_, v in shared.items()}
    zero["x"] = np.zeros_like(x[0])
    for c in range(ncores):
        if c in ACTIVE:
            m = dict(shared)
            m["x"] = np.ascontiguousarray(x[ACTIVE.index(c)])
        else:
            m = zero
        maps.append(m)
    return maps


def kernel(**inputs):
    nc = build()
    maps = make_in_maps(inputs, 8)
    res = run_bass_kernel_spmd(nc, maps, core_ids=list(range(8)))
    out = np.stack([np.asarray(res.results[c]["out"], dtype=np.float32) for c in ACTIVE], axis=0)
    return out
```

```python
import numpy as np
import concourse.bass as bass
import concourse.mybir as mybir

F32 = mybir.dt.float32
BF16 = mybir.dt.bfloat16
AF = mybir.ActivationFunctionType
ALU = mybir.AluOpType
AX = mybir.AxisListType

ENGS = ("pe", "act", "dve", "pool", "sp")
N_DMA_SEMS = 24
N_BG_SEMS = 12
SYNC_SAME_ENGINE = True


class Res:
    __slots__ = ("lw", "rd", "name", "excl")

    def __init__(self, name="", excl=False):
        self.lw = None
        self.rd = []
        self.name = name
        self.excl = excl


class Op:
    __slots__ = ("eng", "fn", "deps", "signaled", "count", "is_dma", "dsem", "dtarget", "prev_same_sem", "idx", "epoch", "bg")


class Sched:
    def __init__(self, nc, es):
        self.nc = nc
        self.ops = {e: [] for e in ENGS}
        self.emitted = {e: 0 for e in ENGS}
        self.sig_count = {e: 0 for e in ENGS}
        self.esem = {e: es.enter_context(nc.semaphore("s_" + e)) for e in ENGS if e != "sp"}
        self.dsems = [es.enter_context(nc.semaphore("d%d" % i)) for i in range(N_DMA_SEMS)]
        self.dsem_count = [0] * N_DMA_SEMS
        self.dsem_last = [None] * N_DMA_SEMS
        self.dma_rr = 0
        self.bsems = [es.enter_context(nc.semaphore("b%d" % i)) for i in range(N_BG_SEMS)]
        self.bsem_count = [0] * N_BG_SEMS
        self.bsem_last = [None] * N_BG_SEMS
        self.bg_rr = 0
        self.waited = {e: {} for e in ENGS}
        self.last_op = {e: None for e in ENGS}
        self.all_dma = []
        self.epoch = 0

    def engine(self, e):
        nc = self.nc
        return {"pe": nc.tensor, "act": nc.scalar, "dve": nc.vector, "pool": nc.gpsimd, "sp": nc.sync}[e]

    def _mk(self, eng, fn, reads, writes, is_dma):
        op = Op()
        op.eng = eng
        op.fn = fn
        op.is_dma = is_dma
        op.signaled = False
        op.count = None
        op.dsem = None
        op.prev_same_sem = None
        op.bg = False
        deps = []
        xr = [r for r in reads if r.excl]
        if xr:
            reads = [r for r in reads if not r.excl]
            writes = list(writes) + [r for r in xr if r not in writes]
        for r in reads:
            if r.lw is not None:
                deps.append(r.lw)
        for w in writes:
            if w.lw is not None:
                deps.append(w.lw)
            deps.extend(w.rd)
        for r in reads:
            r.rd.append(op)
        for w in writes:
            w.lw = op
            w.rd = []
        seen = set()
        ud = []
        for d in deps:
            if id(d) in seen or d is op:
                continue
            seen.add(id(d))
            ud.append(d)
        op.deps = ud
        op.epoch = self.epoch
        op.idx = len(self.ops[eng])
        self.ops[eng].append(op)
        self.last_op[eng] = op
        return op

    def op(self, eng, fn, reads=(), writes=()):
        return self._mk(eng, fn, reads, writes, False)

    def dma(self, eng, out, in_, reads=(), writes=(), bg=False, **kw):
        def fn(e):
            return e.dma_start(out=out, in_=in_, **kw)
        op = self._mk(eng, fn, reads, writes, True)
        if bg:
            op.bg = True
            s = self.bg_rr
            self.bg_rr = (self.bg_rr + 1) % N_BG_SEMS
            op.dsem = s
            self.bsem_count[s] += 16
            op.dtarget = self.bsem_count[s]
            op.prev_same_sem = self.bsem_last[s]
            self.bsem_last[s] = op
            return op
        s = self.dma_rr
        self.dma_rr = (self.dma_rr + 1) % N_DMA_SEMS
        op.dsem = s
        self.dsem_count[s] += 16
        op.dtarget = self.dsem_count[s]
        op.prev_same_sem = self.dsem_last[s]
        self.dsem_last[s] = op
        self.all_dma.append(op)
        return op

    def barrier(self):
        lasts = [self.last_op[e] for e in ENGS if self.last_op[e] is not None and not self.last_op[e].is_dma]
        dl = [d for d in self.dsem_last if d is not None]
        lastc = []
        for e in ENGS:
            for o in reversed(self.ops[e]):
                if not o.is_dma and o.fn is not None:
                    lastc.append(o)
                    break
        for e in ENGS:
            op = Op()
            op.eng = e
            op.fn = None
            op.is_dma = False
            op.signaled = False
            op.count = None
            op.dsem = None
            op.prev_same_sem = None
            op.bg = False
            op.deps = [d for d in (lastc + dl)]
            op.epoch = self.epoch
            op.idx = len(self.ops[e])
            self.ops[e].append(op)
        self.epoch += 1

    def flush(self):
        nc = self.nc
        self.barrier()
        for e in ENGS:
            for o in self.ops[e][self.emitted[e]:]:
                for d in o.deps:
                    if d.epoch < o.epoch and not d.bg:
                        continue
                    if not d.is_dma:
                        if d.eng == o.eng and (d.eng == "pe" or not SYNC_SAME_ENGINE):
                            continue
                        d.signaled = True
        for e in ENGS:
            for o in self.ops[e][self.emitted[e]:]:
                if o.signaled and o.count is None and not o.is_dma:
                    self.sig_count[e] += 1
                    o.count = self.sig_count[e]
        sched = self

        def emit_engine(e, engobj):
            waited = sched.waited[e]
            for o in sched.ops[e][sched.emitted[e]:]:
                deps = list(o.deps)
                if o.is_dma and o.prev_same_sem is not None:
                    deps.append(o.prev_same_sem)
                for d in deps:
                    if d.epoch < o.epoch and not d.bg:
                        continue
                    if d.is_dma and d.bg:
                        key = ("b", d.dsem)
                        val = d.dtarget
                        sem = sched.bsems[d.dsem]
                    elif d.is_dma:
                        key = ("d", d.dsem)
                        val = d.dtarget
                        sem = sched.dsems[d.dsem]
                    else:
                        if d.eng == e and (e == "pe" or not SYNC_SAME_ENGINE):
                            continue
                        assert d.count is not None, "dep on unsignaled op"
                        key = ("e", d.eng)
                        val = d.count
                        sem = sched.esem[d.eng]
                    if waited.get(key, 0) >= val:
                        continue
                    waited[key] = val
                    engobj.wait_ge(sem, val)
                if o.fn is None:
                    continue
                ins = o.fn(engobj)
                if o.is_dma and o.bg:
                    ins.then_inc(sched.bsems[o.dsem], 16)
                elif o.is_dma:
                    ins.then_inc(sched.dsems[o.dsem], 16)
                elif o.signaled:
                    ins.then_inc(sched.esem[e], 1)
            sched.emitted[e] = len(sched.ops[e])

        with nc.Block() as block:
            @block.tensor
            def _(eng):
                emit_engine("pe", eng)

            @block.scalar
            def _(eng):
                emit_engine("act", eng)

            @block.vector
            def _(eng):
                emit_engine("dve", eng)

            @block.gpsimd
            def _(eng):
                emit_engine("pool", eng)

            @block.sync
            def _(eng):
                emit_engine("sp", eng)

import math
from contextlib import ExitStack
from concourse.bass_utils import run_bass_kernel_spmd

S_LEN = 4096
D = 2048
DFF = 5632
NT = S_LEN // 128
EPS = 1e-6
NEG = -30000.0
SCALE = 128 ** -0.5
DIL = (1, 4, 16)


class KB:
    def __init__(self, debug=(), stop_after=None):
        self.nc = bass.Bass("TRN2", target_bir_lowering=False)
        self.debug = set(debug)
        self.stop_after = stop_after
        self.es = ExitStack()
        self.S = Sched(self.nc, self.es)
        self.rid = 0
        self.wres = {}

    def dram(self, name, shape, dtype, kind=None):
        if kind is None:
            kind = "ExternalOutput" if name in self.debug else "Internal"
        return self.nc.dram_tensor(name, list(shape), dtype, kind=kind).ap()

    def inp(self, name, shape, dtype=F32):
        return self.nc.dram_tensor(name, list(shape), dtype, kind="ExternalInput").ap()

    def res(self, name=""):
        self.rid += 1
        return Res(name + str(self.rid))

    def sb(self, st, name, shape, dtype, n=1):
        out = []
        for i in range(n):
            t = st.enter_context(self.nc.sbuf_tensor("%s_%d_%d" % (name, i, self.rid), list(shape), dtype))
            self.rid += 1
            out.append((t, self.res(name)))
        return out

    def ps(self, st, name, shape, dtype, n=1):
        out = []
        for i in range(n):
            t = st.enter_context(self.nc.psum_tensor("%s_%d_%d" % (name, i, self.rid), list(shape), dtype))
            self.rid += 1
            r = self.res(name)
            r.excl = True
            out.append((t, r))
        return out

    def phase_end(self, name):
        self.S.flush()
        return self.stop_after == name

    def precast(self, w, name, K, N, sc, kstep=16):
        KC = K // 128
        ns = N // sc
        wb = self.dram(name, [ns, 128, KC, sc], BF16)
        rs = []
        for s in range(ns):
            parts = []
            for k0 in range(0, KC, kstep):
                k1 = min(KC, k0 + kstep)
                r = self.res(name)
                parts.append(r)
                self.S.dma("pool", wb[s, :, k0:k1, :],
                           w[k0 * 128:k1 * 128, s * sc:(s + 1) * sc].rearrange("(k p) n -> p k n", p=128),
                           writes=[r], bg=True)
            rs.append(parts)
        self.wres[name] = rs
        return wb

    def setup_consts(self):
        st = self.es
        S = self.S
        (identf, r_if), = self.sb(st, "identf", [128, 128], F32)
        (ident, r_id), = self.sb(st, "ident", [128, 128], BF16)

        def mk_ident(e):
            e.memset(identf[:], 0.0)
            return e.affine_select(identf[:], identf[:], pattern=[[-1, 128]], compare_op=ALU.not_equal,
                                   fill=1.0, base=0, channel_multiplier=1)
        S.op("pool", mk_ident, writes=[r_if])
        S.op("dve", lambda e: e.tensor_copy(ident[:], identf[:]), reads=[r_if], writes=[r_id])
        self.ident = ident
        self.r_ident = r_id

    def load_T(self, st, src, t0, ntok, xT, r_xT, col0, gain_b=None, r_gain=None, dt_in=F32, bufs=None):
        S = self.S
        ht_ring, yb_ring, junk, ssr, pT_ring = bufs
        ntile = (ntok + 127) // 128
        for ti in range(ntile):
            r0 = t0 + ti * 128
            n = min(128, ntok - ti * 128)
            ht, r_ht = ht_ring[self.cnt_ht % len(ht_ring)]
            self.cnt_ht += 1
            lo = max(r0, 0)
            hi = min(r0 + n, S_LEN)
            if lo > r0 or hi < r0 + n:
                S.op("pool", lambda e, ht=ht, n=n: e.memset(ht[0:n, :], 0.0), writes=[r_ht])
            if hi > lo:
                S.dma("sp", ht[lo - r0:hi - r0, :], src[lo:hi, :], writes=[r_ht])
            if gain_b is not None:
                yb, r_yb = yb_ring[self.cnt_yb % len(yb_ring)]
                self.cnt_yb += 1
                (jk, r_jk) = junk
                (ss, r_ss) = ssr[self.cnt_yb % len(ssr)]
                S.op("act", lambda e, ht=ht, jk=jk, ss=ss, n=n: e.activation(jk[0:n, :], ht[0:n, :], AF.Square, scale=D ** -0.5, accum_out=ss[0:n, :]),
                     reads=[r_ht], writes=[r_jk, r_ss])
                S.op("dve", lambda e, ss=ss, n=n: e.tensor_scalar(ss[0:n, :], ss[0:n, :], EPS, None, ALU.add), reads=[r_ss], writes=[r_ss])
                S.op("act", lambda e, ss=ss, n=n: e.activation(ss[0:n, :], ss[0:n, :], AF.Sqrt), reads=[r_ss], writes=[r_ss])
                S.op("dve", lambda e, ss=ss, n=n: e.reciprocal(ss[0:n, :], ss[0:n, :]), reads=[r_ss], writes=[r_ss])
                S.op("dve", lambda e, yb=yb, ht=ht, ss=ss, n=n: e.scalar_tensor_tensor(yb[0:n, :], ht[0:n, :], ss[0:n, 0:1], gain_b[0:n, :], ALU.mult, ALU.mult),
                     reads=[r_ht, r_ss, r_gain], writes=[r_yb])
                srcT, r_srcT = yb, r_yb
            else:
                srcT, r_srcT = ht, r_ht
            for g in range(2):
                pT, r_pT = pT_ring[self.cnt_pT % len(pT_ring)]
                self.cnt_pT += 1

                def tr(e, pT=pT, srcT=srcT, g=g, n=n):
                    for j in range(8):
                        k = g * 8 + j
                        i = e.transpose(pT[:, j, 0:n], srcT[0:n, k * 128:(k + 1) * 128], self.ident[0:n, 0:n])
                    return i
                S.op("pe", tr, reads=[r_srcT, self.r_ident], writes=[r_pT])
                c0 = col0 + ti * 128
                eng = "act" if (self.cnt_pT % 2) else "dve"
                if eng == "act":
                    S.op("act", lambda e, pT=pT, g=g, c0=c0, n=n: e.activation(xT[:, g * 8:(g + 1) * 8, c0:c0 + n], pT[:, :, 0:n], AF.Copy),
                         reads=[r_pT], writes=[r_xT])
                else:
                    S.op("dve", lambda e, pT=pT, g=g, c0=c0, n=n: e.tensor_copy(xT[:, g * 8:(g + 1) * 8, c0:c0 + n], pT[:, :, 0:n]),
                         reads=[r_pT], writes=[r_xT])

    def load_T_bufs(self, st, norm, dt_in, npT=2):
        ht_ring = self.sb(st, "ht", [128, D], dt_in, 2)
        if norm:
            yb_ring = self.sb(st, "yb", [128, D], BF16, 2)
            junk = self.sb(st, "junk", [128, D], BF16, 1)[0]
            ssr = self.sb(st, "ss", [128, 1], F32, 4)
        else:
            yb_ring = junk = ssr = None
        pT_ring = self.ps(st, "pT", [128, 8, 128], BF16, npT)
        self.cnt_ht = self.cnt_yb = self.cnt_pT = 0
        return (ht_ring, yb_ring, junk, ssr, pT_ring)

    def load_gain(self, st, g_ap, n=D):
        (gb, r_gb), = self.sb(st, "gb", [128, n], F32)
        self.S.dma("sp", gb[:], g_ap.partition_broadcast(128), writes=[r_gb])
        return gb, r_gb

    def proj_phase(self, h_src, gain_ap, wb, slabs, rope=None, wname=None):
        S = self.S
        nc = self.nc
        wres = self.wres[wname]
        with ExitStack() as st:
            bufs = self.load_T_bufs(st, True, F32)
            gb, r_gb = self.load_gain(st, gain_ap)
            (yT, r_yT), = self.sb(st, "yT", [128, 16, 2048], BF16)
            wt_ring = self.sb(st, "wt", [128, 16, 512], BF16, 2)
            po_ring = self.ps(st, "po", [128, 512], F32, 4)
            qst_ring = self.sb(st, "qst", [128, 2048], BF16, 3)
            vst_ring = {}
            cnt = {"wt": 0, "po": 0, "qst": 0, "ev": 0}
            if rope is not None:
                cos_d, sin_d, gq_ap, gk_ap = rope
                (gqb, r_gqb), = self.sb(st, "gqb", [128, 128], F32)
                (gkb, r_gkb), = self.sb(st, "gkb", [128, 128], F32)
                S.dma("sp", gqb[:], gq_ap.partition_broadcast(128), writes=[r_gqb])
                S.dma("sp", gkb[:], gk_ap.partition_broadcast(128), writes=[r_gkb])
                cs_ring = self.sb(st, "cs", [128, 2, 64], F32, 3)
                sq_ring = self.sb(st, "sq", [128, 512], F32, 2)
                ss4_ring = self.sb(st, "ss4", [128, 4], F32, 3)
                xn_ring = self.sb(st, "xn", [128, 4, 128], F32, 2)
                tt_ring = self.sb(st, "tt", [128, 4, 4, 64], F32, 2)
                xr_ring = self.sb(st, "xr", [128, 4, 128], BF16, 2)
                hst = self.sb(st, "hst", [128, 2048], BF16, 8)
                pTc_ring = self.ps(st, "pTc", [128, 4, 128], BF16, 2)
                cnt.update({"cs": 0, "sq": 0, "xn": 0, "hst": 0, "pTc": 0})

            def get_vst(nh, dv):
                key = (nh, dv)
                if key not in vst_ring:
                    ring = self.sb(st, "vst", [128, nh, dv + 1], BF16, 3)
                    for (t, r) in ring:
                        S.op("dve", lambda e, t=t: e.memset(t[:], 1.0), writes=[r])
                    vst_ring[key] = [ring, 0]
                ent = vst_ring[key]
                t, r = ent[0][ent[1] % 3]
                ent[1] += 1
                return t, r

            def evac(out_ap, in_ap, reads, writes):
                cnt["ev"] += 1
                if cnt["ev"] % 2:
                    S.op("act", lambda e: e.activation(out_ap, in_ap, AF.Copy), reads=reads, writes=writes)
                else:
                    S.op("dve", lambda e: e.tensor_copy(out_ap, in_ap), reads=reads, writes=writes)

            for c in range(2):
                tok0 = c * 2048
                self.load_T(st, h_src, tok0, 2048, yT, r_yT, 0, gain_b=gb, r_gain=r_gb, bufs=bufs)
                for si, spec in enumerate(slabs):
                    wt, r_wt = wt_ring[cnt["wt"] % 2]
                    cnt["wt"] += 1
                    S.dma("sp", wt[:], wb[si], reads=wres[si], writes=[r_wt])
                    kind = spec[0]
                    if kind == "fm":
                        for cc in range(4):
                            qid = spec[1][cc]
                            qst, r_qst = qst_ring[cnt["qst"] % 3]
                            cnt["qst"] += 1
                            for tg in range(4):
                                po, r_po = po_ring[cnt["po"] % 4]
                                cnt["po"] += 1

                                def mm(e, po=po, wt=wt, cc=cc, tg=tg):
                                    for k in range(16):
                                        i = e.matmul(po[:], wt[:, k, cc * 128:(cc + 1) * 128], yT[:, k, tg * 512:(tg + 1) * 512],
                                                     start=(k == 0), stop=(k == 15))
                                    return i
                                S.op("pe", mm, reads=[r_wt, r_yT], writes=[r_po])
                                evac(qst[:, tg * 512:(tg + 1) * 512], po[:], [r_po], [r_qst])
                            S.dma("act", self.qt[qid, :, tok0:tok0 + 2048], qst[:], reads=[r_qst])
                    elif kind == "tmv":
                        _, v1, h0, nh, dv = spec
                        for t in range(16):
                            po, r_po = po_ring[cnt["po"] % 4]
                            cnt["po"] += 1

                            def mm(e, po=po, wt=wt, t=t):
                                for k in range(16):
                                    i = e.matmul(po[:], yT[:, k, t * 128:(t + 1) * 128], wt[:, k, :], start=(k == 0), stop=(k == 15))
                                return i
                            S.op("pe", mm, reads=[r_wt, r_yT], writes=[r_po])
                            vst, r_vst = get_vst(nh, dv)
                            evac(vst[:, :, 0:dv], po[:].rearrange("p (h d) -> p h d", h=nh), [r_po], [r_vst])
                            S.dma("act", v1[tok0 + t * 128:tok0 + (t + 1) * 128, h0:h0 + nh, :], vst[:], reads=[r_vst])
                    elif kind == "tmc":
                        heads = spec[1]
                        hsts = []
                        for hh in range(4):
                            if heads[hh][0] in ("q", "k"):
                                hsts.append(hst[cnt["hst"] % 8])
                                cnt["hst"] += 1
                            else:
                                hsts.append(None)
                        nqk = sum(1 for x in heads if x[0] in ("q", "k"))
                        vheads = [x for x in heads if x[0] == "v"]
                        for t in range(16):
                            po, r_po = po_ring[cnt["po"] % 4]
                            cnt["po"] += 1

                            def mm(e, po=po, wt=wt, t=t):
                                for k in range(16):
                                    i = e.matmul(po[:], yT[:, k, t * 128:(t + 1) * 128], wt[:, k, :], start=(k == 0), stop=(k == 15))
                                return i
                            S.op("pe", mm, reads=[r_wt, r_yT], writes=[r_po])
                            if vheads:
                                nv = len(vheads)
                                vst, r_vst = get_vst(nv, 128)
                                evac(vst[:, :, 0:128], po[:, nqk * 128:512].rearrange("p (h d) -> p h d", h=nv), [r_po], [r_vst])
                                v1 = vheads[0][1]
                                S.dma("act", v1[tok0 + t * 128:tok0 + (t + 1) * 128, vheads[0][2]:vheads[0][2] + nv, :], vst[:], reads=[r_vst])
                            W = nqk * 128
                            sq, r_sq = sq_ring[cnt["sq"] % 2]
                            ss4, r_ss4 = ss4_ring[cnt["sq"] % 3]
                            cnt["sq"] += 1
                            S.op("act", lambda e, sq=sq, po=po, W=W: e.activation(sq[:, 0:W], po[:, 0:W], AF.Square, scale=128 ** -0.5),
                                 reads=[r_po], writes=[r_sq])
                            S.op("dve", lambda e, sq=sq, ss4=ss4, nqk=nqk, W=W: e.reduce_sum(ss4[:, 0:nqk], sq[:, 0:W].rearrange("p (h d) -> p h d", h=nqk), AX.X),
                                 reads=[r_sq], writes=[r_ss4])
                            S.op("dve", lambda e, ss4=ss4, nqk=nqk: e.tensor_scalar(ss4[:, 0:nqk], ss4[:, 0:nqk], EPS, None, ALU.add), reads=[r_ss4], writes=[r_ss4])
                            S.op("act", lambda e, ss4=ss4, nqk=nqk: e.activation(ss4[:, 0:nqk], ss4[:, 0:nqk], AF.Sqrt), reads=[r_ss4], writes=[r_ss4])
                            S.op("dve", lambda e, ss4=ss4, nqk=nqk: e.reciprocal(ss4[:, 0:nqk], ss4[:, 0:nqk]), reads=[r_ss4], writes=[r_ss4])
                            xn, r_xn = xn_ring[cnt["xn"] % 2]
                            tt, r_tt = tt_ring[cnt["xn"] % 2]
                            xr, r_xr = xr_ring[cnt["xn"] % 2]
                            cnt["xn"] += 1
                            for hh in range(nqk):
                                gbx, r_gbx = (gqb, r_gqb) if heads[hh][0] == "q" else (gkb, r_gkb)
                                S.op("dve", lambda e, xn=xn, po=po, ss4=ss4, hh=hh, gbx=gbx: e.scalar_tensor_tensor(
                                    xn[:, hh, :], po[:, hh * 128:(hh + 1) * 128], ss4[:, hh:hh + 1], gbx[:], ALU.mult, ALU.mult),
                                    reads=[r_po, r_ss4, r_gbx], writes=[r_xn])
                            cs, r_cs = cs_ring[cnt["cs"] % 3]
                            cnt["cs"] += 1
                            S.dma("sp", cs[:, 0, :], cos_d[tok0 + t * 128:tok0 + (t + 1) * 128, :], writes=[r_cs])
                            S.dma("sp", cs[:, 1, :], sin_d[tok0 + t * 128:tok0 + (t + 1) * 128, :], writes=[r_cs])
                            x0 = xn[:, 0:nqk, 0:128:2]
                            x1 = xn[:, 0:nqk, 1:128:2]
                            cb = cs[:, 0:1, :].broadcast_to([128, nqk, 64])
                            sb_ = cs[:, 1:2, :].broadcast_to([128, nqk, 64])
                            S.op("dve", lambda e, tt=tt, x0=x0, cb=cb, nqk=nqk: e.tensor_tensor(tt[:, 0, 0:nqk, :], x0, cb, ALU.mult), reads=[r_xn, r_cs], writes=[r_tt])
                            S.op("dve", lambda e, tt=tt, x1=x1, sb_=sb_, nqk=nqk: e.tensor_tensor(tt[:, 1, 0:nqk, :], x1, sb_, ALU.mult), reads=[r_xn, r_cs], writes=[r_tt])
                            S.op("dve", lambda e, tt=tt, x0=x0, sb_=sb_, nqk=nqk: e.tensor_tensor(tt[:, 2, 0:nqk, :], x0, sb_, ALU.mult), reads=[r_xn, r_cs], writes=[r_tt])
                            S.op("dve", lambda e, tt=tt, x1=x1, cb=cb, nqk=nqk: e.tensor_tensor(tt[:, 3, 0:nqk, :], x1, cb, ALU.mult), reads=[r_xn, r_cs], writes=[r_tt])
                            S.op("dve", lambda e, tt=tt, xr=xr, nqk=nqk: e.tensor_tensor(xr[:, 0:nqk, 0:128:2], tt[:, 0, 0:nqk, :], tt[:, 1, 0:nqk, :], ALU.subtract), reads=[r_tt], writes=[r_xr])
                            S.op("dve", lambda e, tt=tt, xr=xr, nqk=nqk: e.tensor_tensor(xr[:, 0:nqk, 1:128:2], tt[:, 2, 0:nqk, :], tt[:, 3, 0:nqk, :], ALU.add), reads=[r_tt], writes=[r_xr])
                            pTc, r_pTc = pTc_ring[cnt["pTc"] % 2]
                            cnt["pTc"] += 1

                            def tr(e, pTc=pTc, xr=xr, nqk=nqk):
                                for hh in range(nqk):
                                    i = e.transpose(pTc[:, hh, :], xr[:, hh, :], self.ident[:])
                                return i
                            S.op("pe", tr, reads=[r_xr, self.r_ident], writes=[r_pTc])
                            for hh in range(nqk):
                                evac(hsts[hh][0][:, t * 128:(t + 1) * 128], pTc[:, hh, :], [r_pTc], [hsts[hh][1]])
                        for hh in range(nqk):
                            S.dma("act", self.qt[heads[hh][1], :, tok0:tok0 + 2048], hsts[hh][0][:], reads=[hsts[hh][1]])
            return self.phase_end("proj")

    def attn_full(self, units, dv, bias=None, finish=None, extra_setup=None):
        S = self.S
        with ExitStack() as st:
            vt_ring = self.sb(st, "vt", [128, NT, dv + 1], BF16, 2)
            kt_ring = self.sb(st, "kt", [128, S_LEN], BF16, 2)
            q_ring = self.sb(st, "qsb", [128, 512], BF16, 2)
            pt_ring = self.sb(st, "pt", [128, 512], BF16, 3)
            pss_ring = self.ps(st, "pss", [128, 512], F32, 3)
            acc_ring = self.ps(st, "acc", [128, dv + 1], F32, 4)
            if bias is not None:
                band_d, constb_d = bias
                band_ring = self.sb(st, "band", [128, 2176], F32, 2)
                tmp_ring = self.sb(st, "tmpb", [128, 512], F32, 2)
                nhb = band_d.shape[0]
                (cb, r_cb), = self.sb(st, "constb", [128, nhb * 2], F32)
                S.dma("sp", cb[:], constb_d, writes=[r_cb])
            ctx = extra_setup(st) if extra_setup is not None else None
            cnt = {"vt": 0, "kt": 0, "q": 0, "pt": 0, "pss": 0, "band": 0, "tmp": 0}
            cur_v = None
            cur_band = None
            for u in units:
                vkey = (id(u["v"][0]), u["v"][1])
                if vkey != cur_v:
                    vt, r_vt = vt_ring[cnt["vt"] % 2]
                    cnt["vt"] += 1
                    v1, hidx = u["v"]
                    for part in range(4):
                        S.dma("pool", vt[:, part * 8:(part + 1) * 8, :],
                              v1[part * 1024:(part + 1) * 1024, hidx, :].rearrange("(t p) c -> p t c", p=128), writes=[r_vt])
                    cur_v = vkey
                kt, r_kt = kt_ring[cnt["kt"] % 2]
                cnt["kt"] += 1
                S.dma("sp", kt[:], self.qt[u["kid"]], writes=[r_kt])
                bh = u.get("bias_h")
                if bias is not None and bh != cur_band:
                    band, r_band = band_ring[cnt["band"] % 2]
                    cnt["band"] += 1
                    S.dma("sp", band[:], band_d[bh], writes=[r_band])
                    cur_band = bh
                for qi in range(8):
                    qsb, r_q = q_ring[cnt["q"] % 2]
                    cnt["q"] += 1
                    S.dma("sp", qsb[:], self.qt[u["qid"], :, qi * 512:(qi + 1) * 512], writes=[r_q])
                    pss_of = {}

                    def issue_qk(ki, kt=kt, r_kt=r_kt, qsb=qsb, r_q=r_q):
                        pss, r_pss = pss_ring[cnt["pss"] % 3]
                        cnt["pss"] += 1
                        pss_of[ki] = (pss, r_pss)
                        S.op("pe", lambda e, pss=pss, ki=ki: e.matmul(pss[:], kt[:, ki * 128:(ki + 1) * 128], qsb[:], start=True, stop=True),
                             reads=[r_kt, r_q], writes=[r_pss])
                    issue_qk(0)
                    issue_qk(1)
                    for ki in range(NT):
                        pss, r_pss = pss_of.pop(ki)
                        pt, r_pt = pt_ring[cnt["pt"] % 3]
                        cnt["pt"] += 1
                        if bias is None:
                            S.op("act", lambda e, pt=pt, pss=pss: e.activation(pt[:], pss[:], AF.Exp, scale=SCALE), reads=[r_pss], writes=[r_pt])
                        else:
                            delta = ki * 128 - qi * 512
                            if delta >= 1070:
                                S.op("act", lambda e, pt=pt, pss=pss, bh=bh: e.activation(pt[:], pss[:], AF.Exp, bias=cb[:, 2 * bh + 1:2 * bh + 2], scale=SCALE),
                                     reads=[r_pss, r_cb], writes=[r_pt])
                            elif delta <= -686:
                                S.op("act", lambda e, pt=pt, pss=pss, bh=bh: e.activation(pt[:], pss[:], AF.Exp, bias=cb[:, 2 * bh:2 * bh + 1], scale=SCALE),
                                     reads=[r_pss, r_cb], writes=[r_pt])
                            else:
                                s0 = 1024 - delta
                                tmp, r_tmp = tmp_ring[cnt["tmp"] % 2]
                                cnt["tmp"] += 1
                                S.op("dve", lambda e, tmp=tmp, pss=pss, band=band, s0=s0: e.scalar_tensor_tensor(tmp[:], pss[:], SCALE, band[:, s0:s0 + 512], ALU.mult, ALU.add),
                                     reads=[r_pss, r_band], writes=[r_tmp])
                                S.op("act", lambda e, pt=pt, tmp=tmp: e.activation(pt[:], tmp[:], AF.Exp), reads=[r_tmp], writes=[r_pt])
                        if ki + 2 < NT:
                            issue_qk(ki + 2)
                        for qs in range(4):
                            acc, r_acc = acc_ring[qs]
                            S.op("pe", lambda e, acc=acc, pt=pt, vt=vt, qs=qs, ki=ki: e.matmul(acc[:], pt[:, qs * 128:(qs + 1) * 128], vt[:, ki, :], start=(ki == 0), stop=(ki == NT - 1)),
                                 reads=[r_pt, r_vt], writes=[r_acc])
                    finish(st, ctx, u, qi, acc_ring)
            return self.phase_end("attn_full")

    def attn_B(self, v1b, band_d, constb_d, lq1, lk1, lq2, lk2, subln, lambda_init):
        S = self.S
        units = []
        for h in range(4):
            for m in range(2):
                units.append(dict(qid=48 + h * 2 + m, kid=56 + h * 2 + m, v=(v1b, h), bias_h=h, h=h, m=m))

        def setup(st):
            c = {}
            (lv, r_lv), = self.sb(st, "lv", [128, 4, 128], F32)
            for i, a in enumerate((lq1, lk1, lq2, lk2)):
                S.dma("sp", lv[:, i, :], a.partition_broadcast(128), writes=[r_lv])
            (pr, r_pr), = self.sb(st, "lpr", [128, 2, 128], F32)
            (sv, r_sv), = self.sb(st, "lsv", [128, 2], F32)
            (nlam, r_nlam), = self.sb(st, "nlam", [128, 1], F32)
            S.op("dve", lambda e: e.tensor_tensor(pr[:, 0, :], lv[:, 0, :], lv[:, 1, :], ALU.mult), reads=[r_lv], writes=[r_pr])
            S.op("dve", lambda e: e.tensor_tensor(pr[:, 1, :], lv[:, 2, :], lv[:, 3, :], ALU.mult), reads=[r_lv], writes=[r_pr])
            S.op("dve", lambda e: e.reduce_sum(sv[:], pr[:], AX.X), reads=[r_pr], writes=[r_sv])
            S.op("act", lambda e: e.activation(sv[:], sv[:], AF.Exp), reads=[r_sv], writes=[r_sv])
            S.op("dve", lambda e: e.tensor_tensor(nlam[:], sv[:, 1:2], sv[:, 0:1], ALU.subtract), reads=[r_sv], writes=[r_nlam])
            S.op("dve", lambda e: e.tensor_scalar(nlam[:], nlam[:], -lambda_init, None, ALU.add), reads=[r_nlam], writes=[r_nlam])
            (sg, r_sg), = self.sb(st, "subg", [128, 256], F32)
            S.dma("sp", sg[:], subln.partition_broadcast(128), writes=[r_sg])
            S.op("dve", lambda e: e.tensor_scalar(sg[:], sg[:], 1.0 - lambda_init, None, ALU.mult), reads=[r_sg], writes=[r_sg])
            (o0, r_o0), = self.sb(st, "o0", [128, 32, 256], F32)
            c["nlam"] = (nlam, r_nlam)
            c["sg"] = (sg, r_sg)
            c["o0"] = (o0, r_o0)
            c["rec"] = self.sb(st, "rec", [128, 1], F32, 4)
            c["o1"] = self.sb(st, "o1", [128, 256], F32, 2)
            c["dd"] = self.sb(st, "dd", [128, 256], F32, 2)
            c["jk"] = self.sb(st, "jkb", [128, 256], BF16, 1)[0]
            c["ss"] = self.sb(st, "ssb", [128, 1], F32, 4)
            c["ost"] = self.sb(st, "ostb", [128, 4, 256], BF16, 2)
            c["n"] = 0
            return c

        def finish(st, c, u, qi, acc_ring):
            h, m = u["h"], u["m"]
            o0, r_o0 = c["o0"]
            if m == 1:
                ost, r_ost = c["ost"][qi % 2]
            for qs in range(4):
                acc, r_acc = acc_ring[qs]
                c["n"] += 1
                rec, r_rec = c["rec"][c["n"] % 4]
                S.op("dve", lambda e, rec=rec, acc=acc: e.reciprocal(rec[:], acc[:, 256:257]), reads=[r_acc], writes=[r_rec])
                if m == 0:
                    S.op("dve", lambda e, acc=acc, rec=rec, qi=qi, qs=qs: e.tensor_scalar(o0[:, qi * 4 + qs, :], acc[:, 0:256], rec[:, 0:1], None, ALU.mult),
                         reads=[r_acc, r_rec], writes=[r_o0])
                else:
                    o1, r_o1 = c["o1"][c["n"] % 2]
                    dd, r_dd = c["dd"][c["n"] % 2]
                    ss, r_ss = c["ss"][c["n"] % 4]
                    jk, r_jk = c["jk"]
                    nlam, r_nlam = c["nlam"]
                    sg, r_sg = c["sg"]
                    S.op("dve", lambda e, o1=o1, acc=acc, rec=rec: e.tensor_scalar(o1[:], acc[:, 0:256], rec[:, 0:1], None, ALU.mult), reads=[r_acc, r_rec], writes=[r_o1])
                    S.op("dve", lambda e, dd=dd, o1=o1, qi=qi, qs=qs: e.scalar_tensor_tensor(dd[:], o1[:], nlam[:, 0:1], o0[:, qi * 4 + qs, :], ALU.mult, ALU.add),
                         reads=[r_o1, r_nlam, r_o0], writes=[r_dd])
                    S.op("act", lambda e, jk=jk, dd=dd, ss=ss: e.activation(jk[:], dd[:], AF.Square, scale=1.0 / 16.0, accum_out=ss[:]), reads=[r_dd], writes=[r_jk, r_ss])
                    S.op("dve", lambda e, ss=ss: e.tensor_scalar(ss[:], ss[:], EPS, None, ALU.add), reads=[r_ss], writes=[r_ss])
                    S.op("act", lambda e, ss=ss: e.activation(ss[:], ss[:], AF.Sqrt), reads=[r_ss], writes=[r_ss])
                    S.op("dve", lambda e, ss=ss: e.reciprocal(ss[:], ss[:]), reads=[r_ss], writes=[r_ss])
                    S.op("dve", lambda e, ost=ost, dd=dd, ss=ss, qs=qs: e.scalar_tensor_tensor(ost[:, qs, :], dd[:], ss[:, 0:1], sg[:], ALU.mult, ALU.mult),
                         reads=[r_dd, r_ss, r_sg], writes=[r_ost])
            if m == 1:
                S.dma("pool", self.mixed[qi * 512:(qi + 1) * 512, 1024 + h * 256:1024 + (h + 1) * 256].rearrange("(s p) c -> p s c", p=128),
                      ost[:], reads=[r_ost])

        return self.attn_full(units, 256, bias=(band_d, constb_d), finish=finish, extra_setup=setup)

    def attn_C(self, v1c):
        S = self.S
        units = []
        for n in range(2):
            for g in range(4):
                h = n * 4 + g
                units.append(dict(qid=h, kid=8 + n, v=(v1c, n), bias_h=None, h=h))

        def setup(st):
            c = {}
            c["rec"] = self.sb(st, "rec", [128, 1], F32, 4)
            c["ost"] = self.sb(st, "ostc", [128, 4, 128], BF16, 2)
            c["n"] = 0
            return c

        def finish(st, c, u, qi, acc_ring):
            h = u["h"]
            ost, r_ost = c["ost"][qi % 2]
            for qs in range(4):
                acc, r_acc = acc_ring[qs]
                c["n"] += 1
                rec, r_rec = c["rec"][c["n"] % 4]
                S.op("dve", lambda e, rec=rec, acc=acc: e.reciprocal(rec[:], acc[:, 128:129]), reads=[r_acc], writes=[r_rec])
                S.op("dve", lambda e, acc=acc, rec=rec, ost=ost, qs=qs: e.tensor_scalar(ost[:, qs, :], acc[:, 0:128], rec[:, 0:1], None, ALU.mult),
                     reads=[r_acc, r_rec], writes=[r_ost])
            S.dma("pool", self.mixed[qi * 512:(qi + 1) * 512, h * 128:(h + 1) * 128].rearrange("(s p) c -> p s c", p=128),
                  ost[:], reads=[r_ost])

        return self.attn_full(units, 128, bias=None, finish=finish, extra_setup=setup)

    def attn_A(self, v1a, mba_d, acca):
        S = self.S
        with ExitStack() as st:
            qn_ring = self.sb(st, "qn", [128, S_LEN], BF16, 2)
            kn_ring = self.sb(st, "kn", [128, S_LEN], BF16, 2)
            qp_ring = self.sb(st, "qp", [128, S_LEN], BF16, 2)
            kp_ring = self.sb(st, "kp", [128, 6144], BF16, 2)
            mb_ring = self.sb(st, "mb", [128, 2, 128], F32, 2)
            vt_ring = self.sb(st, "vta", [128, 48, 129], BF16, 2)
            tmp_ring = self.sb(st, "tmpa", [128, 2, 128], F32, 3)
            pt_ring = self.sb(st, "pta", [128, 2, 128], BF16, 3)
            ost_ring = self.sb(st, "osta", [128, 32, 129], F32, 2)
            pss_ring = self.ps(st, "pssa", [128, 2, 128], F32, 3)
            acc_ring = self.ps(st, "acca", [128, 129], F32, 3)
            n_u = 0
            cntA = {"pss": 0, "b": 0, "t": 0}
            for g in range(3):
                r = DIL[g]
                L = S_LEN // r
                nb = L // 128
                for h in range(8):
                    qn, r_qn = qn_ring[n_u % 2]
                    kn, r_kn = kn_ring[n_u % 2]
                    qp, r_qp = qp_ring[n_u % 2]
                    kp, r_kp = kp_ring[n_u % 2]
                    mb, r_mb = mb_ring[n_u % 2]
                    vtt, r_vt = vt_ring[n_u % 2]
                    ost, r_ost = ost_ring[n_u % 2]
                    n_u += 1
                    S.dma("sp", qn[:], self.qt[g * 16 + h], writes=[r_qn])
                    S.dma("sp", kn[:], self.qt[g * 16 + 8 + h], writes=[r_kn])
                    S.dma("sp", mb[:], mba_d[g * 8 + h], writes=[r_mb])
                    qpv = qp[:].rearrange("d (r n) -> d r n", r=r)
                    kpv = kp[:, 0:r * (L + 128)].rearrange("d (r n) -> d r n", r=r)
                    S.op("pool", lambda e, qpv=qpv, qn=qn, r=r: e.tensor_copy(qpv, qn[:].rearrange("d (n r) -> d r n", r=r)), reads=[r_qn], writes=[r_qp])

                    def kperm(e, kpv=kpv, kn=kn, r=r, L=L):
                        e.memset(kpv[:, :, 0:64], 0.0)
                        e.memset(kpv[:, :, 64 + L:128 + L], 0.0)
                        return e.tensor_copy(kpv[:, :, 64:64 + L], kn[:].rearrange("d (n r) -> d r n", r=r))
                    S.op("pool", kperm, reads=[r_kn], writes=[r_kp])
                    vsrc = v1a[g, :, h, :].rearrange("(n r) c -> r n c", r=r)
                    vt = vtt[:, 0:r * (nb + 1), :].rearrange("p (r i) c -> p r i c", r=r)

                    def vz(e, vt=vt, nb=nb):
                        e.memset(vt[0:64, :, 0, :], 0.0)
                        return e.memset(vt[64:128, :, nb, :], 0.0)
                    S.op("pool", vz, writes=[r_vt])
                    S.dma("sp", vt[64:128, :, 0, :], vsrc[:, 0:64, :].rearrange("r p c -> p r c"), writes=[r_vt])
                    S.dma("sp", vt[0:64, :, nb, :], vsrc[:, L - 64:L, :].rearrange("r p c -> p r c"), writes=[r_vt])
                    for rho in range(r):
                        S.dma("sp", vt[:, rho, 1:nb, :], vsrc[rho, 64:L - 64, :].rearrange("(i p) c -> p i c", p=128), writes=[r_vt])
                    blocks = [(rho, b) for rho in range(r) for b in range(nb)]
                    pss_of = {}

                    def issue_qk(n, kpv=kpv, qpv=qpv, r_kp=r_kp, r_qp=r_qp, blocks=blocks, pss_of=pss_of):
                        rho, b = blocks[n]
                        pss, r_pss = pss_ring[cntA["pss"] % 3]
                        cntA["pss"] += 1
                        pss_of[n] = (pss, r_pss)

                        def qk(e, pss=pss, rho=rho, b=b):
                            e.matmul(pss[:, 0, :], kpv[:, rho, b * 128:(b + 1) * 128], qpv[:, rho, b * 128:(b + 1) * 128], start=True, stop=True)
                            return e.matmul(pss[:, 1, :], kpv[:, rho, (b + 1) * 128:(b + 2) * 128], qpv[:, rho, b * 128:(b + 1) * 128], start=True, stop=True)
                        S.op("pe", qk, reads=[r_kp, r_qp], writes=[r_pss])
                    tmp_of = {}

                    def issue_stt(n, mb=mb, r_mb=r_mb, pss_of=pss_of, tmp_of=tmp_of):
                        pss, r_pss = pss_of.pop(n)
                        tmp, r_tmp = tmp_ring[cntA["t"] % 3]
                        cntA["t"] += 1
                        tmp_of[n] = (tmp, r_tmp)
                        S.op("dve", lambda e, tmp=tmp, pss=pss: e.scalar_tensor_tensor(tmp[:], pss[:], SCALE, mb[:], ALU.mult, ALU.add),
                             reads=[r_pss, r_mb], writes=[r_tmp])
                    issue_qk(0)
                    issue_qk(1)
                    issue_stt(0)
                    for n, (rho, b) in enumerate(blocks):
                        tmp, r_tmp = tmp_of.pop(n)
                        pt, r_pt = pt_ring[cntA["b"] % 3]
                        acc, r_acc = acc_ring[cntA["b"] % 3]
                        cntA["b"] += 1
                        S.op("act", lambda e, pt=pt, tmp=tmp: e.activation(pt[:], tmp[:], AF.Exp), reads=[r_tmp], writes=[r_pt])
                        if n + 2 < len(blocks):
                            issue_qk(n + 2)
                        if n + 1 < len(blocks):
                            issue_stt(n + 1)

                        def pv(e, acc=acc, pt=pt, vt=vt, b=b, rho=rho):
                            e.matmul(acc[:], pt[:, 0, :], vt[:, rho, b, :], start=True, stop=False)
                            return e.matmul(acc[:], pt[:, 1, :], vt[:, rho, b + 1, :], start=False, stop=True)
                        S.op("pe", pv, reads=[r_pt, r_vt], writes=[r_acc])
                        S.op("dve", lambda e, ost=ost, acc=acc, n=n: e.tensor_copy(ost[:, n, :], acc[:]), reads=[r_acc], writes=[r_ost])
                    for rho in range(r):
                        dst = acca[g].rearrange("(b p r) h c -> r p b h c", p=128, r=r)[rho][:, :, h, :]
                        S.dma("pool", dst, ost[:, rho * nb:(rho + 1) * nb, :], reads=[r_ost])
            if self.phase_end("attn_a"):
                return True
        with ExitStack() as st:
            a_ring = self.sb(st, "ca", [128, 3, 8, 129], F32, 2)
            s_ring = self.sb(st, "cs_", [128, 8, 129], F32, 2)
            rc_ring = self.sb(st, "crc", [128, 8], F32, 2)
            o_ring = self.sb(st, "co", [128, 8, 128], BF16, 2)
            for t in range(NT):
                a, r_a = a_ring[t % 2]
                sm, r_sm = s_ring[t % 2]
                rc, r_rc = rc_ring[t % 2]
                o, r_o = o_ring[t % 2]
                for g in range(3):
                    S.dma("sp", a[:, g, :, :], acca[g, t * 128:(t + 1) * 128, :, :], writes=[r_a])
                S.op("dve", lambda e, sm=sm, a=a: e.tensor_tensor(sm[:], a[:, 0, :, :], a[:, 1, :, :], ALU.add), reads=[r_a], writes=[r_sm])
                S.op("dve", lambda e, sm=sm, a=a: e.tensor_tensor(sm[:], sm[:], a[:, 2, :, :], ALU.add), reads=[r_a, r_sm], writes=[r_sm])
                S.op("dve", lambda e, sm=sm, rc=rc: e.reciprocal(rc[:], sm[:, :, 128]), reads=[r_sm], writes=[r_rc])
                for h in range(8):
                    eng = "dve" if h % 2 else "pool"
                    S.op(eng, lambda e, o=o, sm=sm, rc=rc, h=h: e.tensor_scalar(o[:, h, :], sm[:, h, 0:128], rc[:, h:h + 1], None, ALU.mult),
                         reads=[r_sm, r_rc], writes=[r_o])
                S.dma("act", self.mixed[t * 128:(t + 1) * 128, 0:1024], o[:].rearrange("p h d -> p (h d)"), reads=[r_o])
            return self.phase_end("comb_a")

    def outproj_phase(self, h_src, wb_out, h_dst, wname=None):
        S = self.S
        wres = self.wres[wname]
        with ExitStack() as st:
            bufs = self.load_T_bufs(st, False, BF16)
            (mT, r_mT), = self.sb(st, "mT", [128, 16, 2048], BF16)
            wt_ring = self.sb(st, "wto", [128, 16, 512], BF16, 2)
            po_ring = self.ps(st, "poo", [128, 512], F32, 4)
            hr_ring = self.sb(st, "hr", [128, 512], F32, 3)
            n_w = 0
            n_p = 0
            for c in range(2):
                tok0 = c * 2048
                self.load_T(st, self.mixed, tok0, 2048, mT, r_mT, 0, bufs=bufs)
                for s in range(4):
                    wt, r_wt = wt_ring[n_w % 2]
                    n_w += 1
                    S.dma("sp", wt[:], wb_out[s], reads=wres[s], writes=[r_wt])
                    for t in range(16):
                        po, r_po = po_ring[n_p % 4]
                        hr, r_hr = hr_ring[n_p % 3]
                        n_p += 1
                        rows = slice(tok0 + t * 128, tok0 + (t + 1) * 128)
                        S.dma("sp", hr[:], h_src[rows, s * 512:(s + 1) * 512], writes=[r_hr])

                        def mm(e, po=po, wt=wt, t=t):
                            for k in range(16):
                                i = e.matmul(po[:], mT[:, k, t * 128:(t + 1) * 128], wt[:, k, :], start=(k == 0), stop=(k == 15))
                            return i
                        S.op("pe", mm, reads=[r_wt, r_mT], writes=[r_po])
                        S.op("dve", lambda e, hr=hr, po=po: e.tensor_tensor(hr[:], hr[:], po[:], ALU.add), reads=[r_po, r_hr], writes=[r_hr])
                        S.dma("act", h_dst[rows, s * 512:(s + 1) * 512], hr[:], reads=[r_hr])
            return self.phase_end("outproj")

    def ffn_phase(self, h_src, gain_ap, wb_up, wb_dn, conv_w, conv_b, h_dst, wn_up=None, wn_dn=None):
        S = self.S
        NFC = DFF // 128
        wres_up = self.wres[wn_up]
        wres_dn = self.wres[wn_dn]
        with ExitStack() as st:
            bufs = self.load_T_bufs(st, True, F32, npT=1)
            gb, r_gb = self.load_gain(st, gain_ap)
            yT_ring = self.sb(st, "y2T", [128, 16, 514], BF16, 1)
            guT_ring = self.sb(st, "guT", [128, NFC, 512], BF16, 1)
            wg_ring = self.sb(st, "wg", [128, 16, 256], BF16, 2)
            wu_ring = self.sb(st, "wu", [128, 16, 256], BF16, 2)
            wd_ring = self.sb(st, "wd", [128, NFC, 256], BF16, 2)
            (cw, r_cw), = self.sb(st, "cw", [128, 4, NFC], F32)
            for i in range(3):
                S.dma("sp", cw[:, i, :], conv_w[i].rearrange("(c p) -> p c", p=128), writes=[r_cw], allow_slow_non_contiguous=True)
            S.dma("sp", cw[:, 3, :], conv_b.rearrange("(c p) -> p c", p=128), writes=[r_cw], allow_slow_non_contiguous=True)
            gs_ring = self.sb(st, "gs", [128, 516], F32, 2)
            t1_ring = self.sb(st, "t1", [128, 512], F32, 2)
            t2_ring = self.sb(st, "t2", [128, 512], F32, 2)
            gl_ring = self.sb(st, "gl", [128, 512], F32, 2)
            hr_ring = self.sb(st, "hrf", [128, 256], F32, 3)
            psg_ring = self.ps(st, "psg", [128, 512], F32, 2)
            psu_ring = self.ps(st, "psu", [128, 512], F32, 2)
            psh_ring = self.ps(st, "psh", [128, 2], F32, 1)
            psd_ring = self.ps(st, "psd", [128, 256], F32, 2)
            n = {"w": 0, "f": 0, "wd": 0, "d": 0}
            for c in range(S_LEN // 512):
                c0 = c * 512
                yT, r_yT = yT_ring[0]
                guT, r_guT = guT_ring[0]
                self.load_T(st, h_src, c0, 512, yT, r_yT, 0, gain_b=gb, r_gain=r_gb, bufs=bufs)
                self.load_T(st, h_src, c0 - 1, 1, yT, r_yT, 512, gain_b=gb, r_gain=r_gb, bufs=bufs)
                self.load_T(st, h_src, c0 + 512, 1, yT, r_yT, 513, gain_b=gb, r_gain=r_gb, bufs=bufs)
                for sl in range(22):
                    wg, r_wg = wg_ring[n["w"] % 2]
                    wu, r_wu = wu_ring[n["w"] % 2]
                    n["w"] += 1
                    S.dma("sp", wg[:], wb_up[sl], reads=wres_up[sl], writes=[r_wg])
                    S.dma("sp", wu[:], wb_up[22 + sl], reads=wres_up[22 + sl], writes=[r_wu])
                    for j in range(2):
                        fc = sl * 2 + j
                        psg, r_psg = psg_ring[n["f"] % 2]
                        psu, r_psu = psu_ring[n["f"] % 2]
                        psh, r_psh = psh_ring[0]
                        gs, r_gs = gs_ring[n["f"] % 2]
                        t1, r_t1 = t1_ring[n["f"] % 2]
                        t2, r_t2 = t2_ring[n["f"] % 2]
                        gl, r_gl = gl_ring[n["f"] % 2]
                        n["f"] += 1

                        def mmg(e, psg=psg, psh=psh, wg=wg, j=j):
                            for k in range(16):
                                e.matmul(psg[:], wg[:, k, j * 128:(j + 1) * 128], yT[:, k, 0:512], start=(k == 0), stop=(k == 15))
                            for k in range(16):
                                i = e.matmul(psh[:], wg[:, k, j * 128:(j + 1) * 128], yT[:, k, 512:514], start=(k == 0), stop=(k == 15))
                            return i
                        S.op("pe", mmg, reads=[r_wg, r_yT], writes=[r_psg, r_psh])

                        def mmu(e, psu=psu, wu=wu, j=j):
                            for k in range(16):
                                i = e.matmul(psu[:], wu[:, k, j * 128:(j + 1) * 128], yT[:, k, 0:512], start=(k == 0), stop=(k == 15))
                            return i
                        S.op("pe", mmu, reads=[r_wu, r_yT], writes=[r_psu])
                        S.op("act", lambda e, gs=gs, psg=psg: e.activation(gs[:, 1:513], psg[:], AF.Copy), reads=[r_psg], writes=[r_gs])

                        def halo(e, gs=gs, psh=psh):
                            e.tensor_copy(gs[:, 0:1], psh[:, 0:1])
                            return e.tensor_copy(gs[:, 513:514], psh[:, 1:2])
                        S.op("dve", halo, reads=[r_psh], writes=[r_gs])
                        S.op("act", lambda e, t1=t1, gs=gs, fc=fc: e.activation(t1[:], gs[:, 1:513], AF.Identity, bias=cw[:, 3, fc:fc + 1], scale=cw[:, 1, fc:fc + 1]),
                             reads=[r_gs, r_cw], writes=[r_t1])
                        S.op("dve", lambda e, t2=t2, gs=gs, t1=t1, fc=fc: e.scalar_tensor_tensor(t2[:], gs[:, 0:512], cw[:, 0, fc:fc + 1], t1[:], ALU.mult, ALU.add),
                             reads=[r_gs, r_t1, r_cw], writes=[r_t2])
                        S.op("dve", lambda e, t1=t1, gs=gs, t2=t2, fc=fc: e.scalar_tensor_tensor(t1[:], gs[:, 2:514], cw[:, 2, fc:fc + 1], t2[:], ALU.mult, ALU.add),
                             reads=[r_gs, r_t2, r_cw], writes=[r_t1])
                        S.op("act", lambda e, gl=gl, t1=t1: e.activation(gl[:], t1[:], AF.Gelu_apprx_tanh), reads=[r_t1], writes=[r_gl])
                        S.op("dve", lambda e, gl=gl, psu=psu, fc=fc: e.tensor_tensor(guT[:, fc, :], gl[:], psu[:], ALU.mult), reads=[r_gl, r_psu], writes=[r_guT])
                for ds in range(8):
                    wd, r_wd = wd_ring[n["wd"] % 2]
                    n["wd"] += 1
                    S.dma("sp", wd[:, 0:22, :], wb_dn[ds, :, 0:22, :], reads=wres_dn[ds], writes=[r_wd])
                    S.dma("sp", wd[:, 22:44, :], wb_dn[ds, :, 22:44, :], reads=wres_dn[ds], writes=[r_wd])
                    for tt in range(4):
                        psd, r_psd = psd_ring[n["d"] % 2]
                        hr, r_hr = hr_ring[n["d"] % 3]
                        n["d"] += 1
                        rows = slice(c0 + tt * 128, c0 + (tt + 1) * 128)
                        S.dma("sp", hr[:], h_src[rows, ds * 256:(ds + 1) * 256], writes=[r_hr])

                        def mmd(e, psd=psd, wd=wd, tt=tt):
                            for k in range(NFC):
                                i = e.matmul(psd[:], guT[:, k, tt * 128:(tt + 1) * 128], wd[:, k, :], start=(k == 0), stop=(k == NFC - 1))
                            return i
                        S.op("pe", mmd, reads=[r_wd, r_guT], writes=[r_psd])
                        S.op("dve", lambda e, hr=hr, psd=psd: e.tensor_tensor(hr[:], hr[:], psd[:], ALU.add), reads=[r_psd, r_hr], writes=[r_hr])
                        S.dma("act", h_dst[rows, ds * 256:(ds + 1) * 256], hr[:], reads=[r_hr])
            return self.phase_end("ffn")

    def final_norm(self, h_src, gain_ap, out):
        S = self.S
        with ExitStack() as st:
            gb, r_gb = self.load_gain(st, gain_ap)
            ht_ring = self.sb(st, "fht", [128, D], F32, 3)
            junk = self.sb(st, "fjk", [128, D], BF16, 1)[0]
            ssr = self.sb(st, "fss", [128, 1], F32, 4)
            for t in range(NT):
                ht, r_ht = ht_ring[t % 3]
                ss, r_ss = ssr[t % 4]
                jk, r_jk = junk
                S.dma("sp", ht[:], h_src[t * 128:(t + 1) * 128, :], writes=[r_ht])
                S.op("act", lambda e, ht=ht, jk=jk, ss=ss: e.activation(jk[:], ht[:], AF.Square, scale=D ** -0.5, accum_out=ss[:]), reads=[r_ht], writes=[r_jk, r_ss])
                S.op("dve", lambda e, ss=ss: e.tensor_scalar(ss[:], ss[:], EPS, None, ALU.add), reads=[r_ss], writes=[r_ss])
                S.op("act", lambda e, ss=ss: e.activation(ss[:], ss[:], AF.Sqrt), reads=[r_ss], writes=[r_ss])
                S.op("dve", lambda e, ss=ss: e.reciprocal(ss[:], ss[:]), reads=[r_ss], writes=[r_ss])
                S.op("dve", lambda e, ht=ht, ss=ss: e.scalar_tensor_tensor(ht[:], ht[:], ss[:, 0:1], gb[:], ALU.mult, ALU.mult), reads=[r_ht, r_ss, r_gb], writes=[r_ht])
                S.dma("act", out[t * 128:(t + 1) * 128, :], ht[:], reads=[r_ht])
            return self.phase_end("final")


def _t5_bucket_np(rel):
    nb = 16
    max_exact = 8
    rel = np.asarray(rel, dtype=np.int64)
    ret = np.where(rel > 0, nb, 0)
    n = np.abs(rel)
    n_f = np.maximum(n, 1).astype(np.float32)
    large = max_exact + (np.log(n_f / np.float32(max_exact)) / np.float32(math.log(1024 / max_exact)) * np.float32(nb - max_exact)).astype(np.int32)
    large = np.minimum(large, nb - 1)
    return ret + np.where(n < max_exact, n, large)


def host_tables(t5_table, na_rpb):
    t5 = np.asarray(t5_table, dtype=np.float32)
    out = {}
    i = np.arange(128)[:, None]
    c = np.arange(2176)[None, :]
    bk = _t5_bucket_np(i - c + 1024)
    out["bandB"] = np.ascontiguousarray(np.stack([t5[bk, 24 + h] for h in range(4)]).astype(np.float32))
    cb = np.zeros((128, 8), np.float32)
    for h in range(4):
        cb[:, 2 * h] = t5[15, 24 + h]
        cb[:, 2 * h + 1] = t5[31, 24 + h]
    out["constB"] = cb
    p = np.arange(128)[:, None, None]
    t = np.arange(2)[None, :, None]
    j = np.arange(128)[None, None, :]
    rel_sub = (p - 64 + 128 * t) - j
    valid = np.abs(rel_sub) <= 64
    mba = np.zeros((24, 128, 2, 128), np.float32)
    for g, r in enumerate((1, 4, 16)):
        bk = _t5_bucket_np(r * rel_sub)
        for h in range(8):
            mba[g * 8 + h] = np.where(valid, t5[bk, g * 8 + h], np.float32(NEG))
    out["mbA"] = mba
    rpb = np.asarray(na_rpb, dtype=np.float32)[0]
    e = (np.arange(128) // 64)[:, None, None]
    bp = (np.arange(128) % 64)[:, None, None]
    a = np.arange(4)[None, :, None]
    jj = np.arange(64)[None, None, :]
    cs = np.clip(jj - 8, 0, 48)
    validc = (bp >= cs) & (bp < cs + 16)
    dc = np.clip(bp - jj + 15, 0, 30)
    mbd = np.zeros((8, 8, 128, 4, 64), np.float32)
    for di in range(8):
        delta = -di
        dr = delta + 2 * a + e + 7
        drc = np.clip(dr, 0, 14) + 0 * jj
        for h in range(8):
            vals = rpb[h][drc, dc + 0 * a]
            mbd[h, di] = np.where(validc & (dr >= 0) & (dr <= 14), vals, np.float32(NEG))
    out["mbD"] = mbd
    tpos = np.arange(S_LEN)
    row = (tpos // 64).astype(np.float32)
    col = (tpos % 64).astype(np.float32)
    inv_freq = (np.float32(10000.0) ** (-(np.arange(0, 64, 2, dtype=np.float32) / np.float32(64)))).astype(np.float32)
    ang = np.concatenate([row[:, None] * inv_freq[None], col[:, None] * inv_freq[None]], axis=-1).astype(np.float32)
    out["ropec"] = np.cos(ang).astype(np.float32)
    out["ropes"] = np.sin(ang).astype(np.float32)
    return out


def _attn_D(self, v1d, mbd_d):
    S = self.S
    with ExitStack() as st:
        qn_ring = self.sb(st, "qd", [128, S_LEN], BF16, 2)
        kn_ring = self.sb(st, "kd", [128, S_LEN], BF16, 2)
        v0_ring = self.sb(st, "vd0", [128, 32, 129], BF16, 2)
        v1_ring = self.sb(st, "vd1", [128, 31, 129], BF16, 2)
        mb_ring = self.sb(st, "mbd", [128, 8, 4, 64], F32, 2)
        tmp_ring = self.sb(st, "tmpd", [128, 4, 64], F32, 3)
        pt_ring = self.sb(st, "ptd", [128, 4, 64], BF16, 3)
        rec_ring = self.sb(st, "recd", [64, 1], F32, 4)
        ost_ring = self.sb(st, "ostd", [64, 64, 128], BF16, 2)
        pss_ring = self.ps(st, "pssd", [128, 4, 64], F32, 3)
        acc_ring = self.ps(st, "accd", [64, 129], F32, 3)
        n_b = 0
        cntD = {"pss": 0, "t": 0}
        for h in range(8):
            qn, r_qn = qn_ring[h % 2]
            kn, r_kn = kn_ring[h % 2]
            v0, r_v0 = v0_ring[h % 2]
            v1, r_v1 = v1_ring[h % 2]
            mb, r_mb = mb_ring[h % 2]
            ost, r_ost = ost_ring[h % 2]
            S.dma("sp", qn[:], self.qt[10 + h], writes=[r_qn])
            S.dma("sp", kn[:], self.qt[18 + h], writes=[r_kn])
            S.dma("sp", mb[:], mbd_d[h].rearrange("d p a j -> p d a j"), writes=[r_mb])
            for part in range(4):
                S.dma("pool", v0[:, part * 8:(part + 1) * 8, :], v1d[part * 1024:(part + 1) * 1024, h, :].rearrange("(t p) c -> p t c", p=128), writes=[r_v0])
            S.dma("pool", v1[:, 0:16, :], v1d[64:64 + 2048, h, :].rearrange("(t p) c -> p t c", p=128), writes=[r_v1])
            S.dma("pool", v1[:, 16:31, :], v1d[64 + 2048:64 + 2048 + 1920, h, :].rearrange("(t p) c -> p t c", p=128), writes=[r_v1])
            pss_of = {}

            def issue_qk(i, kn=kn, qn=qn, r_kn=r_kn, r_qn=r_qn, pss_of=pss_of):
                rs = min(max(i - 4, 0), 56)
                pss, r_pss = pss_ring[cntD["pss"] % 3]
                cntD["pss"] += 1
                pss_of[i] = (pss, r_pss)

                def qk(e, pss=pss, rs=rs, i=i):
                    for a in range(4):
                        k0 = 64 * rs + 128 * a
                        ins = e.matmul(pss[:, a, :], kn[:, k0:k0 + 128], qn[:, 64 * i:64 * i + 64], start=True, stop=True)
                    return ins
                S.op("pe", qk, reads=[r_kn, r_qn], writes=[r_pss])
            tmp_of = {}

            def issue_stt(i, mb=mb, r_mb=r_mb, pss_of=pss_of, tmp_of=tmp_of):
                rs = min(max(i - 4, 0), 56)
                di = i - rs
                pss, r_pss = pss_of.pop(i)
                tmp, r_tmp = tmp_ring[cntD["t"] % 3]
                cntD["t"] += 1
                tmp_of[i] = (tmp, r_tmp)
                S.op("dve", lambda e, tmp=tmp, pss=pss, di=di: e.scalar_tensor_tensor(tmp[:], pss[:], SCALE, mb[:, di, :, :], ALU.mult, ALU.add),
                     reads=[r_pss, r_mb], writes=[r_tmp])
            issue_qk(0)
            issue_qk(1)
            issue_stt(0)
            for i in range(64):
                rs = min(max(i - 4, 0), 56)
                tmp, r_tmp = tmp_of.pop(i)
                pt, r_pt = pt_ring[n_b % 3]
                acc, r_acc = acc_ring[n_b % 3]
                rec, r_rec = rec_ring[n_b % 4]
                n_b += 1
                S.op("act", lambda e, pt=pt, tmp=tmp: e.activation(pt[:], tmp[:], AF.Exp), reads=[r_tmp], writes=[r_pt])
                if i + 2 < 64:
                    issue_qk(i + 2)
                if i + 1 < 64:
                    issue_stt(i + 1)
                if rs % 2 == 0:
                    vv, r_vv, tb = v0, r_v0, rs // 2
                else:
                    vv, r_vv, tb = v1, r_v1, (rs - 1) // 2

                def pv(e, acc=acc, pt=pt, vv=vv, tb=tb):
                    for a in range(4):
                        ins = e.matmul(acc[:], pt[:, a, :], vv[:, tb + a, :], start=(a == 0), stop=(a == 3))
                    return ins
                S.op("pe", pv, reads=[r_pt, r_vv], writes=[r_acc])
                S.op("dve", lambda e, rec=rec, acc=acc: e.reciprocal(rec[:], acc[:, 128:129]), reads=[r_acc], writes=[r_rec])
                S.op("dve", lambda e, ost=ost, acc=acc, rec=rec, i=i: e.tensor_scalar(ost[:, i, :], acc[:, 0:128], rec[:, 0:1], None, ALU.mult),
                     reads=[r_acc, r_rec], writes=[r_ost])
            S.dma("pool", self.mixed[:, 1024 + h * 128:1024 + (h + 1) * 128].rearrange("(i p) c -> p i c", p=64), ost[:], reads=[r_ost])
        return self.phase_end("attn_d")


KB.attn_D = _attn_D


def build(debug=(), stop_after=None, l1_only=False):
    kb = KB(debug, stop_after)
    nc = kb.nc
    S = kb.S
    I = {}
    I["x"] = kb.inp("x", [S_LEN, D])
    I["ln_mix"] = kb.inp("ln_mix", [2, D])
    I["ln_ffn"] = kb.inp("ln_ffn", [2, D])
    I["ln_final"] = kb.inp("ln_final", [D])
    I["ev_w_in"] = kb.inp("ev_w_in", [D, 12288])
    I["ev_w_out"] = kb.inp("ev_w_out", [D, D])
    for nme in ("diff_lq1", "diff_lk1", "diff_lq2", "diff_lk2"):
        I[nme] = kb.inp(nme, [128])
    I["diff_subln"] = kb.inp("diff_subln", [256])
    I["od_w_in"] = kb.inp("od_w_in", [D, 4608])
    I["od_w_out"] = kb.inp("od_w_out", [D, D])
    I["gqa_q_norm"] = kb.inp("gqa_q_norm", [128])
    I["gqa_k_norm"] = kb.inp("gqa_k_norm", [128])
    I["ffn_w_up"] = kb.inp("ffn_w_up", [2, D, 2 * DFF])
    I["ffn_conv_w"] = kb.inp("ffn_conv_w", [2, 3, DFF])
    I["ffn_conv_b"] = kb.inp("ffn_conv_b", [2, DFF])
    I["ffn_w_down"] = kb.inp("ffn_w_down", [2, DFF, D])
    I["bandB"] = kb.inp("bandB", [4, 128, 2176])
    I["constB"] = kb.inp("constB", [128, 8])
    I["mbA"] = kb.inp("mbA", [24, 128, 2, 128])
    I["mbD"] = kb.inp("mbD", [8, 8, 128, 4, 64])
    I["ropec"] = kb.inp("ropec", [S_LEN, 64])
    I["ropes"] = kb.inp("ropes", [S_LEN, 64])
    out = nc.dram_tensor("out", [S_LEN, D], F32, kind="ExternalOutput").ap()

    kb.qt = kb.dram("qt", [64, 128, S_LEN], BF16)
    kb.mixed = kb.dram("mixed", [S_LEN, D], BF16)
    v1a = kb.dram("v1a", [3, S_LEN, 8, 129], BF16)
    v1b = kb.dram("v1b", [S_LEN, 4, 257], BF16)
    v1c = kb.dram("v1c", [S_LEN, 2, 129], BF16)
    v1d = kb.dram("v1d", [S_LEN, 8, 129], BF16)
    acca = kb.dram("acca", [3, S_LEN, 8, 129], F32)
    hA = kb.dram("hA", [S_LEN, D], F32)
    hB = kb.dram("hB", [S_LEN, D], F32)

    def done():
        kb.es.close()
        return nc

    kb.setup_consts()
    if l1_only:
        hB = kb.inp("hB_in", [S_LEN, D])
    if not l1_only:
        wb_in0 = kb.precast(I["ev_w_in"], "wb_in0", D, 12288, 512)
        wb_out0 = kb.precast(I["ev_w_out"], "wb_out0", D, D, 512)
        wb_up0 = kb.precast(I["ffn_w_up"][0], "wb_up0", D, 2 * DFF, 256)
        wb_dn0 = kb.precast(I["ffn_w_down"][0], "wb_dn0", DFF, D, 256, kstep=11)
    wb_in1 = kb.precast(I["od_w_in"], "wb_in1", D, 4608, 512)
    wb_out1 = kb.precast(I["od_w_out"], "wb_out1", D, D, 512)
    wb_up1 = kb.precast(I["ffn_w_up"][1], "wb_up1", D, 2 * DFF, 256)
    wb_dn1 = kb.precast(I["ffn_w_down"][1], "wb_dn1", DFF, D, 256, kstep=11)
    if kb.stop_after == "precast":
        kb.phase_end("precast")
        return done()

    if l1_only:
        return _layer1(kb, I, hA, hB, v1c, v1d, wb_in1, wb_out1, wb_up1, wb_dn1, out, done)
    slabs0 = []
    for g in range(3):
        slabs0.append(("fm", [g * 16 + h for h in range(0, 4)]))
        slabs0.append(("fm", [g * 16 + h for h in range(4, 8)]))
        slabs0.append(("fm", [g * 16 + 8 + h for h in range(0, 4)]))
        slabs0.append(("fm", [g * 16 + 8 + h for h in range(4, 8)]))
        slabs0.append(("tmv", v1a[g], 0, 4, 128))
        slabs0.append(("tmv", v1a[g], 4, 4, 128))
    slabs0.append(("fm", [48 + i for i in range(0, 4)]))
    slabs0.append(("fm", [48 + i for i in range(4, 8)]))
    slabs0.append(("fm", [56 + i for i in range(0, 4)]))
    slabs0.append(("fm", [56 + i for i in range(4, 8)]))
    slabs0.append(("tmv", v1b, 0, 2, 256))
    slabs0.append(("tmv", v1b, 2, 2, 256))
    if kb.proj_phase(I["x"], I["ln_mix"][0], wb_in0, slabs0, wname="wb_in0"):
        return done()
    if kb.attn_A(v1a, I["mbA"], acca):
        return done()
    lambda_init0 = 0.8 - 0.6 * math.exp(-0.3 * 0)
    if kb.attn_B(v1b, I["bandB"], I["constB"], I["diff_lq1"], I["diff_lk1"], I["diff_lq2"], I["diff_lk2"], I["diff_subln"], lambda_init0):
        return done()
    if kb.outproj_phase(I["x"], wb_out0, hA, wname="wb_out0"):
        return done()
    if kb.ffn_phase(hA, I["ln_ffn"][0], wb_up0, wb_dn0, I["ffn_conv_w"][0], I["ffn_conv_b"][0], hB, wn_up="wb_up0", wn_dn="wb_dn0"):
        return done()
    return _layer1(kb, I, hA, hB, v1c, v1d, wb_in1, wb_out1, wb_up1, wb_dn1, out, done)


def _layer1(kb, I, hA, hB, v1c, v1d, wb_in1, wb_out1, wb_up1, wb_dn1, out, done):
    slabs1 = [
        ("tmc", [("q", 0), ("q", 1), ("q", 2), ("q", 3)]),
        ("tmc", [("q", 4), ("q", 5), ("q", 6), ("q", 7)]),
        ("tmc", [("k", 8), ("k", 9), ("v", v1c, 0), ("v", v1c, 1)]),
        ("fm", [10, 11, 12, 13]), ("fm", [14, 15, 16, 17]),
        ("fm", [18, 19, 20, 21]), ("fm", [22, 23, 24, 25]),
        ("tmv", v1d, 0, 4, 128), ("tmv", v1d, 4, 4, 128),
    ]
    if kb.proj_phase(hB, I["ln_mix"][1], wb_in1, slabs1, rope=(I["ropec"], I["ropes"], I["gqa_q_norm"], I["gqa_k_norm"]), wname="wb_in1"):
        return done()
    if kb.attn_C(v1c):
        return done()
    if kb.attn_D(v1d, I["mbD"]):
        return done()
    if kb.outproj_phase(hB, wb_out1, hA, wname="wb_out1"):
        return done()
    if kb.ffn_phase(hA, I["ln_ffn"][1], wb_up1, wb_dn1, I["ffn_conv_w"][1], I["ffn_conv_b"][1], hB, wn_up="wb_up1", wn_dn="wb_dn1"):
        return done()
    kb.final_norm(hB, I["ln_final"], out)
    return done()


ACTIVE = [0, 1, 4, 5]


def make_in_maps(inputs, ncores=8):
    aux = host_tables(inputs["t5_table"], inputs["na_rpb"])
    f = lambda a: np.ascontiguousarray(np.asarray(a, dtype=np.float32))
    shared = {
        "ln_mix": f(inputs["ln_mix"]), "ln_ffn": f(inputs["ln_ffn"]), "ln_final": f(inputs["ln_final"]),
        "ev_w_in": f(inputs["ev_w_in"][0]), "ev_w_out": f(inputs["ev_w_out"][0]),
        "diff_lq1": f(inputs["diff_lq1"][0]), "diff_lk1": f(inputs["diff_lk1"][0]),
        "diff_lq2": f(inputs["diff_lq2"][0]), "diff_lk2": f(inputs["diff_lk2"][0]),
        "diff_subln": f(inputs["diff_subln"][0]),
        "od_w_in": f(inputs["od_w_in"][0]), "od_w_out": f(inputs["od_w_out"][0]),
        "gqa_q_norm": f(inputs["gqa_q_norm"][0]), "gqa_k_norm": f(inputs["gqa_k_norm"][0]),
        "ffn_w_up": f(inputs["ffn_w_up"]), "ffn_conv_w": f(inputs["ffn_conv_w"]), "ffn_conv_b": f(inputs["ffn_conv_b"]),
        "ffn_w_down": f(inputs["ffn_w_down"]),
    }
    shared.update(aux)
    x = np.asarray(inputs["x"], dtype=np.float32)
    maps = []
    if ncores == 1:
        m = dict(shared)
        m["x"] = np.ascontiguousarray(x[0])
        return [m]
    zero = {k_: np.zeros_like(v) for k_, v in shared.items()}
    zero["x"] = np.zeros_like(x[0])
    for c in range(ncores):
        if c in ACTIVE:
            m = dict(shared)
            m["x"] = np.ascontiguousarray(x[ACTIVE.index(c)])
        else:
            m = zero
        maps.append(m)
    return maps


def kernel(**inputs):
    nc = build()
    maps = make_in_maps(inputs, 8)
    res = run_bass_kernel_spmd(nc, maps, core_ids=list(range(8)))
    out = np.stack([np.asarray(res.results[c]["out"], dtype=np.float32) for c in ACTIVE], axis=0)
    return out
```

```python
import numpy as np
import concourse.bass as bass
import concourse.mybir as mybir

F32 = mybir.dt.float32
BF16 = mybir.dt.bfloat16
AF = mybir.ActivationFunctionType
ALU = mybir.AluOpType
AX = mybir.AxisListType

ENGS = ("pe", "act", "dve", "pool", "sp")
N_DMA_SEMS = 24
N_BG_SEMS = 12
SYNC_SAME_ENGINE = True


class Res:
    __slots__ = ("lw", "rd", "name", "excl")

    def __init__(self, name="", excl=False):
        self.lw = None
        self.rd = []
        self.name = name
        self.excl = excl


class Op:
    __slots__ = ("eng", "fn", "deps", "signaled", "count", "is_dma", "dsem", "dtarget", "prev_same_sem", "idx", "epoch", "bg")


class Sched:
    def __init__(self, nc, es):
        self.nc = nc
        self.ops = {e: [] for e in ENGS}
        self.emitted = {e: 0 for e in ENGS}
        self.sig_count = {e: 0 for e in ENGS}
        self.esem = {e: es.enter_context(nc.semaphore("s_" + e)) for e in ENGS if e != "sp"}
        self.dsems = [es.enter_context(nc.semaphore("d%d" % i)) for i in range(N_DMA_SEMS)]
        self.dsem_count = [0] * N_DMA_SEMS
        self.dsem_last = [None] * N_DMA_SEMS
        self.dma_rr = 0
        self.bsems = [es.enter_context(nc.semaphore("b%d" % i)) for i in range(N_BG_SEMS)]
        self.bsem_count = [0] * N_BG_SEMS
        self.bsem_last = [None] * N_BG_SEMS
        self.bg_rr = 0
        self.waited = {e: {} for e in ENGS}
        self.last_op = {e: None for e in ENGS}
        self.all_dma = []
        self.epoch = 0

    def engine(self, e):
        nc = self.nc
        return {"pe": nc.tensor, "act": nc.scalar, "dve": nc.vector, "pool": nc.gpsimd, "sp": nc.sync}[e]

    def _mk(self, eng, fn, reads, writes, is_dma):
        op = Op()
        op.eng = eng
        op.fn = fn
        op.is_dma = is_dma
        op.signaled = False
        op.count = None
        op.dsem = None
        op.prev_same_sem = None
        op.bg = False
        deps = []
        xr = [r for r in reads if r.excl]
        if xr:
            reads = [r for r in reads if not r.excl]
            writes = list(writes) + [r for r in xr if r not in writes]
        for r in reads:
            if r.lw is not None:
                deps.append(r.lw)
        for w in writes:
            if w.lw is not None:
                deps.append(w.lw)
            deps.extend(w.rd)
        for r in reads:
            r.rd.append(op)
        for w in writes:
            w.lw = op
            w.rd = []
        seen = set()
        ud = []
        for d in deps:
            if id(d) in seen or d is op:
                continue
            seen.add(id(d))
            ud.append(d)
        op.deps = ud
        op.epoch = self.epoch
        op.idx = len(self.ops[eng])
        self.ops[eng].append(op)
        self.last_op[eng] = op
        return op

    def op(self, eng, fn, reads=(), writes=()):
        return self._mk(eng, fn, reads, writes, False)

    def dma(self, eng, out, in_, reads=(), writes=(), bg=False, **kw):
        def fn(e):
            return e.dma_start(out=out, in_=in_, **kw)
        op = self._mk(eng, fn, reads, writes, True)
        if bg:
            op.bg = True
            s = self.bg_rr
            self.bg_rr = (self.bg_rr + 1) % N_BG_SEMS
            op.dsem = s
            self.bsem_count[s] += 16
            op.dtarget = self.bsem_count[s]
            op.prev_same_sem = self.bsem_last[s]
            self.bsem_last[s] = op
            return op
        s = self.dma_rr
        self.dma_rr = (self.dma_rr + 1) % N_DMA_SEMS
        op.dsem = s
        self.dsem_count[s] += 16
        op.dtarget = self.dsem_count[s]
        op.prev_same_sem = self.dsem_last[s]
        self.dsem_last[s] = op
        self.all_dma.append(op)
        return op

    def barrier(self):
        lasts = [self.last_op[e] for e in ENGS if self.last_op[e] is not None and not self.last_op[e].is_dma]
        dl = [d for d in self.dsem_last if d is not None]
        lastc = []
        for e in ENGS:
            for o in reversed(self.ops[e]):
                if not o.is_dma and o.fn is not None:
                    lastc.append(o)
                    break
        for e in ENGS:
            op = Op()
            op.eng = e
            op.fn = None
            op.is_dma = False
            op.signaled = False
            op.count = None
            op.dsem = None
            op.prev_same_sem = None
            op.bg = False
            op.deps = [d for d in (lastc + dl)]
            op.epoch = self.epoch
            op.idx = len(self.ops[e])
            self.ops[e].append(op)
        self.epoch += 1

    def flush(self):
        nc = self.nc
        self.barrier()
        for e in ENGS:
            for o in self.ops[e][self.emitted[e]:]:
                for d in o.deps:
                    if d.epoch < o.epoch and not d.bg:
                        continue
                    if not d.is_dma:
                        if d.eng == o.eng and (d.eng == "pe" or not SYNC_SAME_ENGINE):
                            continue
                        d.signaled = True
        for e in ENGS:
            for o in self.ops[e][self.emitted[e]:]:
                if o.signaled and o.count is None and not o.is_dma:
                    self.sig_count[e] += 1
                    o.count = self.sig_count[e]
        sched = self

        def emit_engine(e, engobj):
            waited = sched.waited[e]
            for o in sched.ops[e][sched.emitted[e]:]:
                deps = list(o.deps)
                if o.is_dma and o.prev_same_sem is not None:
                    deps.append(o.prev_same_sem)
                for d in deps:
                    if d.epoch < o.epoch and not d.bg:
                        continue
                    if d.is_dma and d.bg:
                        key = ("b", d.dsem)
                        val = d.dtarget
                        sem = sched.bsems[d.dsem]
                    elif d.is_dma:
                        key = ("d", d.dsem)
                        val = d.dtarget
                        sem = sched.dsems[d.dsem]
                    else:
                        if d.eng == e and (e == "pe" or not SYNC_SAME_ENGINE):
                            continue
                        assert d.count is not None, "dep on unsignaled op"
                        key = ("e", d.eng)
                        val = d.count
                        sem = sched.esem[d.eng]
                    if waited.get(key, 0) >= val:
                        continue
                    waited[key] = val
                    engobj.wait_ge(sem, val)
                if o.fn is None:
                    continue
                ins = o.fn(engobj)
                if o.is_dma and o.bg:
                    ins.then_inc(sched.bsems[o.dsem], 16)
                elif o.is_dma:
                    ins.then_inc(sched.dsems[o.dsem], 16)
                elif o.signaled:
                    ins.then_inc(sched.esem[e], 1)
            sched.emitted[e] = len(sched.ops[e])

        with nc.Block() as block:
            @block.tensor
            def _(eng):
                emit_engine("pe", eng)

            @block.scalar
            def _(eng):
                emit_engine("act", eng)

            @block.vector
            def _(eng):
                emit_engine("dve", eng)

            @block.gpsimd
            def _(eng):
                emit_engine("pool", eng)

            @block.sync
            def _(eng):
                emit_engine("sp", eng)

import math
from contextlib import ExitStack
from concourse.bass_utils import run_bass_kernel_spmd

S_LEN = 4096
D = 2048
DFF = 5632
NT = S_LEN // 128
EPS = 1e-6
NEG = -30000.0
SCALE = 128 ** -0.5
DIL = (1, 4, 16)


class KB:
    def __init__(self, debug=(), stop_after=None):
        self.nc = bass.Bass("TRN2", target_bir_lowering=False)
        self.debug = set(debug)
        self.stop_after = stop_after
        self.es = ExitStack()
        self.S = Sched(self.nc, self.es)
        self.rid = 0
        self.wres = {}

    def dram(self, name, shape, dtype, kind=None):
        if kind is None:
            kind = "ExternalOutput" if name in self.debug else "Internal"
        return self.nc.dram_tensor(name, list(shape), dtype, kind=kind).ap()

    def inp(self, name, shape, dtype=F32):
        return self.nc.dram_tensor(name, list(shape), dtype, kind="ExternalInput").ap()

    def res(self, name=""):
        self.rid += 1
        return Res(name + str(self.rid))

    def sb(self, st, name, shape, dtype, n=1):
        out = []
        for i in range(n):
            t = st.enter_context(self.nc.sbuf_tensor("%s_%d_%d" % (name, i, self.rid), list(shape), dtype))
            self.rid += 1
            out.append((t, self.res(name)))
        return out

    def ps(self, st, name, shape, dtype, n=1):
        out = []
        for i in range(n):
            t = st.enter_context(self.nc.psum_tensor("%s_%d_%d" % (name, i, self.rid), list(shape), dtype))
            self.rid += 1
            r = self.res(name)
            r.excl = True
            out.append((t, r))
        return out

    def phase_end(self, name):
        self.S.flush()
        return self.stop_after == name

    def precast(self, w, name, K, N, sc, kstep=16):
        KC = K // 128
        ns = N // sc
        wb = self.dram(name, [ns, 128, KC, sc], BF16)
        rs = []
        for s in range(ns):
            parts = []
            for k0 in range(0, KC, kstep):
                k1 = min(KC, k0 + kstep)
                r = self.res(name)
                parts.append(r)
                self.S.dma("pool", wb[s, :, k0:k1, :],
                           w[k0 * 128:k1 * 128, s * sc:(s + 1) * sc].rearrange("(k p) n -> p k n", p=128),
                           writes=[r], bg=True)
            rs.append(parts)
        self.wres[name] = rs
        return wb

    def setup_consts(self):
        st = self.es
        S = self.S
        (identf, r_if), = self.sb(st, "identf", [128, 128], F32)
        (ident, r_id), = self.sb(st, "ident", [128, 128], BF16)

        def mk_ident(e):
            e.memset(identf[:], 0.0)
            return e.affine_select(identf[:], identf[:], pattern=[[-1, 128]], compare_op=ALU.not_equal,
                                   fill=1.0, base=0, channel_multiplier=1)
        S.op("pool", mk_ident, writes=[r_if])
        S.op("dve", lambda e: e.tensor_copy(ident[:], identf[:]), reads=[r_if], writes=[r_id])
        self.ident = ident
        self.r_ident = r_id

    def load_T(self, st, src, t0, ntok, xT, r_xT, col0, gain_b=None, r_gain=None, dt_in=F32, bufs=None):
        S = self.S
        ht_ring, yb_ring, junk, ssr, pT_ring = bufs
        ntile = (ntok + 127) // 128
        for ti in range(ntile):
            r0 = t0 + ti * 128
            n = min(128, ntok - ti * 128)
            ht, r_ht = ht_ring[self.cnt_ht % len(ht_ring)]
            self.cnt_ht += 1
            lo = max(r0, 0)
            hi = min(r0 + n, S_LEN)
            if lo > r0 or hi < r0 + n:
                S.op("pool", lambda e, ht=ht, n=n: e.memset(ht[0:n, :], 0.0), writes=[r_ht])
            if hi > lo:
                S.dma("sp", ht[lo - r0:hi - r0, :], src[lo:hi, :], writes=[r_ht])
            if gain_b is not None:
                yb, r_yb = yb_ring[self.cnt_yb % len(yb_ring)]
                self.cnt_yb += 1
                (jk, r_jk) = junk
                (ss, r_ss) = ssr[self.cnt_yb % len(ssr)]
                S.op("act", lambda e, ht=ht, jk=jk, ss=ss, n=n: e.activation(jk[0:n, :], ht[0:n, :], AF.Square, scale=D ** -0.5, accum_out=ss[0:n, :]),
                     reads=[r_ht], writes=[r_jk, r_ss])
                S.op("dve", lambda e, ss=ss, n=n: e.tensor_scalar(ss[0:n, :], ss[0:n, :], EPS, None, ALU.add), reads=[r_ss], writes=[r_ss])
                S.op("act", lambda e, ss=ss, n=n: e.activation(ss[0:n, :], ss[0:n, :], AF.Sqrt), reads=[r_ss], writes=[r_ss])
                S.op("dve", lambda e, ss=ss, n=n: e.reciprocal(ss[0:n, :], ss[0:n, :]), reads=[r_ss], writes=[r_ss])
                S.op("dve", lambda e, yb=yb, ht=ht, ss=ss, n=n: e.scalar_tensor_tensor(yb[0:n, :], ht[0:n, :], ss[0:n, 0:1], gain_b[0:n, :], ALU.mult, ALU.mult),
                     reads=[r_ht, r_ss, r_gain], writes=[r_yb])
                srcT, r_srcT = yb, r_yb
            else:
                srcT, r_srcT = ht, r_ht
            for g in range(2):
                pT, r_pT = pT_ring[self.cnt_pT % len(pT_ring)]
                self.cnt_pT += 1

                def tr(e, pT=pT, srcT=srcT, g=g, n=n):
                    for j in range(8):
                        k = g * 8 + j
                        i = e.transpose(pT[:, j, 0:n], srcT[0:n, k * 128:(k + 1) * 128], self.ident[0:n, 0:n])
                    return i
                S.op("pe", tr, reads=[r_srcT, self.r_ident], writes=[r_pT])
                c0 = col0 + ti * 128
                eng = "act" if (self.cnt_pT % 2) else "dve"
                if eng == "act":
                    S.op("act", lambda e, pT=pT, g=g, c0=c0, n=n: e.activation(xT[:, g * 8:(g + 1) * 8, c0:c0 + n], pT[:, :, 0:n], AF.Copy),
                         reads=[r_pT], writes=[r_xT])
                else:
                    S.op("dve", lambda e, pT=pT, g=g, c0=c0, n=n: e.tensor_copy(xT[:, g * 8:(g + 1) * 8, c0:c0 + n], pT[:, :, 0:n]),
                         reads=[r_pT], writes=[r_xT])

    def load_T_bufs(self, st, norm, dt_in, npT=2):
        ht_ring = self.sb(st, "ht", [128, D], dt_in, 2)
        if norm:
            yb_ring = self.sb(st, "yb", [128, D], BF16, 2)
            junk = self.sb(st, "junk", [128, D], BF16, 1)[0]
            ssr = self.sb(st, "ss", [128, 1], F32, 4)
        else:
            yb_ring = junk = ssr = None
        pT_ring = self.ps(st, "pT", [128, 8, 128], BF16, npT)
        self.cnt_ht = self.cnt_yb = self.cnt_pT = 0
        return (ht_ring, yb_ring, junk, ssr, pT_ring)

    def load_gain(self, st, g_ap, n=D):
        (gb, r_gb), = self.sb(st, "gb", [128, n], F32)
        self.S.dma("sp", gb[:], g_ap.partition_broadcast(128), writes=[r_gb])
        return gb, r_gb

    def proj_phase(self, h_src, gain_ap, wb, slabs, rope=None, wname=None):
        S = self.S
        nc = self.nc
        wres = self.wres[wname]
        with ExitStack() as st:
            bufs = self.load_T_bufs(st, True, F32)
            gb, r_gb = self.load_gain(st, gain_ap)
            (yT, r_yT), = self.sb(st, "yT", [128, 16, 2048], BF16)
            wt_ring = self.sb(st, "wt", [128, 16, 512], BF16, 2)
            po_ring = self.ps(st, "po", [128, 512], F32, 4)
            qst_ring = self.sb(st, "qst", [128, 2048], BF16, 3)
            vst_ring = {}
            cnt = {"wt": 0, "po": 0, "qst": 0, "ev": 0}
            if rope is not None:
                cos_d, sin_d, gq_ap, gk_ap = rope
                (gqb, r_gqb), = self.sb(st, "gqb", [128, 128], F32)
                (gkb, r_gkb), = self.sb(st, "gkb", [128, 128], F32)
                S.dma("sp", gqb[:], gq_ap.partition_broadcast(128), writes=[r_gqb])
                S.dma("sp", gkb[:], gk_ap.partition_broadcast(128), writes=[r_gkb])
                cs_ring = self.sb(st, "cs", [128, 2, 64], F32, 3)
                sq_ring = self.sb(st, "sq", [128, 512], F32, 2)
                ss4_ring = self.sb(st, "ss4", [128, 4], F32, 3)
                xn_ring = self.sb(st, "xn", [128, 4, 128], F32, 2)
                tt_ring = self.sb(st, "tt", [128, 4, 4, 64], F32, 2)
                xr_ring = self.sb(st, "xr", [128, 4, 128], BF16, 2)
                hst = self.sb(st, "hst", [128, 2048], BF16, 8)
                pTc_ring = self.ps(st, "pTc", [128, 4, 128], BF16, 2)
                cnt.update({"cs": 0, "sq": 0, "xn": 0, "hst": 0, "pTc": 0})

            def get_vst(nh, dv):
                key = (nh, dv)
                if key not in vst_ring:
                    ring = self.sb(st, "vst", [128, nh, dv + 1], BF16, 3)
                    for (t, r) in ring:
                        S.op("dve", lambda e, t=t: e.memset(t[:], 1.0), writes=[r])
                    vst_ring[key] = [ring, 0]
                ent = vst_ring[key]
                t, r = ent[0][ent[1] % 3]
                ent[1] += 1
                return t, r

            def evac(out_ap, in_ap, reads, writes):
                cnt["ev"] += 1
                if cnt["ev"] % 2:
                    S.op("act", lambda e: e.activation(out_ap, in_ap, AF.Copy), reads=reads, writes=writes)
                else:
                    S.op("dve", lambda e: e.tensor_copy(out_ap, in_ap), reads=reads, writes=writes)

            for c in range(2):
                tok0 = c * 2048
                self.load_T(st, h_src, tok0, 2048, yT, r_yT, 0, gain_b=gb, r_gain=r_gb, bufs=bufs)
                for si, spec in enumerate(slabs):
                    wt, r_wt = wt_ring[cnt["wt"] % 2]
                    cnt["wt"] += 1
                    S.dma("sp", wt[:], wb[si], reads=wres[si], writes=[r_wt])
                    kind = spec[0]
                    if kind == "fm":
                        for cc in range(4):
                            qid = spec[1][cc]
                            qst, r_qst = qst_ring[cnt["qst"] % 3]
                            cnt["qst"] += 1
                            for tg in range(4):
                                po, r_po = po_ring[cnt["po"] % 4]
                                cnt["po"] += 1

                                def mm(e, po=po, wt=wt, cc=cc, tg=tg):
                                    for k in range(16):
                                        i = e.matmul(po[:], wt[:, k, cc * 128:(cc + 1) * 128], yT[:, k, tg * 512:(tg + 1) * 512],
                                                     start=(k == 0), stop=(k == 15))
                                    return i
                                S.op("pe", mm, reads=[r_wt, r_yT], writes=[r_po])
                                evac(qst[:, tg * 512:(tg + 1) * 512], po[:], [r_po], [r_qst])
                            S.dma("act", self.qt[qid, :, tok0:tok0 + 2048], qst[:], reads=[r_qst])
                    elif kind == "tmv":
                        _, v1, h0, nh, dv = spec
                        for t in range(16):
                            po, r_po = po_ring[cnt["po"] % 4]
                            cnt["po"] += 1

                            def mm(e, po=po, wt=wt, t=t):
                                for k in range(16):
                                    i = e.matmul(po[:], yT[:, k, t * 128:(t + 1) * 128], wt[:, k, :], start=(k == 0), stop=(k == 15))
                                return i
                            S.op("pe", mm, reads=[r_wt, r_yT], writes=[r_po])
                            vst, r_vst = get_vst(nh, dv)
                            evac(vst[:, :, 0:dv], po[:].rearrange("p (h d) -> p h d", h=nh), [r_po], [r_vst])
                            S.dma("act", v1[tok0 + t * 128:tok0 + (t + 1) * 128, h0:h0 + nh, :], vst[:], reads=[r_vst])
                    elif kind == "tmc":
                        heads = spec[1]
                        hsts = []
                        for hh in range(4):
                            if heads[hh][0] in ("q", "k"):
                                hsts.append(hst[cnt["hst"] % 8])
                                cnt["hst"] += 1
                            else:
                                hsts.append(None)
                        nqk = sum(1 for x in heads if x[0] in ("q", "k"))
                        vheads = [x for x in heads if x[0] == "v"]
                        for t in range(16):
                            po, r_po = po_ring[cnt["po"] % 4]
                            cnt["po"] += 1

                            def mm(e, po=po, wt=wt, t=t):
                                for k in range(16):
                                    i = e.matmul(po[:], yT[:, k, t * 128:(t + 1) * 128], wt[:, k, :], start=(k == 0), stop=(k == 15))
                                return i
                            S.op("pe", mm, reads=[r_wt, r_yT], writes=[r_po])
                            if vheads:
                                nv = len(vheads)
                                vst, r_vst = get_vst(nv, 128)
                                evac(vst[:, :, 0:128], po[:, nqk * 128:512].rearrange("p (h d) -> p h d", h=nv), [r_po], [r_vst])
                                v1 = vheads[0][1]
                                S.dma("act", v1[tok0 + t * 128:tok0 + (t + 1) * 128, vheads[0][2]:vheads[0][2] + nv, :], vst[:], reads=[r_vst])
                            W = nqk * 128
                            sq, r_sq = sq_ring[cnt["sq"] % 2]
                            ss4, r_ss4 = ss4_ring[cnt["sq"] % 3]
                            cnt["sq"] += 1
                            S.op("act", lambda e, sq=sq, po=po, W=W: e.activation(sq[:, 0:W], po[:, 0:W], AF.Square, scale=128 ** -0.5),
                                 reads=[r_po], writes=[r_sq])
                            S.op("dve", lambda e, sq=sq, ss4=ss4, nqk=nqk, W=W: e.reduce_sum(ss4[:, 0:nqk], sq[:, 0:W].rearrange("p (h d) -> p h d", h=nqk), AX.X),
                                 reads=[r_sq], writes=[r_ss4])
                            S.op("dve", lambda e, ss4=ss4, nqk=nqk: e.tensor_scalar(ss4[:, 0:nqk], ss4[:, 0:nqk], EPS, None, ALU.add), reads=[r_ss4], writes=[r_ss4])
                            S.op("act", lambda e, ss4=ss4, nqk=nqk: e.activation(ss4[:, 0:nqk], ss4[:, 0:nqk], AF.Sqrt), reads=[r_ss4], writes=[r_ss4])
                            S.op("dve", lambda e, ss4=ss4, nqk=nqk: e.reciprocal(ss4[:, 0:nqk], ss4[:, 0:nqk]), reads=[r_ss4], writes=[r_ss4])
                            xn, r_xn = xn_ring[cnt["xn"] % 2]
                            tt, r_tt = tt_ring[cnt["xn"] % 2]
                            xr, r_xr = xr_ring[cnt["xn"] % 2]
                            cnt["xn"] += 1
                            for hh in range(nqk):
                                gbx, r_gbx = (gqb, r_gqb) if heads[hh][0] == "q" else (gkb, r_gkb)
                                S.op("dve", lambda e, xn=xn, po=po, ss4=ss4, hh=hh, gbx=gbx: e.scalar_tensor_tensor(
                                    xn[:, hh, :], po[:, hh * 128:(hh + 1) * 128], ss4[:, hh:hh + 1], gbx[:], ALU.mult, ALU.mult),
                                    reads=[r_po, r_ss4, r_gbx], writes=[r_xn])
                            cs, r_cs = cs_ring[cnt["cs"] % 3]
                            cnt["cs"] += 1
                            S.dma("sp", cs[:, 0, :], cos_d[tok0 + t * 128:tok0 + (t + 1) * 128, :], writes=[r_cs])
                            S.dma("sp", cs[:, 1, :], sin_d[tok0 + t * 128:tok0 + (t + 1) * 128, :], writes=[r_cs])
                            x0 = xn[:, 0:nqk, 0:128:2]
                            x1 = xn[:, 0:nqk, 1:128:2]
                            cb = cs[:, 0:1, :].broadcast_to([128, nqk, 64])
                            sb_ = cs[:, 1:2, :].broadcast_to([128, nqk, 64])
                            S.op("dve", lambda e, tt=tt, x0=x0, cb=cb, nqk=nqk: e.tensor_tensor(tt[:, 0, 0:nqk, :], x0, cb, ALU.mult), reads=[r_xn, r_cs], writes=[r_tt])
                            S.op("dve", lambda e, tt=tt, x1=x1, sb_=sb_, nqk=nqk: e.tensor_tensor(tt[:, 1, 0:nqk, :], x1, sb_, ALU.mult), reads=[r_xn, r_cs], writes=[r_tt])
                            S.op("dve", lambda e, tt=tt, x0=x0, sb_=sb_, nqk=nqk: e.tensor_tensor(tt[:, 2, 0:nqk, :], x0, sb_, ALU.mult), reads=[r_xn, r_cs], writes=[r_tt])
                            S.op("dve", lambda e, tt=tt, x1=x1, cb=cb, nqk=nqk: e.tensor_tensor(tt[:, 3, 0:nqk, :], x1, cb, ALU.mult), reads=[r_xn, r_cs], writes=[r_tt])
                            S.op("dve", lambda e, tt=tt, xr=xr, nqk=nqk: e.tensor_tensor(xr[:, 0:nqk, 0:128:2], tt[:, 0, 0:nqk, :], tt[:, 1, 0:nqk, :], ALU.subtract), reads=[r_tt], writes=[r_xr])
                            S.op("dve", lambda e, tt=tt, xr=xr, nqk=nqk: e.tensor_tensor(xr[:, 0:nqk, 1:128:2], tt[:, 2, 0:nqk, :], tt[:, 3, 0:nqk, :], ALU.add), reads=[r_tt], writes=[r_xr])
                            pTc, r_pTc = pTc_ring[cnt["pTc"] % 2]
                            cnt["pTc"] += 1

                            def tr(e, pTc=pTc, xr=xr, nqk=nqk):
                                for hh in range(nqk):
                                    i = e.transpose(pTc[:, hh, :], xr[:, hh, :], self.ident[:])
                                return i
                            S.op("pe", tr, reads=[r_xr, self.r_ident], writes=[r_pTc])
                            for hh in range(nqk):
                                evac(hsts[hh][0][:, t * 128:(t + 1) * 128], pTc[:, hh, :], [r_pTc], [hsts[hh][1]])
                        for hh in range(nqk):
                            S.dma("act", self.qt[heads[hh][1], :, tok0:tok0 + 2048], hsts[hh][0][:], reads=[hsts[hh][1]])
            return self.phase_end("proj")

    def attn_full(self, units, dv, bias=None, finish=None, extra_setup=None):
        S = self.S
        with ExitStack() as st:
            vt_ring = self.sb(st, "vt", [128, NT, dv + 1], BF16, 2)
            kt_ring = self.sb(st, "kt", [128, S_LEN], BF16, 2)
            q_ring = self.sb(st, "qsb", [128, 512], BF16, 2)
            pt_ring = self.sb(st, "pt", [128, 512], BF16, 3)
            pss_ring = self.ps(st, "pss", [128, 512], F32, 3)
            acc_ring = self.ps(st, "acc", [128, dv + 1], F32, 4)
            if bias is not None:
                band_d, constb_d = bias
                band_ring = self.sb(st, "band", [128, 2176], F32, 2)
                tmp_ring = self.sb(st, "tmpb", [128, 512], F32, 2)
                nhb = band_d.shape[0]
                (cb, r_cb), = self.sb(st, "constb", [128, nhb * 2], F32)
                S.dma("sp", cb[:], constb_d, writes=[r_cb])
            ctx = extra_setup(st) if extra_setup is not None else None
            cnt = {"vt": 0, "kt": 0, "q": 0, "pt": 0, "pss": 0, "band": 0, "tmp": 0}
            cur_v = None
            cur_band = None
            for u in units:
                vkey = (id(u["v"][0]), u["v"][1])
                if vkey != cur_v:
                    vt, r_vt = vt_ring[cnt["vt"] % 2]
                    cnt["vt"] += 1
                    v1, hidx = u["v"]
                    for part in range(4):
                        S.dma("pool", vt[:, part * 8:(part + 1) * 8, :],
                              v1[part * 1024:(part + 1) * 1024, hidx, :].rearrange("(t p) c -> p t c", p=128), writes=[r_vt])
                    cur_v = vkey
                kt, r_kt = kt_ring[cnt["kt"] % 2]
                cnt["kt"] += 1
                S.dma("sp", kt[:], self.qt[u["kid"]], writes=[r_kt])
                bh = u.get("bias_h")
                if bias is not None and bh != cur_band:
                    band, r_band = band_ring[cnt["band"] % 2]
                    cnt["band"] += 1
                    S.dma("sp", band[:], band_d[bh], writes=[r_band])
                    cur_band = bh
                for qi in range(8):
                    qsb, r_q = q_ring[cnt["q"] % 2]
                    cnt["q"] += 1
                    S.dma("sp", qsb[:], self.qt[u["qid"], :, qi * 512:(qi + 1) * 512], writes=[r_q])
                    pss_of = {}

                    def issue_qk(ki, kt=kt, r_kt=r_kt, qsb=qsb, r_q=r_q):
                        pss, r_pss = pss_ring[cnt["pss"] % 3]
                        cnt["pss"] += 1
                        pss_of[ki] = (pss, r_pss)
                        S.op("pe", lambda e, pss=pss, ki=ki: e.matmul(pss[:], kt[:, ki * 128:(ki + 1) * 128], qsb[:], start=True, stop=True),
                             reads=[r_kt, r_q], writes=[r_pss])
                    issue_qk(0)
                    issue_qk(1)
                    for ki in range(NT):
                        pss, r_pss = pss_of.pop(ki)
                        pt, r_pt = pt_ring[cnt["pt"] % 3]
                        cnt["pt"] += 1
                        if bias is None:
                            S.op("act", lambda e, pt=pt, pss=pss: e.activation(pt[:], pss[:], AF.Exp, scale=SCALE), reads=[r_pss], writes=[r_pt])
                        else:
                            delta = ki * 128 - qi * 512
                            if delta >= 1070:
                                S.op("act", lambda e, pt=pt, pss=pss, bh=bh: e.activation(pt[:], pss[:], AF.Exp, bias=cb[:, 2 * bh + 1:2 * bh + 2], scale=SCALE),
                                     reads=[r_pss, r_cb], writes=[r_pt])
                            elif delta <= -686:
                                S.op("act", lambda e, pt=pt, pss=pss, bh=bh: e.activation(pt[:], pss[:], AF.Exp, bias=cb[:, 2 * bh:2 * bh + 1], scale=SCALE),
                                     reads=[r_pss, r_cb], writes=[r_pt])
                            else:
                                s0 = 1024 - delta
                                tmp, r_tmp = tmp_ring[cnt["tmp"] % 2]
                                cnt["tmp"] += 1
                                S.op("dve", lambda e, tmp=tmp, pss=pss, band=band, s0=s0: e.scalar_tensor_tensor(tmp[:], pss[:], SCALE, band[:, s0:s0 + 512], ALU.mult, ALU.add),
                                     reads=[r_pss, r_band], writes=[r_tmp])
                                S.op("act", lambda e, pt=pt, tmp=tmp: e.activation(pt[:], tmp[:], AF.Exp), reads=[r_tmp], writes=[r_pt])
                        if ki + 2 < NT:
                            issue_qk(ki + 2)
                        for qs in range(4):
                            acc, r_acc = acc_ring[qs]
                            S.op("pe", lambda e, acc=acc, pt=pt, vt=vt, qs=qs, ki=ki: e.matmul(acc[:], pt[:, qs * 128:(qs + 1) * 128], vt[:, ki, :], start=(ki == 0), stop=(ki == NT - 1)),
                                 reads=[r_pt, r_vt], writes=[r_acc])
                    finish(st, ctx, u, qi, acc_ring)
            return self.phase_end("attn_full")

    def attn_B(self, v1b, band_d, constb_d, lq1, lk1, lq2, lk2, subln, lambda_init):
        S = self.S
        units = []
        for h in range(4):
            for m in range(2):
                units.append(dict(qid=48 + h * 2 + m, kid=56 + h * 2 + m, v=(v1b, h), bias_h=h, h=h, m=m))

        def setup(st):
            c = {}
            (lv, r_lv), = self.sb(st, "lv", [128, 4, 128], F32)
            for i, a in enumerate((lq1, lk1, lq2, lk2)):
                S.dma("sp", lv[:, i, :], a.partition_broadcast(128), writes=[r_lv])
            (pr, r_pr), = self.sb(st, "lpr", [128, 2, 128], F32)
            (sv, r_sv), = self.sb(st, "lsv", [128, 2], F32)
            (nlam, r_nlam), = self.sb(st, "nlam", [128, 1], F32)
            S.op("dve", lambda e: e.tensor_tensor(pr[:, 0, :], lv[:, 0, :], lv[:, 1, :], ALU.mult), reads=[r_lv], writes=[r_pr])
            S.op("dve", lambda e: e.tensor_tensor(pr[:, 1, :], lv[:, 2, :], lv[:, 3, :], ALU.mult), reads=[r_lv], writes=[r_pr])
            S.op("dve", lambda e: e.reduce_sum(sv[:], pr[:], AX.X), reads=[r_pr], writes=[r_sv])
            S.op("act", lambda e: e.activation(sv[:], sv[:], AF.Exp), reads=[r_sv], writes=[r_sv])
            S.op("dve", lambda e: e.tensor_tensor(nlam[:], sv[:, 1:2], sv[:, 0:1], ALU.subtract), reads=[r_sv], writes=[r_nlam])
            S.op("dve", lambda e: e.tensor_scalar(nlam[:], nlam[:], -lambda_init, None, ALU.add), reads=[r_nlam], writes=[r_nlam])
            (sg, r_sg), = self.sb(st, "subg", [128, 256], F32)
            S.dma("sp", sg[:], subln.partition_broadcast(128), writes=[r_sg])
            S.op("dve", lambda e: e.tensor_scalar(sg[:], sg[:], 1.0 - lambda_init, None, ALU.mult), reads=[r_sg], writes=[r_sg])
            (o0, r_o0), = self.sb(st, "o0", [128, 32, 256], F32)
            c["nlam"] = (nlam, r_nlam)
            c["sg"] = (sg, r_sg)
            c["o0"] = (o0, r_o0)
            c["rec"] = self.sb(st, "rec", [128, 1], F32, 4)
            c["o1"] = self.sb(st, "o1", [128, 256], F32, 2)
            c["dd"] = self.sb(st, "dd", [128, 256], F32, 2)
            c["jk"] = self.sb(st, "jkb", [128, 256], BF16, 1)[0]
            c["ss"] = self.sb(st, "ssb", [128, 1], F32, 4)
            c["ost"] = self.sb(st, "ostb", [128, 4, 256], BF16, 2)
            c["n"] = 0
            return c

        def finish(st, c, u, qi, acc_ring):
            h, m = u["h"], u["m"]
            o0, r_o0 = c["o0"]
            if m == 1:
                ost, r_ost = c["ost"][qi % 2]
            for qs in range(4):
                acc, r_acc = acc_ring[qs]
                c["n"] += 1
                rec, r_rec = c["rec"][c["n"] % 4]
                S.op("dve", lambda e, rec=rec, acc=acc: e.reciprocal(rec[:], acc[:, 256:257]), reads=[r_acc], writes=[r_rec])
                if m == 0:
                    S.op("dve", lambda e, acc=acc, rec=rec, qi=qi, qs=qs: e.tensor_scalar(o0[:, qi * 4 + qs, :], acc[:, 0:256], rec[:, 0:1], None, ALU.mult),
                         reads=[r_acc, r_rec], writes=[r_o0])
                else:
                    o1, r_o1 = c["o1"][c["n"] % 2]
                    dd, r_dd = c["dd"][c["n"] % 2]
                    ss, r_ss = c["ss"][c["n"] % 4]
                    jk, r_jk = c["jk"]
                    nlam, r_nlam = c["nlam"]
                    sg, r_sg = c["sg"]
                    S.op("dve", lambda e, o1=o1, acc=acc, rec=rec: e.tensor_scalar(o1[:], acc[:, 0:256], rec[:, 0:1], None, ALU.mult), reads=[r_acc, r_rec], writes=[r_o1])
                    S.op("dve", lambda e, dd=dd, o1=o1, qi=qi, qs=qs: e.scalar_tensor_tensor(dd[:], o1[:], nlam[:, 0:1], o0[:, qi * 4 + qs, :], ALU.mult, ALU.add),
                         reads=[r_o1, r_nlam, r_o0], writes=[r_dd])
                    S.op("act", lambda e, jk=jk, dd=dd, ss=ss: e.activation(jk[:], dd[:], AF.Square, scale=1.0 / 16.0, accum_out=ss[:]), reads=[r_dd], writes=[r_jk, r_ss])
                    S.op("dve", lambda e, ss=ss: e.tensor_scalar(ss[:], ss[:], EPS, None, ALU.add), reads=[r_ss], writes=[r_ss])
                    S.op("act", lambda e, ss=ss: e.activation(ss[:], ss[:], AF.Sqrt), reads=[r_ss], writes=[r_ss])
                    S.op("dve", lambda e, ss=ss: e.reciprocal(ss[:], ss[:]), reads=[r_ss], writes=[r_ss])
                    S.op("dve", lambda e, ost=ost, dd=dd, ss=ss, qs=qs: e.scalar_tensor_tensor(ost[:, qs, :], dd[:], ss[:, 0:1], sg[:], ALU.mult, ALU.mult),
                         reads=[r_dd, r_ss, r_sg], writes=[r_ost])
            if m == 1:
                S.dma("pool", self.mixed[qi * 512:(qi + 1) * 512, 1024 + h * 256:1024 + (h + 1) * 256].rearrange("(s p) c -> p s c", p=128),
                      ost[:], reads=[r_ost])

        return self.attn_full(units, 256, bias=(band_d, constb_d), finish=finish, extra_setup=setup)

    def attn_C(self, v1c):
        S = self.S
        units = []
        for n in range(2):
            for g in range(4):
                h = n * 4 + g
                units.append(dict(qid=h, kid=8 + n, v=(v1c, n), bias_h=None, h=h))

        def setup(st):
            c = {}
            c["rec"] = self.sb(st, "rec", [128, 1], F32, 4)
            c["ost"] = self.sb(st, "ostc", [128, 4, 128], BF16, 2)
            c["n"] = 0
            return c

        def finish(st, c, u, qi, acc_ring):
            h = u["h"]
            ost, r_ost = c["ost"][qi % 2]
            for qs in range(4):
                acc, r_acc = acc_ring[qs]
                c["n"] += 1
                rec, r_rec = c["rec"][c["n"] % 4]
                S.op("dve", lambda e, rec=rec, acc=acc: e.reciprocal(rec[:], acc[:, 128:129]), reads=[r_acc], writes=[r_rec])
                S.op("dve", lambda e, acc=acc, rec=rec, ost=ost, qs=qs: e.tensor_scalar(ost[:, qs, :], acc[:, 0:128], rec[:, 0:1], None, ALU.mult),
                     reads=[r_acc, r_rec], writes=[r_ost])
            S.dma("pool", self.mixed[qi * 512:(qi + 1) * 512, h * 128:(h + 1) * 128].rearrange("(s p) c -> p s c", p=128),
                  ost[:], reads=[r_ost])

        return self.attn_full(units, 128, bias=None, finish=finish, extra_setup=setup)

    def attn_A(self, v1a, mba_d, acca):
        S = self.S
        with ExitStack() as st:
            qn_ring = self.sb(st, "qn", [128, S_LEN], BF16, 2)
            kn_ring = self.sb(st, "kn", [128, S_LEN], BF16, 2)
            qp_ring = self.sb(st, "qp", [128, S_LEN], BF16, 2)
            kp_ring = self.sb(st, "kp", [128, 6144], BF16, 2)
            mb_ring = self.sb(st, "mb", [128, 2, 128], F32, 2)
            vt_ring = self.sb(st, "vta", [128, 48, 129], BF16, 2)
            tmp_ring = self.sb(st, "tmpa", [128, 2, 128], F32, 3)
            pt_ring = self.sb(st, "pta", [128, 2, 128], BF16, 3)
            ost_ring = self.sb(st, "osta", [128, 32, 129], F32, 2)
            pss_ring = self.ps(st, "pssa", [128, 2, 128], F32, 3)
            acc_ring = self.ps(st, "acca", [128, 129], F32, 3)
            n_u = 0
            cntA = {"pss": 0, "b": 0, "t": 0}
            for g in range(3):
                r = DIL[g]
                L = S_LEN // r
                nb = L // 128
                for h in range(8):
                    qn, r_qn = qn_ring[n_u % 2]
                    kn, r_kn = kn_ring[n_u % 2]
                    qp, r_qp = qp_ring[n_u % 2]
                    kp, r_kp = kp_ring[n_u % 2]
                    mb, r_mb = mb_ring[n_u % 2]
                    vtt, r_vt = vt_ring[n_u % 2]
                    ost, r_ost = ost_ring[n_u % 2]
                    n_u += 1
                    S.dma("sp", qn[:], self.qt[g * 16 + h], writes=[r_qn])
                    S.dma("sp", kn[:], self.qt[g * 16 + 8 + h], writes=[r_kn])
                    S.dma("sp", mb[:], mba_d[g * 8 + h], writes=[r_mb])
                    qpv = qp[:].rearrange("d (r n) -> d r n", r=r)
                    kpv = kp[:, 0:r * (L + 128)].rearrange("d (r n) -> d r n", r=r)
                    S.op("pool", lambda e, qpv=qpv, qn=qn, r=r: e.tensor_copy(qpv, qn[:].rearrange("d (n r) -> d r n", r=r)), reads=[r_qn], writes=[r_qp])

                    def kperm(e, kpv=kpv, kn=kn, r=r, L=L):
                        e.memset(kpv[:, :, 0:64], 0.0)
                        e.memset(kpv[:, :, 64 + L:128 + L], 0.0)
                        return e.tensor_copy(kpv[:, :, 64:64 + L], kn[:].rearrange("d (n r) -> d r n", r=r))
                    S.op("pool", kperm, reads=[r_kn], writes=[r_kp])
                    vsrc = v1a[g, :, h, :].rearrange("(n r) c -> r n c", r=r)
                    vt = vtt[:, 0:r * (nb + 1), :].rearrange("p (r i) c -> p r i c", r=r)

                    def vz(e, vt=vt, nb=nb):
                        e.memset(vt[0:64, :, 0, :], 0.0)
                        return e.memset(vt[64:128, :, nb, :], 0.0)
                    S.op("pool", vz, writes=[r_vt])
                    S.dma("sp", vt[64:128, :, 0, :], vsrc[:, 0:64, :].rearrange("r p c -> p r c"), writes=[r_vt])
                    S.dma("sp", vt[0:64, :, nb, :], vsrc[:, L - 64:L, :].rearrange("r p c -> p r c"), writes=[r_vt])
                    for rho in range(r):
                        S.dma("sp", vt[:, rho, 1:nb, :], vsrc[rho, 64:L - 64, :].rearrange("(i p) c -> p i c", p=128), writes=[r_vt])
                    blocks = [(rho, b) for rho in range(r) for b in range(nb)]
                    pss_of = {}

                    def issue_qk(n, kpv=kpv, qpv=qpv, r_kp=r_kp, r_qp=r_qp, blocks=blocks, pss_of=pss_of):
                        rho, b = blocks[n]
                        pss, r_pss = pss_ring[cntA["pss"] % 3]
                        cntA["pss"] += 1
                        pss_of[n] = (pss, r_pss)

                        def qk(e, pss=pss, rho=rho, b=b):
                            e.matmul(pss[:, 0, :], kpv[:, rho, b * 128:(b + 1) * 128], qpv[:, rho, b * 128:(b + 1) * 128], start=True, stop=True)
                            return e.matmul(pss[:, 1, :], kpv[:, rho, (b + 1) * 128:(b + 2) * 128], qpv[:, rho, b * 128:(b + 1) * 128], start=True, stop=True)
                        S.op("pe", qk, reads=[r_kp, r_qp], writes=[r_pss])
                    tmp_of = {}

                    def issue_stt(n, mb=mb, r_mb=r_mb, pss_of=pss_of, tmp_of=tmp_of):
                        pss, r_pss = pss_of.pop(n)
                        tmp, r_tmp = tmp_ring[cntA["t"] % 3]
                        cntA["t"] += 1
                        tmp_of[n] = (tmp, r_tmp)
                        S.op("dve", lambda e, tmp=tmp, pss=pss: e.scalar_tensor_tensor(tmp[:], pss[:], SCALE, mb[:], ALU.mult, ALU.add),
                             reads=[r_pss, r_mb], writes=[r_tmp])
                    issue_qk(0)
                    issue_qk(1)
                    issue_stt(0)
                    for n, (rho, b) in enumerate(blocks):
                        tmp, r_tmp = tmp_of.pop(n)
                        pt, r_pt = pt_ring[cntA["b"] % 3]
                        acc, r_acc = acc_ring[cntA["b"] % 3]
                        cntA["b"] += 1
                        S.op("act", lambda e, pt=pt, tmp=tmp: e.activation(pt[:], tmp[:], AF.Exp), reads=[r_tmp], writes=[r_pt])
                        if n + 2 < len(blocks):
                            issue_qk(n + 2)
                        if n + 1 < len(blocks):
                            issue_stt(n + 1)

                        def pv(e, acc=acc, pt=pt, vt=vt, b=b, rho=rho):
                            e.matmul(acc[:], pt[:, 0, :], vt[:, rho, b, :], start=True, stop=False)
                            return e.matmul(acc[:], pt[:, 1, :], vt[:, rho, b + 1, :], start=False, stop=True)
                        S.op("pe", pv, reads=[r_pt, r_vt], writes=[r_acc])
                        S.op("dve", lambda e, ost=ost, acc=acc, n=n: e.tensor_copy(ost[:, n, :], acc[:]), reads=[r_acc], writes=[r_ost])
                    for rho in range(r):
                        dst = acca[g].rearrange("(b p r) h c -> r p b h c", p=128, r=r)[rho][:, :, h, :]
                        S.dma("act", dst, ost[:, rho * nb:(rho + 1) * nb, :], reads=[r_ost])
            if self.phase_end("attn_a"):
                return True
        with ExitStack() as st:
            a_ring = self.sb(st, "ca", [128, 3, 8, 129], F32, 2)
            s_ring = self.sb(st, "cs_", [128, 8, 129], F32, 2)
            rc_ring = self.sb(st, "crc", [128, 8], F32, 2)
            o_ring = self.sb(st, "co", [128, 8, 128], BF16, 2)
            for t in range(NT):
                a, r_a = a_ring[t % 2]
                sm, r_sm = s_ring[t % 2]
                rc, r_rc = rc_ring[t % 2]
                o, r_o = o_ring[t % 2]
                for g in range(3):
                    S.dma("sp", a[:, g, :, :], acca[g, t * 128:(t + 1) * 128, :, :], writes=[r_a])
                S.op("dve", lambda e, sm=sm, a=a: e.tensor_tensor(sm[:], a[:, 0, :, :], a[:, 1, :, :], ALU.add), reads=[r_a], writes=[r_sm])
                S.op("dve", lambda e, sm=sm, a=a: e.tensor_tensor(sm[:], sm[:], a[:, 2, :, :], ALU.add), reads=[r_a, r_sm], writes=[r_sm])
                S.op("dve", lambda e, sm=sm, rc=rc: e.reciprocal(rc[:], sm[:, :, 128]), reads=[r_sm], writes=[r_rc])
                for h in range(8):
                    eng = "dve" if h % 2 else "pool"
                    S.op(eng, lambda e, o=o, sm=sm, rc=rc, h=h: e.tensor_scalar(o[:, h, :], sm[:, h, 0:128], rc[:, h:h + 1], None, ALU.mult),
                         reads=[r_sm, r_rc], writes=[r_o])
                S.dma("act", self.mixed[t * 128:(t + 1) * 128, 0:1024], o[:].rearrange("p h d -> p (h d)"), reads=[r_o])
            return self.phase_end("comb_a")

    def outproj_phase(self, h_src, wb_out, h_dst, wname=None):
        S = self.S
        wres = self.wres[wname]
        with ExitStack() as st:
            bufs = self.load_T_bufs(st, False, BF16)
            (mT, r_mT), = self.sb(st, "mT", [128, 16, 2048], BF16)
            wt_ring = self.sb(st, "wto", [128, 16, 512], BF16, 2)
            po_ring = self.ps(st, "poo", [128, 512], F32, 4)
            hr_ring = self.sb(st, "hr", [128, 512], F32, 3)
            n_w = 0
            n_p = 0
            for c in range(2):
                tok0 = c * 2048
                self.load_T(st, self.mixed, tok0, 2048, mT, r_mT, 0, bufs=bufs)
                for s in range(4):
                    wt, r_wt = wt_ring[n_w % 2]
                    n_w += 1
                    S.dma("sp", wt[:], wb_out[s], reads=wres[s], writes=[r_wt])
                    for t in range(16):
                        po, r_po = po_ring[n_p % 4]
                        hr, r_hr = hr_ring[n_p % 3]
                        n_p += 1
                        rows = slice(tok0 + t * 128, tok0 + (t + 1) * 128)
                        S.dma("sp", hr[:], h_src[rows, s * 512:(s + 1) * 512], writes=[r_hr])

                        def mm(e, po=po, wt=wt, t=t):
                            for k in range(16):
                                i = e.matmul(po[:], mT[:, k, t * 128:(t + 1) * 128], wt[:, k, :], start=(k == 0), stop=(k == 15))
                            return i
                        S.op("pe", mm, reads=[r_wt, r_mT], writes=[r_po])
                        S.op("dve", lambda e, hr=hr, po=po: e.tensor_tensor(hr[:], hr[:], po[:], ALU.add), reads=[r_po, r_hr], writes=[r_hr])
                        S.dma("act", h_dst[rows, s * 512:(s + 1) * 512], hr[:], reads=[r_hr])
            return self.phase_end("outproj")

    def ffn_phase(self, h_src, gain_ap, wb_up, wb_dn, conv_w, conv_b, h_dst, wn_up=None, wn_dn=None):
        S = self.S
        NFC = DFF // 128
        wres_up = self.wres[wn_up]
        wres_dn = self.wres[wn_dn]
        with ExitStack() as st:
            bufs = self.load_T_bufs(st, True, F32, npT=1)
            gb, r_gb = self.load_gain(st, gain_ap)
            yT_ring = self.sb(st, "y2T", [128, 16, 514], BF16, 1)
            guT_ring = self.sb(st, "guT", [128, NFC, 512], BF16, 1)
            wg_ring = self.sb(st, "wg", [128, 16, 256], BF16, 2)
            wu_ring = self.sb(st, "wu", [128, 16, 256], BF16, 2)
            wd_ring = self.sb(st, "wd", [128, NFC, 256], BF16, 2)
            (cw, r_cw), = self.sb(st, "cw", [128, 4, NFC], F32)
            for i in range(3):
                S.dma("sp", cw[:, i, :], conv_w[i].rearrange("(c p) -> p c", p=128), writes=[r_cw], allow_slow_non_contiguous=True)
            S.dma("sp", cw[:, 3, :], conv_b.rearrange("(c p) -> p c", p=128), writes=[r_cw], allow_slow_non_contiguous=True)
            gs_ring = self.sb(st, "gs", [128, 516], F32, 2)
            t1_ring = self.sb(st, "t1", [128, 512], F32, 2)
            t2_ring = self.sb(st, "t2", [128, 512], F32, 2)
            gl_ring = self.sb(st, "gl", [128, 512], F32, 2)
            hr_ring = self.sb(st, "hrf", [128, 256], F32, 3)
            psg_ring = self.ps(st, "psg", [128, 512], F32, 2)
            psu_ring = self.ps(st, "psu", [128, 512], F32, 2)
            psh_ring = self.ps(st, "psh", [128, 2], F32, 1)
            psd_ring = self.ps(st, "psd", [128, 256], F32, 2)
            n = {"w": 0, "f": 0, "wd": 0, "d": 0}
            for c in range(S_LEN // 512):
                c0 = c * 512
                yT, r_yT = yT_ring[0]
                guT, r_guT = guT_ring[0]
                self.load_T(st, h_src, c0, 512, yT, r_yT, 0, gain_b=gb, r_gain=r_gb, bufs=bufs)
                self.load_T(st, h_src, c0 - 1, 1, yT, r_yT, 512, gain_b=gb, r_gain=r_gb, bufs=bufs)
                self.load_T(st, h_src, c0 + 512, 1, yT, r_yT, 513, gain_b=gb, r_gain=r_gb, bufs=bufs)
                for sl in range(22):
                    wg, r_wg = wg_ring[n["w"] % 2]
                    wu, r_wu = wu_ring[n["w"] % 2]
                    n["w"] += 1
                    S.dma("sp", wg[:], wb_up[sl], reads=wres_up[sl], writes=[r_wg])
                    S.dma("sp", wu[:], wb_up[22 + sl], reads=wres_up[22 + sl], writes=[r_wu])
                    for j in range(2):
                        fc = sl * 2 + j
                        psg, r_psg = psg_ring[n["f"] % 2]
                        psu, r_psu = psu_ring[n["f"] % 2]
                        psh, r_psh = psh_ring[0]
                        gs, r_gs = gs_ring[n["f"] % 2]
                        t1, r_t1 = t1_ring[n["f"] % 2]
                        t2, r_t2 = t2_ring[n["f"] % 2]
                        gl, r_gl = gl_ring[n["f"] % 2]
                        n["f"] += 1

                        def mmg(e, psg=psg, psh=psh, wg=wg, j=j):
                            for k in range(16):
                                e.matmul(psg[:], wg[:, k, j * 128:(j + 1) * 128], yT[:, k, 0:512], start=(k == 0), stop=(k == 15))
                            for k in range(16):
                                i = e.matmul(psh[:], wg[:, k, j * 128:(j + 1) * 128], yT[:, k, 512:514], start=(k == 0), stop=(k == 15))
                            return i
                        S.op("pe", mmg, reads=[r_wg, r_yT], writes=[r_psg, r_psh])

                        def mmu(e, psu=psu, wu=wu, j=j):
                            for k in range(16):
                                i = e.matmul(psu[:], wu[:, k, j * 128:(j + 1) * 128], yT[:, k, 0:512], start=(k == 0), stop=(k == 15))
                            return i
                        S.op("pe", mmu, reads=[r_wu, r_yT], writes=[r_psu])
                        S.op("act", lambda e, gs=gs, psg=psg: e.activation(gs[:, 1:513], psg[:], AF.Copy), reads=[r_psg], writes=[r_gs])

                        def halo(e, gs=gs, psh=psh):
                            e.tensor_copy(gs[:, 0:1], psh[:, 0:1])
                            return e.tensor_copy(gs[:, 513:514], psh[:, 1:2])
                        S.op("dve", halo, reads=[r_psh], writes=[r_gs])
                        S.op("act", lambda e, t1=t1, gs=gs, fc=fc: e.activation(t1[:], gs[:, 1:513], AF.Identity, bias=cw[:, 3, fc:fc + 1], scale=cw[:, 1, fc:fc + 1]),
                             reads=[r_gs, r_cw], writes=[r_t1])
                        S.op("dve", lambda e, t2=t2, gs=gs, t1=t1, fc=fc: e.scalar_tensor_tensor(t2[:], gs[:, 0:512], cw[:, 0, fc:fc + 1], t1[:], ALU.mult, ALU.add),
                             reads=[r_gs, r_t1, r_cw], writes=[r_t2])
                        S.op("dve", lambda e, t1=t1, gs=gs, t2=t2, fc=fc: e.scalar_tensor_tensor(t1[:], gs[:, 2:514], cw[:, 2, fc:fc + 1], t2[:], ALU.mult, ALU.add),
                             reads=[r_gs, r_t2, r_cw], writes=[r_t1])
                        S.op("act", lambda e, gl=gl, t1=t1: e.activation(gl[:], t1[:], AF.Gelu_apprx_tanh), reads=[r_t1], writes=[r_gl])
                        S.op("dve", lambda e, gl=gl, psu=psu, fc=fc: e.tensor_tensor(guT[:, fc, :], gl[:], psu[:], ALU.mult), reads=[r_gl, r_psu], writes=[r_guT])
                for ds in range(8):
                    wd, r_wd = wd_ring[n["wd"] % 2]
                    n["wd"] += 1
                    S.dma("sp", wd[:, 0:22, :], wb_dn[ds, :, 0:22, :], reads=wres_dn[ds], writes=[r_wd])
                    S.dma("sp", wd[:, 22:44, :], wb_dn[ds, :, 22:44, :], reads=wres_dn[ds], writes=[r_wd])
                    for tt in range(4):
                        psd, r_psd = psd_ring[n["d"] % 2]
                        hr, r_hr = hr_ring[n["d"] % 3]
                        n["d"] += 1
                        rows = slice(c0 + tt * 128, c0 + (tt + 1) * 128)
                        S.dma("sp", hr[:], h_src[rows, ds * 256:(ds + 1) * 256], writes=[r_hr])

                        def mmd(e, psd=psd, wd=wd, tt=tt):
                            for k in range(NFC):
                                i = e.matmul(psd[:], guT[:, k, tt * 128:(tt + 1) * 128], wd[:, k, :], start=(k == 0), stop=(k == NFC - 1))
                            return i
                        S.op("pe", mmd, reads=[r_wd, r_guT], writes=[r_psd])
                        S.op("dve", lambda e, hr=hr, psd=psd: e.tensor_tensor(hr[:], hr[:], psd[:], ALU.add), reads=[r_psd, r_hr], writes=[r_hr])
                        S.dma("act", h_dst[rows, ds * 256:(ds + 1) * 256], hr[:], reads=[r_hr])
            return self.phase_end("ffn")

    def final_norm(self, h_src, gain_ap, out):
        S = self.S
        with ExitStack() as st:
            gb, r_gb = self.load_gain(st, gain_ap)
            ht_ring = self.sb(st, "fht", [128, D], F32, 3)
            junk = self.sb(st, "fjk", [128, D], BF16, 1)[0]
            ssr = self.sb(st, "fss", [128, 1], F32, 4)
            for t in range(NT):
                ht, r_ht = ht_ring[t % 3]
                ss, r_ss = ssr[t % 4]
                jk, r_jk = junk
                S.dma("sp", ht[:], h_src[t * 128:(t + 1) * 128, :], writes=[r_ht])
                S.op("act", lambda e, ht=ht, jk=jk, ss=ss: e.activation(jk[:], ht[:], AF.Square, scale=D ** -0.5, accum_out=ss[:]), reads=[r_ht], writes=[r_jk, r_ss])
                S.op("dve", lambda e, ss=ss: e.tensor_scalar(ss[:], ss[:], EPS, None, ALU.add), reads=[r_ss], writes=[r_ss])
                S.op("act", lambda e, ss=ss: e.activation(ss[:], ss[:], AF.Sqrt), reads=[r_ss], writes=[r_ss])
                S.op("dve", lambda e, ss=ss: e.reciprocal(ss[:], ss[:]), reads=[r_ss], writes=[r_ss])
                S.op("dve", lambda e, ht=ht, ss=ss: e.scalar_tensor_tensor(ht[:], ht[:], ss[:, 0:1], gb[:], ALU.mult, ALU.mult), reads=[r_ht, r_ss, r_gb], writes=[r_ht])
                S.dma("act", out[t * 128:(t + 1) * 128, :], ht[:], reads=[r_ht])
            return self.phase_end("final")


def _t5_bucket_np(rel):
    nb = 16
    max_exact = 8
    rel = np.asarray(rel, dtype=np.int64)
    ret = np.where(rel > 0, nb, 0)
    n = np.abs(rel)
    n_f = np.maximum(n, 1).astype(np.float32)
    large = max_exact + (np.log(n_f / np.float32(max_exact)) / np.float32(math.log(1024 / max_exact)) * np.float32(nb - max_exact)).astype(np.int32)
    large = np.minimum(large, nb - 1)
    return ret + np.where(n < max_exact, n, large)


def host_tables(t5_table, na_rpb):
    t5 = np.asarray(t5_table, dtype=np.float32)
    out = {}
    i = np.arange(128)[:, None]
    c = np.arange(2176)[None, :]
    bk = _t5_bucket_np(i - c + 1024)
    out["bandB"] = np.ascontiguousarray(np.stack([t5[bk, 24 + h] for h in range(4)]).astype(np.float32))
    cb = np.zeros((128, 8), np.float32)
    for h in range(4):
        cb[:, 2 * h] = t5[15, 24 + h]
        cb[:, 2 * h + 1] = t5[31, 24 + h]
    out["constB"] = cb
    p = np.arange(128)[:, None, None]
    t = np.arange(2)[None, :, None]
    j = np.arange(128)[None, None, :]
    rel_sub = (p - 64 + 128 * t) - j
    valid = np.abs(rel_sub) <= 64
    mba = np.zeros((24, 128, 2, 128), np.float32)
    for g, r in enumerate((1, 4, 16)):
        bk = _t5_bucket_np(r * rel_sub)
        for h in range(8):
            mba[g * 8 + h] = np.where(valid, t5[bk, g * 8 + h], np.float32(NEG))
    out["mbA"] = mba
    rpb = np.asarray(na_rpb, dtype=np.float32)[0]
    e = (np.arange(128) // 64)[:, None, None]
    bp = (np.arange(128) % 64)[:, None, None]
    a = np.arange(4)[None, :, None]
    jj = np.arange(64)[None, None, :]
    cs = np.clip(jj - 8, 0, 48)
    validc = (bp >= cs) & (bp < cs + 16)
    dc = np.clip(bp - jj + 15, 0, 30)
    mbd = np.zeros((8, 8, 128, 4, 64), np.float32)
    for di in range(8):
        delta = -di
        dr = delta + 2 * a + e + 7
        drc = np.clip(dr, 0, 14) + 0 * jj
        for h in range(8):
            vals = rpb[h][drc, dc + 0 * a]
            mbd[h, di] = np.where(validc & (dr >= 0) & (dr <= 14), vals, np.float32(NEG))
    out["mbD"] = mbd
    tpos = np.arange(S_LEN)
    row = (tpos // 64).astype(np.float32)
    col = (tpos % 64).astype(np.float32)
    inv_freq = (np.float32(10000.0) ** (-(np.arange(0, 64, 2, dtype=np.float32) / np.float32(64)))).astype(np.float32)
    ang = np.concatenate([row[:, None] * inv_freq[None], col[:, None] * inv_freq[None]], axis=-1).astype(np.float32)
    out["ropec"] = np.cos(ang).astype(np.float32)
    out["ropes"] = np.sin(ang).astype(np.float32)
    return out


def _attn_D(self, v1d, mbd_d):
    S = self.S
    with ExitStack() as st:
        qn_ring = self.sb(st, "qd", [128, S_LEN], BF16, 2)
        kn_ring = self.sb(st, "kd", [128, S_LEN], BF16, 2)
        v0_ring = self.sb(st, "vd0", [128, 32, 129], BF16, 2)
        v1_ring = self.sb(st, "vd1", [128, 31, 129], BF16, 2)
        mb_ring = self.sb(st, "mbd", [128, 8, 4, 64], F32, 2)
        tmp_ring = self.sb(st, "tmpd", [128, 4, 64], F32, 3)
        pt_ring = self.sb(st, "ptd", [128, 4, 64], BF16, 3)
        rec_ring = self.sb(st, "recd", [64, 1], F32, 4)
        ost_ring = self.sb(st, "ostd", [64, 64, 128], BF16, 2)
        pss_ring = self.ps(st, "pssd", [128, 4, 64], F32, 3)
        acc_ring = self.ps(st, "accd", [64, 129], F32, 3)
        n_b = 0
        cntD = {"pss": 0, "t": 0}
        for h in range(8):
            qn, r_qn = qn_ring[h % 2]
            kn, r_kn = kn_ring[h % 2]
            v0, r_v0 = v0_ring[h % 2]
            v1, r_v1 = v1_ring[h % 2]
            mb, r_mb = mb_ring[h % 2]
            ost, r_ost = ost_ring[h % 2]
            S.dma("sp", qn[:], self.qt[10 + h], writes=[r_qn])
            S.dma("sp", kn[:], self.qt[18 + h], writes=[r_kn])
            S.dma("sp", mb[:], mbd_d[h].rearrange("d p a j -> p d a j"), writes=[r_mb])
            for part in range(4):
                S.dma("pool", v0[:, part * 8:(part + 1) * 8, :], v1d[part * 1024:(part + 1) * 1024, h, :].rearrange("(t p) c -> p t c", p=128), writes=[r_v0])
            S.dma("pool", v1[:, 0:16, :], v1d[64:64 + 2048, h, :].rearrange("(t p) c -> p t c", p=128), writes=[r_v1])
            S.dma("pool", v1[:, 16:31, :], v1d[64 + 2048:64 + 2048 + 1920, h, :].rearrange("(t p) c -> p t c", p=128), writes=[r_v1])
            pss_of = {}

            def issue_qk(i, kn=kn, qn=qn, r_kn=r_kn, r_qn=r_qn, pss_of=pss_of):
                rs = min(max(i - 4, 0), 56)
                pss, r_pss = pss_ring[cntD["pss"] % 3]
                cntD["pss"] += 1
                pss_of[i] = (pss, r_pss)

                def qk(e, pss=pss, rs=rs, i=i):
                    for a in range(4):
                        k0 = 64 * rs + 128 * a
                        ins = e.matmul(pss[:, a, :], kn[:, k0:k0 + 128], qn[:, 64 * i:64 * i + 64], start=True, stop=True)
                    return ins
                S.op("pe", qk, reads=[r_kn, r_qn], writes=[r_pss])
            tmp_of = {}

            def issue_stt(i, mb=mb, r_mb=r_mb, pss_of=pss_of, tmp_of=tmp_of):
                rs = min(max(i - 4, 0), 56)
                di = i - rs
                pss, r_pss = pss_of.pop(i)
                tmp, r_tmp = tmp_ring[cntD["t"] % 3]
                cntD["t"] += 1
                tmp_of[i] = (tmp, r_tmp)
                S.op("dve", lambda e, tmp=tmp, pss=pss, di=di: e.scalar_tensor_tensor(tmp[:], pss[:], SCALE, mb[:, di, :, :], ALU.mult, ALU.add),
                     reads=[r_pss, r_mb], writes=[r_tmp])
            issue_qk(0)
            issue_qk(1)
            issue_stt(0)
            for i in range(64):
                rs = min(max(i - 4, 0), 56)
                tmp, r_tmp = tmp_of.pop(i)
                pt, r_pt = pt_ring[n_b % 3]
                acc, r_acc = acc_ring[n_b % 3]
                rec, r_rec = rec_ring[n_b % 4]
                n_b += 1
                S.op("act", lambda e, pt=pt, tmp=tmp: e.activation(pt[:], tmp[:], AF.Exp), reads=[r_tmp], writes=[r_pt])
                if i + 2 < 64:
                    issue_qk(i + 2)
                if i + 1 < 64:
                    issue_stt(i + 1)
                if rs % 2 == 0:
                    vv, r_vv, tb = v0, r_v0, rs // 2
                else:
                    vv, r_vv, tb = v1, r_v1, (rs - 1) // 2

                def pv(e, acc=acc, pt=pt, vv=vv, tb=tb):
                    for a in range(4):
                        ins = e.matmul(acc[:], pt[:, a, :], vv[:, tb + a, :], start=(a == 0), stop=(a == 3))
                    return ins
                S.op("pe", pv, reads=[r_pt, r_vv], writes=[r_acc])
                S.op("dve", lambda e, rec=rec, acc=acc: e.reciprocal(rec[:], acc[:, 128:129]), reads=[r_acc], writes=[r_rec])
                S.op("dve", lambda e, ost=ost, acc=acc, rec=rec, i=i: e.tensor_scalar(ost[:, i, :], acc[:, 0:128], rec[:, 0:1], None, ALU.mult),
                     reads=[r_acc, r_rec], writes=[r_ost])
            S.dma("pool", self.mixed[:, 1024 + h * 128:1024 + (h + 1) * 128].rearrange("(i p) c -> p i c", p=64), ost[:], reads=[r_ost])
        return self.phase_end("attn_d")


KB.attn_D = _attn_D


def build(debug=(), stop_after=None, l1_only=False):
    kb = KB(debug, stop_after)
    nc = kb.nc
    S = kb.S
    I = {}
    I["x"] = kb.inp("x", [S_LEN, D])
    I["ln_mix"] = kb.inp("ln_mix", [2, D])
    I["ln_ffn"] = kb.inp("ln_ffn", [2, D])
    I["ln_final"] = kb.inp("ln_final", [D])
    I["ev_w_in"] = kb.inp("ev_w_in", [D, 12288])
    I["ev_w_out"] = kb.inp("ev_w_out", [D, D])
    for nme in ("diff_lq1", "diff_lk1", "diff_lq2", "diff_lk2"):
        I[nme] = kb.inp(nme, [128])
    I["diff_subln"] = kb.inp("diff_subln", [256])
    I["od_w_in"] = kb.inp("od_w_in", [D, 4608])
    I["od_w_out"] = kb.inp("od_w_out", [D, D])
    I["gqa_q_norm"] = kb.inp("gqa_q_norm", [128])
    I["gqa_k_norm"] = kb.inp("gqa_k_norm", [128])
    I["ffn_w_up"] = kb.inp("ffn_w_up", [2, D, 2 * DFF])
    I["ffn_conv_w"] = kb.inp("ffn_conv_w", [2, 3, DFF])
    I["ffn_conv_b"] = kb.inp("ffn_conv_b", [2, DFF])
    I["ffn_w_down"] = kb.inp("ffn_w_down", [2, DFF, D])
    I["bandB"] = kb.inp("bandB", [4, 128, 2176])
    I["constB"] = kb.inp("constB", [128, 8])
    I["mbA"] = kb.inp("mbA", [24, 128, 2, 128])
    I["mbD"] = kb.inp("mbD", [8, 8, 128, 4, 64])
    I["ropec"] = kb.inp("ropec", [S_LEN, 64])
    I["ropes"] = kb.inp("ropes", [S_LEN, 64])
    out = nc.dram_tensor("out", [S_LEN, D], F32, kind="ExternalOutput").ap()

    kb.qt = kb.dram("qt", [64, 128, S_LEN], BF16)
    kb.mixed = kb.dram("mixed", [S_LEN, D], BF16)
    v1a = kb.dram("v1a", [3, S_LEN, 8, 129], BF16)
    v1b = kb.dram("v1b", [S_LEN, 4, 257], BF16)
    v1c = kb.dram("v1c", [S_LEN, 2, 129], BF16)
    v1d = kb.dram("v1d", [S_LEN, 8, 129], BF16)
    acca = kb.dram("acca", [3, S_LEN, 8, 129], F32)
    hA = kb.dram("hA", [S_LEN, D], F32)
    hB = kb.dram("hB", [S_LEN, D], F32)

    def done():
        kb.es.close()
        return nc

    kb.setup_consts()
    if l1_only:
        hB = kb.inp("hB_in", [S_LEN, D])
    if not l1_only:
        wb_in0 = kb.precast(I["ev_w_in"], "wb_in0", D, 12288, 512)
        wb_out0 = kb.precast(I["ev_w_out"], "wb_out0", D, D, 512)
        wb_up0 = kb.precast(I["ffn_w_up"][0], "wb_up0", D, 2 * DFF, 256)
        wb_dn0 = kb.precast(I["ffn_w_down"][0], "wb_dn0", DFF, D, 256, kstep=11)
    wb_in1 = kb.precast(I["od_w_in"], "wb_in1", D, 4608, 512)
    wb_out1 = kb.precast(I["od_w_out"], "wb_out1", D, D, 512)
    wb_up1 = kb.precast(I["ffn_w_up"][1], "wb_up1", D, 2 * DFF, 256)
    wb_dn1 = kb.precast(I["ffn_w_down"][1], "wb_dn1", DFF, D, 256, kstep=11)
    if kb.stop_after == "precast":
        kb.phase_end("precast")
        return done()

    if l1_only:
        return _layer1(kb, I, hA, hB, v1c, v1d, wb_in1, wb_out1, wb_up1, wb_dn1, out, done)
    slabs0 = []
    for g in range(3):
        slabs0.append(("fm", [g * 16 + h for h in range(0, 4)]))
        slabs0.append(("fm", [g * 16 + h for h in range(4, 8)]))
        slabs0.append(("fm", [g * 16 + 8 + h for h in range(0, 4)]))
        slabs0.append(("fm", [g * 16 + 8 + h for h in range(4, 8)]))
        slabs0.append(("tmv", v1a[g], 0, 4, 128))
        slabs0.append(("tmv", v1a[g], 4, 4, 128))
    slabs0.append(("fm", [48 + i for i in range(0, 4)]))
    slabs0.append(("fm", [48 + i for i in range(4, 8)]))
    slabs0.append(("fm", [56 + i for i in range(0, 4)]))
    slabs0.append(("fm", [56 + i for i in range(4, 8)]))
    slabs0.append(("tmv", v1b, 0, 2, 256))
    slabs0.append(("tmv", v1b, 2, 2, 256))
    if kb.proj_phase(I["x"], I["ln_mix"][0], wb_in0, slabs0, wname="wb_in0"):
        return done()
    if kb.attn_A(v1a, I["mbA"], acca):
        return done()
    lambda_init0 = 0.8 - 0.6 * math.exp(-0.3 * 0)
    if kb.attn_B(v1b, I["bandB"], I["constB"], I["diff_lq1"], I["diff_lk1"], I["diff_lq2"], I["diff_lk2"], I["diff_subln"], lambda_init0):
        return done()
    if kb.outproj_phase(I["x"], wb_out0, hA, wname="wb_out0"):
        return done()
    if kb.ffn_phase(hA, I["ln_ffn"][0], wb_up0, wb_dn0, I["ffn_conv_w"][0], I["ffn_conv_b"][0], hB, wn_up="wb_up0", wn_dn="wb_dn0"):
        return done()
    return _layer1(kb, I, hA, hB, v1c, v1d, wb_in1, wb_out1, wb_up1, wb_dn1, out, done)


def _layer1(kb, I, hA, hB, v1c, v1d, wb_in1, wb_out1, wb_up1, wb_dn1, out, done):
    slabs1 = [
        ("tmc", [("q", 0), ("q", 1), ("q", 2), ("q", 3)]),
        ("tmc", [("q", 4), ("q", 5), ("q", 6), ("q", 7)]),
        ("tmc", [("k", 8), ("k", 9), ("v", v1c, 0), ("v", v1c, 1)]),
        ("fm", [10, 11, 12, 13]), ("fm", [14, 15, 16, 17]),
        ("fm", [18, 19, 20, 21]), ("fm", [22, 23, 24, 25]),
        ("tmv", v1d, 0, 4, 128), ("tmv", v1d, 4, 4, 128),
    ]
    if kb.proj_phase(hB, I["ln_mix"][1], wb_in1, slabs1, rope=(I["ropec"], I["ropes"], I["gqa_q_norm"], I["gqa_k_norm"]), wname="wb_in1"):
        return done()
    if kb.attn_C(v1c):
        return done()
    if kb.attn_D(v1d, I["mbD"]):
        return done()
    if kb.outproj_phase(hB, wb_out1, hA, wname="wb_out1"):
        return done()
    if kb.ffn_phase(hA, I["ln_ffn"][1], wb_up1, wb_dn1, I["ffn_conv_w"][1], I["ffn_conv_b"][1], hB, wn_up="wb_up1", wn_dn="wb_dn1"):
        return done()
    kb.final_norm(hB, I["ln_final"], out)
    return done()


ACTIVE = [0, 1, 4, 5]


def make_in_maps(inputs, ncores=8):
    aux = host_tables(inputs["t5_table"], inputs["na_rpb"])
    f = lambda a: np.ascontiguousarray(np.asarray(a, dtype=np.float32))
    shared = {
        "ln_mix": f(inputs["ln_mix"]), "ln_ffn": f(inputs["ln_ffn"]), "ln_final": f(inputs["ln_final"]),
        "ev_w_in": f(inputs["ev_w_in"][0]), "ev_w_out": f(inputs["ev_w_out"][0]),
        "diff_lq1": f(inputs["diff_lq1"][0]), "diff_lk1": f(inputs["diff_lk1"][0]),
        "diff_lq2": f(inputs["diff_lq2"][0]), "diff_lk2": f(inputs["diff_lk2"][0]),
        "diff_subln": f(inputs["diff_subln"][0]),
        "od_w_in": f(inputs["od_w_in"][0]), "od_w_out": f(inputs["od_w_out"][0]),
        "gqa_q_norm": f(inputs["gqa_q_norm"][0]), "gqa_k_norm": f(inputs["gqa_k_norm"][0]),
        "ffn_w_up": f(inputs["ffn_w_up"]), "ffn_conv_w": f(inputs["ffn_conv_w"]), "ffn_conv_b": f(inputs["ffn_conv_b"]),
        "ffn_w_down": f(inputs["ffn_w_down"]),
    }
    shared.update(aux)
    x = np.asarray(inputs["x"], dtype=np.float32)
    maps = []
    if ncores == 1:
        m = dict(shared)
        m["x"] = np.ascontiguousarray(x[0])
        return [m]
    zero = {k_: np.zeros_like(v) for k_, v in shared.items()}
    zero["x"] = np.zeros_like(x[0])
    for c in range(ncores):
        if c in ACTIVE:
            m = dict(shared)
            m["x"] = np.ascontiguousarray(x[ACTIVE.index(c)])
        else:
            m = zero
        maps.append(m)
    return maps


def kernel(**inputs):
    nc = build()
    maps = make_in_maps(inputs, 8)
    res = run_bass_kernel_spmd(nc, maps, core_ids=list(range(8)))
    out = np.stack([np.asarray(res.results[c]["out"], dtype=np.float32) for c in ACTIVE], axis=0)
    return out
```
